# Optimizing a Trainium2 kernel written in Bass

```python
import math
import jax, jax.numpy as jnp
from jax import lax
import numpy as np

D_MODEL = 2048
BATCH = 4
SEQ = 2048
DEPTH = 2

EXPAND = 2
MIX_WIDTH = EXPAND * D_MODEL
SSD_WIDTH = MIX_WIDTH // 2
SSD_HEAD_DIM = 64
SSD_HEADS = SSD_WIDTH // SSD_HEAD_DIM
SSD_GROUPS = 8
SSD_STATE = 128
SSD_CONV = 4
SSD_CHUNK = 128
SSD_CONV_DIM = SSD_WIDTH + 2 * SSD_GROUPS * SSD_STATE
SGU_WIDTH = MIX_WIDTH - SSD_WIDTH
SGU_CHUNK = 128
SGU_GROUPS = 16
SGU_GROUP_DIM = SGU_WIDTH // SGU_GROUPS
EVEN_SPLITS = (SSD_WIDTH, SSD_CONV_DIM, SSD_HEADS, SGU_WIDTH, SGU_WIDTH, SGU_WIDTH)
EVEN_IN = sum(EVEN_SPLITS)
DIFF_HEADS = 16
DIFF_HEAD_DIM = 128
DIFF_V_DIM = 2 * DIFF_HEAD_DIM
DIFF_WIDTH = DIFF_HEADS * DIFF_V_DIM
ODD_IN = 4 * DIFF_WIDTH
Q_BLOCK = 128
EPS = 1e-6
N_EVEN = (DEPTH + 1) // 2
N_ODD = DEPTH // 2

kernel_name = "hybrid_ssd_sgu_diffattn_block"


def rmsnorm(x, w):
    xf = x.astype(jnp.float32)
    y = xf * lax.rsqrt(jnp.mean(xf * xf, axis=-1, keepdims=True) + EPS)
    return (y * w.astype(jnp.float32)).astype(x.dtype)


def layernorm(x, w, b):
    xf = x.astype(jnp.float32)
    mu = jnp.mean(xf, axis=-1, keepdims=True)
    xc = xf - mu
    y = xc * lax.rsqrt(jnp.mean(xc * xc, axis=-1, keepdims=True) + EPS)
    return (y * w.astype(jnp.float32) + b.astype(jnp.float32)).astype(x.dtype)


def causal_depthwise_conv(x, w, b):
    K, C = w.shape
    y = lax.conv_general_dilated(x, w[:, None, :].astype(x.dtype), window_strides=(1,),
                                 padding=[(K - 1, 0)], dimension_numbers=('NWC', 'WIO', 'NWC'),
                                 feature_group_count=C)
    return y + b.astype(x.dtype)


def ssd_chunked(x, dt, a, bmat, cmat, d_skip):
    Bsz, S, H, P = x.shape
    G, N = bmat.shape[2], bmat.shape[3]
    R = H // G
    L = SSD_CHUNK
    nc = S // L
    xd = (x * dt[..., None].astype(x.dtype)).reshape(Bsz, nc, L, G, R, P)
    da = (dt.astype(jnp.float32) * a).reshape(Bsz, nc, L, G, R)
    da = jnp.moveaxis(da, 2, -1)
    cs = jnp.cumsum(da, axis=-1)
    bc = bmat.reshape(Bsz, nc, L, G, N)
    cc = cmat.reshape(Bsz, nc, L, G, N)
    causal = jnp.tril(jnp.ones((L, L), dtype=bool))
    seg = cs[..., :, None] - cs[..., None, :]
    decay = jnp.where(causal, jnp.exp(jnp.where(causal, seg, 0.0)), 0.0)
    cb = jnp.einsum('bclgn,bcsgn->bcgls', cc, bc)
    y_diag = jnp.einsum('bcgls,bcgrls,bcsgrp->bclgrp', cb, decay, xd)
    decay_to_end = jnp.exp(cs[..., -1:] - cs)
    states = jnp.einsum('bclgn,bcgrl,bclgrp->bcgrpn', bc, decay_to_end, xd).astype(jnp.float32)
    chunk_decay = jnp.exp(cs[..., -1])

    def step(h, inp):
        st, dec = inp
        return h * dec[..., None, None] + st, h

    h0 = jnp.zeros((Bsz, G, R, P, N), jnp.float32)
    _, prev = lax.scan(step, h0, (jnp.moveaxis(states, 1, 0), jnp.moveaxis(chunk_decay, 1, 0)))
    prev = jnp.moveaxis(prev, 0, 1)
    y_off = jnp.einsum('bclgn,bcgrpn,bcgrl->bclgrp', cc, prev, jnp.exp(cs))
    y = (y_diag + y_off).reshape(Bsz, S, H, P).astype(x.dtype)
    return y + x * d_skip[:, None].astype(x.dtype)


def even_mixer(h, w_in, conv_w, conv_b, dt_bias, a_log, d_skip, ssd_norm_w,
               sgu_ln_w, sgu_ln_b, sgu_ws, sgu_b, w_out):
    Bsz, S, _ = h.shape
    proj = h @ w_in
    z_a, xbc, dt_raw, z_b, u, v = jnp.split(proj, [int(c) for c in np.cumsum(EVEN_SPLITS)[:-1]], axis=-1)
    xbc = jax.nn.silu(causal_depthwise_conv(xbc, conv_w, conv_b))
    xs, bm, cm = jnp.split(xbc, [SSD_WIDTH, SSD_WIDTH + SSD_GROUPS * SSD_STATE], axis=-1)
    dt = jax.nn.softplus(dt_raw.astype(jnp.float32) + dt_bias.astype(jnp.float32))
    a = -jnp.exp(a_log.astype(jnp.float32))
    y = ssd_chunked(xs.reshape(Bsz, S, SSD_HEADS, SSD_HEAD_DIM), dt, a,
                    bm.reshape(Bsz, S, SSD_GROUPS, SSD_STATE),
                    cm.reshape(Bsz, S, SSD_GROUPS, SSD_STATE), d_skip)
    y = y.reshape(Bsz, S, SSD_WIDTH) * jax.nn.silu(z_a)
    y_a = rmsnorm(y.reshape(Bsz, S, SSD_GROUPS, SSD_WIDTH // SSD_GROUPS),
                  ssd_norm_w.reshape(SSD_GROUPS, -1)).reshape(Bsz, S, SSD_WIDTH)
    u = jax.nn.gelu(u)
    v = layernorm(jax.nn.gelu(v), sgu_ln_w, sgu_ln_b)
    nc = S // SGU_CHUNK
    vg = v.reshape(Bsz, nc, SGU_CHUNK, SGU_GROUPS, SGU_GROUP_DIM)
    ws = sgu_ws * jnp.tril(jnp.ones((SGU_CHUNK, SGU_CHUNK), sgu_ws.dtype))
    mixed = jnp.einsum('gts,bnsgc->bntgc', ws, vg) + sgu_b.T[:, :, None]
    y_b = u * mixed.reshape(Bsz, S, SGU_WIDTH) * jax.nn.silu(z_b)
    return jnp.concatenate([y_a, y_b], axis=-1) @ w_out


def odd_mixer(h, w_in, lam_q1, lam_k1, lam_q2, lam_k2, subln_w, w_out, lambda_init):
    Bsz, S, _ = h.shape
    proj = h @ w_in
    q, k, v, g = jnp.split(proj, 4, axis=-1)
    q = q.reshape(Bsz, S, DIFF_HEADS, 2, DIFF_HEAD_DIM).transpose(0, 2, 3, 1, 4)
    k = k.reshape(Bsz, S, DIFF_HEADS, 2, DIFF_HEAD_DIM).transpose(0, 2, 3, 1, 4)
    v = v.reshape(Bsz, S, DIFF_HEADS, DIFF_V_DIM).transpose(0, 2, 1, 3)
    lam = (jnp.exp(jnp.sum(lam_q1.astype(jnp.float32) * lam_k1.astype(jnp.float32)))
           - jnp.exp(jnp.sum(lam_q2.astype(jnp.float32) * lam_k2.astype(jnp.float32)))
           + lambda_init)
    scale = DIFF_HEAD_DIM ** -0.5
    outs = []
    for i in range(S // Q_BLOCK):
        end = (i + 1) * Q_BLOCK
        qb = q[:, :, :, i * Q_BLOCK:end]
        kb = k[:, :, :, :end]
        s = jnp.einsum('bhjqd,bhjkd->bhjqk', qb, kb).astype(jnp.float32) * scale
        qpos = i * Q_BLOCK + jnp.arange(Q_BLOCK)
        kpos = jnp.arange(end)
        s = jnp.where(kpos[None, :] <= qpos[:, None], s, -jnp.inf)
        p = jax.nn.softmax(s, axis=-1)
        attn = p[:, :, 0] - lam * p[:, :, 1]
        outs.append(jnp.einsum('bhqk,bhkd->bhqd', attn.astype(v.dtype), v[:, :, :end]))
    o = jnp.concatenate(outs, axis=2)
    o = rmsnorm(o, subln_w) * (1.0 - lambda_init)
    o = o.transpose(0, 2, 1, 3).reshape(Bsz, S, DIFF_WIDTH) * jax.nn.silu(g)
    return o @ w_out


def setup_inputs(seed: int = 0) -> dict:
    key = jax.random.key(seed)
    ks = jax.random.split(key, 24)
    f32 = jnp.float32
    nrm = lambda k, shape, s: jax.random.normal(k, shape, f32) * s
    dt0 = jnp.exp(jax.random.uniform(ks[5], (N_EVEN, SSD_HEADS), f32, math.log(1e-3), math.log(1e-1)))
    return {
        "x": nrm(ks[0], (BATCH, SEQ, D_MODEL), 1.0),
        "norm_w": 1.0 + nrm(ks[1], (DEPTH, D_MODEL), 0.02),
        "even_w_in": nrm(ks[2], (N_EVEN, D_MODEL, EVEN_IN), D_MODEL ** -0.5),
        "even_conv_w": nrm(ks[3], (N_EVEN, SSD_CONV, SSD_CONV_DIM), SSD_CONV ** -0.5),
        "even_conv_b": nrm(ks[4], (N_EVEN, SSD_CONV_DIM), 0.02),
        "even_dt_bias": dt0 + jnp.log(-jnp.expm1(-dt0)),
        "even_a_log": jnp.log(jax.random.uniform(ks[6], (N_EVEN, SSD_HEADS), f32, 1.0, 16.0)),
        "even_d_skip": 1.0 + nrm(ks[7], (N_EVEN, SSD_HEADS), 0.02),
        "even_ssd_norm_w": 1.0 + nrm(ks[8], (N_EVEN, SSD_WIDTH), 0.02),
        "even_sgu_ln_w": 1.0 + nrm(ks[9], (N_EVEN, SGU_WIDTH), 0.02),
        "even_sgu_ln_b": nrm(ks[10], (N_EVEN, SGU_WIDTH), 0.02),
        "even_sgu_ws": nrm(ks[11], (N_EVEN, SGU_GROUPS, SGU_CHUNK, SGU_CHUNK), SGU_CHUNK ** -0.5),
        "even_sgu_b": 1.0 + nrm(ks[12], (N_EVEN, SGU_GROUPS, SGU_CHUNK), 0.02),
        "even_w_out": nrm(ks[13], (N_EVEN, MIX_WIDTH, D_MODEL), MIX_WIDTH ** -0.5),
        "odd_w_in": nrm(ks[14], (N_ODD, D_MODEL, ODD_IN), D_MODEL ** -0.5),
        "odd_lam_q1": nrm(ks[15], (N_ODD, DIFF_HEAD_DIM), 0.1),
        "odd_lam_k1": nrm(ks[16], (N_ODD, DIFF_HEAD_DIM), 0.1),
        "odd_lam_q2": nrm(ks[17], (N_ODD, DIFF_HEAD_DIM), 0.1),
        "odd_lam_k2": nrm(ks[18], (N_ODD, DIFF_HEAD_DIM), 0.1),
        "odd_subln_w": 1.0 + nrm(ks[19], (N_ODD, DIFF_V_DIM), 0.02),
        "odd_w_out": nrm(ks[20], (N_ODD, DIFF_WIDTH, D_MODEL), DIFF_WIDTH ** -0.5),
        "final_norm_w": 1.0 + nrm(ks[21], (D_MODEL,), 0.02),
    }


def reference(x, norm_w, even_w_in, even_conv_w, even_conv_b, even_dt_bias, even_a_log,
              even_d_skip, even_ssd_norm_w, even_sgu_ln_w, even_sgu_ln_b, even_sgu_ws,
              even_sgu_b, even_w_out, odd_w_in, odd_lam_q1, odd_lam_k1, odd_lam_q2,
              odd_lam_k2, odd_subln_w, odd_w_out, final_norm_w):
    h = x
    for layer in range(DEPTH):
        hn = rmsnorm(h, norm_w[layer])
        i = layer // 2
        if layer % 2 == 0:
            h = h + even_mixer(hn, even_w_in[i], even_conv_w[i], even_conv_b[i], even_dt_bias[i],
                               even_a_log[i], even_d_skip[i], even_ssd_norm_w[i],
                               even_sgu_ln_w[i], even_sgu_ln_b[i], even_sgu_ws[i],
                               even_sgu_b[i], even_w_out[i])
        else:
            lambda_init = 0.8 - 0.6 * math.exp(-0.3 * layer)
            h = h + odd_mixer(hn, odd_w_in[i], odd_lam_q1[i], odd_lam_k1[i], odd_lam_q2[i],
                              odd_lam_k2[i], odd_subln_w[i], odd_w_out[i], lambda_init)
    return rmsnorm(h, final_norm_w)
```

```python
import math
from contextlib import ExitStack

import numpy as np
import ml_dtypes
import concourse.bass as bass
import concourse.mybir as mybir
from concourse.bass_utils import run_bass_kernel_spmd

F32 = mybir.dt.float32
BF16 = mybir.dt.bfloat16
AF = mybir.ActivationFunctionType
ALU = mybir.AluOpType
AX = mybir.AxisListType
ENG = ["sync", "scalar", "vector", "gpsimd", "tensor"]
EPS = 1e-6
NPBF = ml_dtypes.bfloat16

D = 2048
S = 2048
TL = 1024
KC = D // 128


class Buf:
    def __init__(self, name, t=None):
        self.name = name
        self.t = t
        self.w = {}
        self.r = {}
        self.dsem = None
        self.excl = False


class Prog:
    def __init__(self, nc, es):
        self.nc = nc
        self.es = es
        self.ccsem = None
        self.q = {e: [] for e in ENG}
        self.cnt = {e: 0 for e in ENG}
        self.esem = {e: es.enter_context(nc.semaphore("prog_" + e)) for e in ENG}
        self.pool = [[es.enter_context(nc.semaphore("dsem%d" % i)), 0] for i in range(76)]
        self.free_slots = list(range(len(self.pool)))
        self.phase_slots = []
        self.waited = {e: {} for e in ENG}
        self.pending = []
        self.nblocks = 0

    def _wait(self, e, tok):
        if tok is None:
            return
        key, h, val = tok
        if self.waited[e].get(key, 0) >= val:
            return
        self.waited[e][key] = val
        self.q[e].append(lambda E, h=h, val=val: E.wait_ge(h, val))

    def _deps(self, e, reads, writes, pwrites=()):
        for b in reads:
            for tok in b.w.values():
                self._wait(e, tok)
            if b.excl:
                for tok in b.r.values():
                    if tok[0] != "E" + e:
                        self._wait(e, tok)
        for b in list(writes) + list(pwrites):
            if b in writes:
                for tok in b.w.values():
                    self._wait(e, tok)
            for tok in b.r.values():
                if tok[0] == "E" + e:
                    continue
                self._wait(e, tok)

    def _commit(self, tok, reads, writes, pwrites=()):
        for b in reads:
            if b in writes or b in pwrites:
                continue
            b.r[tok[0]] = tok
        for b in writes:
            b.w = {tok[0]: tok}
            b.r = {}
        for b in pwrites:
            b.w[tok[0]] = tok

    def op(self, e, fn, reads=(), writes=(), pwrites=()):
        self._deps(e, reads, writes, pwrites)
        self.cnt[e] += 1
        h = self.esem[e]
        self.q[e].append(lambda E, fn=fn, h=h: fn(E).then_inc(h, 1))
        tok = ("E" + e, h, self.cnt[e])
        self._commit(tok, reads, writes, pwrites)
        return tok

    def _slot(self, b):
        if b.dsem is None:
            i = self.free_slots.pop(0)
            self.phase_slots.append((b, i))
            b.dsem = i
        return b.dsem

    def dma(self, e, out, in_, reads=(), writes=(), pwrites=(), dram_write=False, sem_buf=None):
        self._deps(e, reads, writes, pwrites)
        sb = sem_buf if sem_buf is not None else (list(writes) + list(pwrites) + list(reads))[0]
        i = self._slot(sb)
        self.pool[i][1] += 16
        h, val = self.pool[i][0], self.pool[i][1]

        def issue(E, out=out, in_=in_, h=h):
            o_ = out(E) if callable(out) else out
            i_ = in_(E) if callable(in_) else in_
            return E.dma_start(out=o_, in_=i_).then_inc(h, 16)

        self.q[e].append(issue)
        tok = ("D%d" % i, h, val)
        self._commit(tok, reads, writes, pwrites)
        if dram_write:
            self.pending.append(tok)
        return tok

    def collective(self, kind, ins, outs, groups):
        if self.ccsem is None:
            self.ccsem = [self.es.enter_context(self.nc.semaphore("ccsem")), 0]
        self.ccsem[1] += 1
        h, val = self.ccsem
        self.q["gpsimd"].append(lambda E: E.collective_compute(kind, ALU.bypass, replica_groups=groups, ins=ins,
                                                               outs=outs).then_inc(h, 1))
        self.pending.append(("CC", h, val))

    def end_phase(self, last=False):
        for tok in self.pending:
            self._wait("sync", tok)
        self.pending = []
        nc = self.nc
        q = self.q
        with nc.Block() as block:
            for e in ENG:
                if not q[e]:
                    continue

                def body(E, lst=q[e]):
                    for fn in lst:
                        fn(E)

                getattr(block, e)(body)
        self.q = {e: [] for e in ENG}
        self.nblocks += 1

    def release_bufs(self):
        for b, i in self.phase_slots:
            b.dsem = None
            self.free_slots.append(i)
        self.phase_slots = []


class Ctx:
    def __init__(self, nc, es):
        self.nc = nc
        self.es = es
        self.P = Prog(nc, es)
        self.n = 0
        self.banks = []
        for i in range(8):
            t = es.enter_context(nc.psum_tensor("bank%d" % i, [128, 512], F32))
            self.banks.append(Buf("bank%d" % i, t))
            self.banks[-1].excl = True

    def sb(self, es, shape, dtype, name=None):
        self.n += 1
        nm = "%s_%d" % (name or "t", self.n)
        t = es.enter_context(self.nc.sbuf_tensor(nm, list(shape), dtype))
        return Buf(nm, t)


def bfv(bank):
    return bank.t[:].bitcast(BF16)


def load_w(P, wb, W, c0, cw, kc=KC):
    src = W[:, c0:c0 + cw].rearrange("(kc p) c -> p kc c", p=128)
    P.dma("gpsimd", wb.t[:, 0:kc, 0:cw], src, writes=[wb])


def mm_group(P, bank, out_ap, pairs, reads):
    def fn(E, pairs=pairs, out_ap=out_ap):
        n = len(pairs)
        ins = None
        for i, (l, r) in enumerate(pairs):
            ins = E.matmul(out_ap, l, r, start=(i == 0), stop=(i == n - 1))
        return ins

    return P.op("tensor", fn, reads=reads, writes=[bank])


def rstd_from_ss(P, ss, rstd, n):
    P.op("scalar", lambda E: E.activation(out=rstd.t[:], in_=ss.t[:], func=AF.Sqrt, scale=1.0 / n, bias=EPS),
         reads=[ss], writes=[rstd])
    P.op("vector", lambda E: E.reciprocal(out=rstd.t[:], in_=rstd.t[:]), reads=[rstd], writes=[rstd])


def phase_normt(C, x_d, nw_d, hnT_d, ident, NT=8):
    P = C.P
    with ExitStack() as es:
        nwbc = C.sb(es, [128, D], F32, "nwbc")
        P.dma("sync", nwbc.t[:], nw_d.partition_broadcast(128), writes=[nwbc])
        xr = [C.sb(es, [128, D], F32, "x") for _ in range(2)]
        junk = C.sb(es, [128, D], BF16, "junk")
        ssr = [C.sb(es, [128, 1], F32, "ss") for _ in range(2)]
        rsr = [C.sb(es, [128, 1], F32, "rs") for _ in range(2)]
        hnr = [C.sb(es, [128, D], BF16, "hn") for _ in range(2)]
        hnT = C.sb(es, [128, KC, NT * 128], BF16, "hnT")
        for i in range(NT):
            x, ss, rs, hn = xr[i % 2], ssr[i % 2], rsr[i % 2], hnr[i % 2]
            P.dma("sync", x.t[:], x_d[i * 128:(i + 1) * 128, :], writes=[x])
            P.op("scalar", lambda E, x=x, ss=ss: E.activation(out=junk.t[:], in_=x.t[:], func=AF.Square,
                                                              accum_out=ss.t[:]),
                 reads=[x], writes=[junk, ss])
            rstd_from_ss(P, ss, rs, D)
            P.op("vector", lambda E, x=x, rs=rs, hn=hn: E.scalar_tensor_tensor(
                out=hn.t[:], in0=x.t[:], scalar=rs.t[:, 0:1], in1=nwbc.t[:], op0=ALU.mult, op1=ALU.mult),
                reads=[x, rs, nwbc], writes=[hn])
            for half in range(2):
                bank = C.banks[(2 * i + half) % 4]

                def tr(E, hn=hn, bank=bank, half=half):
                    ins = None
                    for j in range(8):
                        k = half * 8 + j
                        ins = E.transpose(out=bfv(bank)[:, j * 128:(j + 1) * 128],
                                          in_=hn.t[:, k * 128:(k + 1) * 128], identity=ident.t[:])
                    return ins

                P.op("tensor", tr, reads=[hn, ident], writes=[bank])
                eng = "scalar" if half == 0 else "vector"

                def cp(E, bank=bank, half=half, i=i, eng=eng):
                    src = bfv(bank).rearrange("p (j t) -> p j t", t=128)
                    dst = hnT.t[:, half * 8:(half + 1) * 8, i * 128:(i + 1) * 128]
                    if eng == "scalar":
                        return E.copy(out=dst, in_=src)
                    return E.tensor_copy(out=dst, in_=src)

                P.op(eng, cp, reads=[bank], pwrites=[hnT])
        for h in range(2):
            P.dma("sync", hnT_d[h * 1024:(h + 1) * 1024, :].rearrange("(kc p) t -> p kc t", p=128),
                  hnT.t[:, h * 8:(h + 1) * 8, :], reads=[hnT], dram_write=True)
        P.end_phase()
        P.release_bufs()


class Stager:
    def __init__(self, C, es, shape, dtype, n=3, name="stg"):
        self.bufs = [C.sb(es, shape, dtype, name) for _ in range(n)]
        self.i = 0

    def next(self):
        b = self.bufs[self.i % len(self.bufs)]
        self.i += 1
        return b


def gemm_T(C, actT, kc, ntt, W, blocks, epi, wring, banks):
    P = C.P
    cnt = 0
    for bi, (c0, cw, tag) in enumerate(blocks):
        wb = wring[bi % len(wring)]
        load_w(P, wb, W, c0, cw, kc)
        for tt in range(ntt):
            bank = banks[cnt % len(banks)]
            cnt += 1
            pairs = [(actT.t[:, k, tt * 128:(tt + 1) * 128], wb.t[:, k, 0:cw]) for k in range(kc)]
            mm_group(P, bank, bank.t[:, 0:cw], pairs, reads=[actT, wb])
            epi(tag, c0, cw, tt, bank)


def gemm_F(C, actT, kc, T, W, blocks, epi, wring, banks):
    P = C.P
    cnt = 0
    for bi, (c0, cw, tag) in enumerate(blocks):
        wb = wring[bi % len(wring)]
        load_w(P, wb, W, c0, cw, kc)
        for ch in range(cw // 128):
            for tb in range(T // 512):
                bank = banks[cnt % len(banks)]
                cnt += 1
                pairs = [(wb.t[:, k, ch * 128:(ch + 1) * 128], actT.t[:, k, tb * 512:(tb + 1) * 512])
                         for k in range(kc)]
                mm_group(P, bank, bank.t[:, 0:512], pairs, reads=[actT, wb])
                epi(tag, c0 + ch * 128, tb, bank)


def act_epi(P, bank, src, dst_buf, dst, func, eng="scalar"):
    if eng == "scalar":
        P.op("scalar", lambda E: E.activation(out=dst, in_=src, func=func), reads=[bank], writes=[dst_buf])
    else:
        P.op("vector", lambda E: E.tensor_copy(out=dst, in_=src), reads=[bank], writes=[dst_buf])


GELU_C = 0.044715
GELU_S = 2.0 * math.sqrt(2.0 / math.pi)


def gelu_epi(P, bank, src, tmp, tmp2, dst_buf, dst, n):
    P.op("scalar", lambda E: E.activation(out=tmp.t[:, 0:n], in_=src, func=AF.Square), reads=[bank], writes=[tmp])
    P.op("vector", lambda E: E.tensor_scalar(out=tmp.t[:, 0:n], in0=tmp.t[:, 0:n], scalar1=GELU_C, scalar2=1.0,
                                             op0=ALU.mult, op1=ALU.add), reads=[tmp], writes=[tmp])
    P.op("vector", lambda E: E.tensor_tensor(out=tmp.t[:, 0:n], in0=tmp.t[:, 0:n], in1=src, op=ALU.mult),
         reads=[tmp, bank], writes=[tmp])
    P.op("scalar", lambda E: E.activation(out=tmp2.t[:, 0:n], in_=tmp.t[:, 0:n], func=AF.Sigmoid, scale=GELU_S),
         reads=[tmp], writes=[tmp2])
    P.op("vector", lambda E: E.tensor_tensor(out=dst, in0=tmp2.t[:, 0:n], in1=src, op=ALU.mult),
         reads=[tmp2, bank], writes=[dst_buf])


def load_hnT(C, es, src):
    hnT = C.sb(es, [128, KC, S], BF16, "hnTall")
    if not isinstance(src, list):
        src = [(src[j, h * 1024:(h + 1) * 1024, :], h * 8, 8, j * TL, TL) for j in range(2) for h in range(2)]
    for ap, kc0, nk, t0, nt in src:
        C.P.dma("sync", hnT.t[:, kc0:kc0 + nk, t0:t0 + nt], ap.rearrange("(kc p) t -> p kc t", p=128), pwrites=[hnT])
    return hnT


def phase_out(C, yT_d, W_d, res_d, h_d, ysrc=None, dyn_eng="sync"):
    P = C.P
    with ExitStack() as es:
        yT = C.sb(es, [128, 32, TL], BF16, "yT")
        if ysrc is None:
            for q in range(4):
                P.dma("sync", yT.t[:, q * 8:(q + 1) * 8, :],
                      yT_d[q * 1024:(q + 1) * 1024, :].rearrange("(kc p) t -> p kc t", p=128), pwrites=[yT])
        else:
            for s_ in range(2):
                P.dma(dyn_eng, yT.t[:, s_ * 16:(s_ + 1) * 16, :], ysrc(s_), pwrites=[yT])
        wring = [C.sb(es, [128, 32, 512], BF16, "wo") for _ in range(2)]
        rring = Stager(C, es, [128, 512], F32, 3, "res")
        oring = Stager(C, es, [128, 512], F32, 3, "ho")

        def epi(tag, c0, cw, tt, bank):
            r = rring.next()
            o = oring.next()
            P.dma("sync", r.t[:], res_d[tt * 128:(tt + 1) * 128, c0:c0 + cw], writes=[r])
            P.op("vector", lambda E: E.tensor_tensor(out=o.t[:], in0=bank.t[:, 0:cw], in1=r.t[:], op=ALU.add),
                 reads=[bank, r], writes=[o])
            P.dma("sync", h_d[tt * 128:(tt + 1) * 128, c0:c0 + cw], o.t[:], reads=[o], dram_write=True)

        gemm_T(C, yT, 32, TL // 128, W_d, [(c * 512, 512, None) for c in range(4)], epi, wring, C.banks[0:4])
        P.end_phase()
        P.release_bufs()


def phase_finalnorm(C, x_d, nw_d, out_d, NT=8):
    P = C.P
    with ExitStack() as es:
        nwbc = C.sb(es, [128, D], F32, "nwbc")
        P.dma("sync", nwbc.t[:], nw_d.partition_broadcast(128), writes=[nwbc])
        xr = [C.sb(es, [128, D], F32, "x") for _ in range(2)]
        junk = C.sb(es, [128, D], BF16, "junk")
        ssr = [C.sb(es, [128, 1], F32, "ss") for _ in range(2)]
        rsr = [C.sb(es, [128, 1], F32, "rs") for _ in range(2)]
        orr = [C.sb(es, [128, D], F32, "o") for _ in range(2)]
        for i in range(NT):
            x, ss, rs, o = xr[i % 2], ssr[i % 2], rsr[i % 2], orr[i % 2]
            P.dma("sync", x.t[:], x_d[i * 128:(i + 1) * 128, :], writes=[x])
            P.op("scalar", lambda E, x=x, ss=ss: E.activation(out=junk.t[:], in_=x.t[:], func=AF.Square,
                                                              accum_out=ss.t[:]),
                 reads=[x], writes=[junk, ss])
            rstd_from_ss(P, ss, rs, D)
            P.op("vector", lambda E, x=x, rs=rs, o=o: E.scalar_tensor_tensor(
                out=o.t[:], in0=x.t[:], scalar=rs.t[:, 0:1], in1=nwbc.t[:], op0=ALU.mult, op1=ALU.mult),
                reads=[x, rs, nwbc], writes=[o])
            P.dma("sync", out_d[i * 128:(i + 1) * 128, :], o.t[:], reads=[o], dram_write=True)
        P.end_phase()
        P.release_bufs()


NH1 = 8
MIX1_ATT = True
YT_FLAT = False
LAMBDA_INIT = 0.8 - 0.6 * math.exp(-0.3 * 1)


def phase_mix1(C, hnT_all_d, W1_d, lam_d, subw_d, yT_d, scr, cst, ygather=None):
    P = C.P
    nc = C.nc
    qkT_d, v_d, sg_d = scr["qkT"], scr["v"], scr["sg"]
    with ExitStack() as es:
        hnT = load_hnT(C, es, hnT_all_d)
        wring = [C.sb(es, [128, KC, 512], BF16, "w1") for _ in range(2)]
        stq = Stager(C, es, [128, 512], BF16, 3, "stq")
        stv = Stager(C, es, [128, 256], BF16, 3, "stv")
        stg = Stager(C, es, [128, 256], F32, 3, "stg")
        cntq = [0]

        def epiF(tag, c0, tb, bank):
            hh = tag
            cc = (c0 - hh * 1024) // 128
            st = stq.next()
            eng = "scalar" if cntq[0] % 2 == 0 else "vector"
            cntq[0] += 1
            act_epi(P, bank, bank.t[:, 0:512], st, st.t[:], AF.Copy, eng)
            P.dma("sync", qkT_d[hh * 4 + cc, :, tb * 512:(tb + 1) * 512], st.t[:], reads=[st], dram_write=True)

        def epiT(tag, c0, cw, tt, bank):
            hh = tag
            sv = stv.next()
            sg = stg.next()
            P.op("vector", lambda E: E.tensor_copy(out=sv.t[:], in_=bank.t[:, 0:256]), reads=[bank], writes=[sv])
            P.op("scalar", lambda E: E.activation(out=sg.t[:], in_=bank.t[:, 256:512], func=AF.Silu),
                 reads=[bank], writes=[sg])
            P.dma("sync", v_d[tt * 128:(tt + 1) * 128, hh, :], sv.t[:], reads=[sv], dram_write=True)
            P.dma("sync", sg_d[tt * 128:(tt + 1) * 128, hh, :], sg.t[:], reads=[sg], dram_write=True)

        for hh in range(NH1):
            gemm_F(C, hnT, KC, S, W1_d, [(hh * 1024, 512, hh)], epiF, [wring[0]], C.banks[0:4])
            gemm_T(C, hnT, KC, S // 128, W1_d, [(hh * 1024 + 512, 512, hh)], epiT, [wring[1]], C.banks[4:8])
        P.end_phase()
        P.release_bufs()

    if not MIX1_ATT:
        return
    with ExitStack() as es:
        ident, Ubf = cst["ident"], cst["Ubf"]
        scale = 128.0 ** -0.5
        lamv = C.sb(es, [128, 4, 128], F32, "lamv")
        P.dma("sync", lamv.t[:], lam_d.rearrange("a d -> (a d)").partition_broadcast(128)
              .rearrange("p (a d) -> p a d", a=4), writes=[lamv])
        lprod = C.sb(es, [128, 2, 128], F32, "lprod")
        lsum = C.sb(es, [128, 2], F32, "lsum")
        lam = C.sb(es, [128, 1], F32, "lam")
        P.op("vector", lambda E: E.tensor_tensor(out=lprod.t[:, 0, :], in0=lamv.t[:, 0, :], in1=lamv.t[:, 1, :],
                                                 op=ALU.mult), reads=[lamv], writes=[lprod])
        P.op("vector", lambda E: E.tensor_tensor(out=lprod.t[:, 1, :], in0=lamv.t[:, 2, :], in1=lamv.t[:, 3, :],
                                                 op=ALU.mult), reads=[lamv, lprod], writes=[lprod])
        P.op("vector", lambda E: E.tensor_reduce(out=lsum.t[:], in_=lprod.t[:], axis=AX.X, op=ALU.add),
             reads=[lprod], writes=[lsum])
        P.op("scalar", lambda E: E.activation(out=lsum.t[:], in_=lsum.t[:], func=AF.Exp), reads=[lsum], writes=[lsum])
        P.op("vector", lambda E: E.tensor_tensor(out=lam.t[:], in0=lsum.t[:, 0:1], in1=lsum.t[:, 1:2],
                                                 op=ALU.subtract), reads=[lsum], writes=[lam])
        P.op("vector", lambda E: E.tensor_scalar(out=lam.t[:], in0=lam.t[:], scalar1=LAMBDA_INIT, scalar2=None,
                                                 op0=ALU.add), reads=[lam], writes=[lam])
        subw = C.sb(es, [128, 256], F32, "subw")
        P.dma("sync", subw.t[:], subw_d.partition_broadcast(128), writes=[subw])
        P.op("vector", lambda E: E.tensor_scalar(out=subw.t[:], in0=subw.t[:], scalar1=1.0 - LAMBDA_INIT,
                                                 scalar2=None, op0=ALU.mult), reads=[subw], writes=[subw])

        qkr = [C.sb(es, [128, 4, S], BF16, "qk") for _ in range(2)]
        vr = [C.sb(es, [128, 16, 258], BF16, "vaug") for _ in range(2)]
        sgr = [C.sb(es, [128, 16, 256], F32, "sg") for _ in range(2)]
        for v in vr:
            P.op("vector", lambda E, v=v: E.memset(v.t[:, :, 256:258], 1.0), writes=[v])
        PTS = [[[C.sb(es, [128, 512], BF16, "PT") for _ in range(16)] for _ in range(2)] for _ in range(2)]
        yTr = [C.sb(es, [128, 2, S], BF16, "yTh") for _ in range(2)]
        o2r = Stager(C, es, [128, 256], F32, 2, "o2")
        orr = Stager(C, es, [128, 256], F32, 2, "o")
        junk = C.sb(es, [128, 256], BF16, "junk")
        smr = Stager(C, es, [128, 4], F32, 4, "small")
        wgr = Stager(C, es, [128, 256], F32, 2, "wg")
        ybr = Stager(C, es, [128, 256], BF16, 3, "yb")
        sbanks = [C.banks[0], C.banks[1], C.banks[6]]
        abanks = C.banks[2:6]
        tbanks = [C.banks[7]]
        ns = [0]
        na = [0]
        nt = [0]
        units = [(hh, R) for hh in range(NH1) for R in range(4)]
        deferred = []

        def flush():
            while deferred:
                deferred.pop(0)()

        def st_tiles(u):
            hh, R = units[u]
            qk, va, sg = qkr[hh % 2], vr[hh % 2], sgr[hh % 2]
            PT = PTS[u % 2]
            out = []
            if R == 0:
                def loads():
                    P.dma("sync", qk.t[:], qkT_d[hh * 4:(hh + 1) * 4, :, :].rearrange("c p t -> p c t"), writes=[qk])
                    P.dma("sync", va.t[:, :, 0:256], v_d[:, hh, :].rearrange("(tt p) d -> p tt d", p=128), pwrites=[va])
                    P.dma("sync", sg.t[:], sg_d[:, hh, :].rearrange("(tt p) d -> p tt d", p=128), writes=[sg])
                out.append(loads)
            nk = 4 * R + 4
            for j in range(2):
                for kc in range(nk):
                    def tile(j=j, kc=kc):
                        off = max(0, kc - 4 * R) * 128
                        bank = sbanks[ns[0] % 3]
                        ns[0] += 1
                        pt = PT[j][kc]
                        mm_group(P, bank, bank.t[:, off:512],
                                 [(qk.t[:, 2 + j, kc * 128:(kc + 1) * 128],
                                   qk.t[:, j, R * 512 + off:(R + 1) * 512])], reads=[qk])
                        P.op("scalar", lambda E: E.activation(out=pt.t[:, off:512], in_=bank.t[:, off:512],
                                                              func=AF.Exp, scale=scale), reads=[bank], writes=[pt])
                        if kc >= 4 * R:
                            P.op("gpsimd", lambda E: E.tensor_tensor(
                                out=pt.t[:, off:off + 128], in0=pt.t[:, off:off + 128], in1=Ubf.t[:], op=ALU.mult),
                                reads=[pt, Ubf], writes=[pt])
                    out.append(tile)
            return out

        def emit_PV(u, nxt):
            hh, R = units[u]
            va, sg, yTh = vr[hh % 2], sgr[hh % 2], yTr[hh % 2]
            PT = PTS[u % 2]
            per = (len(nxt) + 7) // 8
            for qs in range(4):
                qb = 4 * R + qs
                accs = []
                for j in range(2):
                    bank = abanks[na[0] % 4]
                    na[0] += 1
                    accs.append(bank)
                    pairs = [(PT[j][kc].t[:, qs * 128:(qs + 1) * 128], va.t[:, kc, 0:257]) for kc in range(qb + 1)]
                    mm_group(P, bank, bank.t[:, 0:257], pairs, reads=[va] + [PT[j][kc] for kc in range(qb + 1)])
                    if j == 1:
                        flush()
                    for _ in range(per):
                        if nxt:
                            nxt.pop(0)()
                a1, a2 = accs
                sm, o2, o, wg, yb = smr.next(), o2r.next(), orr.next(), wgr.next(), ybr.next()
                P.op("vector", lambda E, sm=sm, a1=a1: E.reciprocal(out=sm.t[:, 0:1], in_=a1.t[:, 256:257]),
                     reads=[a1], writes=[sm])
                P.op("vector", lambda E, sm=sm, a2=a2: E.reciprocal(out=sm.t[:, 1:2], in_=a2.t[:, 256:257]),
                     reads=[a2, sm], writes=[sm])
                P.op("vector", lambda E, sm=sm: E.tensor_tensor(out=sm.t[:, 1:2], in0=sm.t[:, 1:2], in1=lam.t[:],
                                                                op=ALU.mult), reads=[sm, lam], writes=[sm])
                P.op("vector", lambda E, sm=sm, a2=a2, o2=o2: E.tensor_scalar(
                    out=o2.t[:], in0=a2.t[:, 0:256], scalar1=sm.t[:, 1:2], scalar2=None, op0=ALU.mult),
                    reads=[a2, sm], writes=[o2])
                P.op("vector", lambda E, sm=sm, a1=a1, o2=o2, o=o: E.scalar_tensor_tensor(
                    out=o.t[:], in0=a1.t[:, 0:256], scalar=sm.t[:, 0:1], in1=o2.t[:], op0=ALU.mult,
                    op1=ALU.subtract), reads=[a1, sm, o2], writes=[o])
                sm2 = smr.next()
                P.op("vector", lambda E, o=o, sm2=sm2: E.scalar_tensor_tensor(
                    out=junk.t[:], in0=o.t[:], scalar=1.0, in1=o.t[:], op0=ALU.mult, op1=ALU.mult,
                    accum_out=sm2.t[:, 0:1]), reads=[o], writes=[junk, sm2])
                P.op("scalar", lambda E, sm2=sm2: E.activation(out=sm2.t[:, 1:2], in_=sm2.t[:, 0:1], func=AF.Sqrt,
                                                               scale=1.0 / 256, bias=EPS), reads=[sm2], writes=[sm2])
                P.op("vector", lambda E, sm2=sm2: E.reciprocal(out=sm2.t[:, 2:3], in_=sm2.t[:, 1:2]),
                     reads=[sm2], writes=[sm2])
                P.op("gpsimd", lambda E, wg=wg, sg=sg, qb=qb: E.tensor_tensor(
                    out=wg.t[:], in0=sg.t[:, qb, :], in1=subw.t[:], op=ALU.mult), reads=[sg, subw], writes=[wg])
                P.op("vector", lambda E, o=o, sm2=sm2, wg=wg, yb=yb: E.scalar_tensor_tensor(
                    out=yb.t[:], in0=o.t[:], scalar=sm2.t[:, 2:3], in1=wg.t[:], op0=ALU.mult, op1=ALU.mult),
                    reads=[o, sm2, wg], writes=[yb])

                def late(yb=yb, yTh=yTh, qb=qb):
                    tbank = tbanks[0]
                    nt[0] += 1

                    def tr(E):
                        ins = None
                        for c in range(2):
                            ins = E.transpose(out=bfv(tbank)[:, c * 128:(c + 1) * 128],
                                              in_=yb.t[:, c * 128:(c + 1) * 128], identity=ident.t[:])
                        return ins

                    P.op("tensor", tr, reads=[yb, ident], writes=[tbank])
                    P.op("vector", lambda E: E.tensor_copy(
                        out=yTh.t[:, :, qb * 128:(qb + 1) * 128],
                        in_=bfv(tbank)[:, 0:256].rearrange("p (c t) -> p c t", t=128)), reads=[tbank], pwrites=[yTh])

                deferred.append(late)
            if R == 3:
                def store(hh=hh, yTh=yTh):
                    if YT_FLAT:
                        tok = P.dma("sync", yT_d[hh * 256:(hh + 1) * 256, :].rearrange("(c p) t -> p c t", p=128),
                                    yTh.t[:], reads=[yTh], dram_write=True)
                        if ygather is not None:
                            P._wait("gpsimd", tok)
                            ygather(hh)
                    else:
                        for j in range(2):
                            P.dma("sync", yT_d[j, hh * 256:(hh + 1) * 256, :].rearrange("(c p) t -> p c t", p=128),
                                  yTh.t[:, :, j * TL:(j + 1) * TL], reads=[yTh], dram_write=True)

                deferred.append(store)

        for f in st_tiles(0):
            f()
        for u in range(len(units)):
            nxt = st_tiles(u + 1) if u + 1 < len(units) else []
            emit_PV(u, nxt)
            while nxt:
                nxt.pop(0)()
        flush()
        P.end_phase()
        P.release_bufs()


W0_F = 4096
W0_COLS = 4096 + 1024 + 16 + 2048


class nc_allow:
    def __init__(self, C):
        self.C = C

    def __enter__(self):
        return self

    def __exit__(self, *a):
        return False


def bc3(ap2, n):
    return ap2.unsqueeze(2).to_broadcast([ap2.shape[0], ap2.shape[1], n])


def phase_mix0(C, hnT_all_d, W0_d, prm, yT_d, scr, cst, ygather=None):
    P = C.P
    raw_d, zb_d, u_d, za_d, dt_d, vg_d = scr["raw"], scr["zb"], scr["u"], scr["za"], scr["dt"], scr["vg"]
    ident, Ubf, Uf, Lgt, ones = cst["ident"], cst["Ubf"], cst["Uf"], cst["Lgt"], cst["ones"]
    with ExitStack() as es:
        hnT = load_hnT(C, es, hnT_all_d)
        wring = [C.sb(es, [128, KC, 512], BF16, "w0") for _ in range(2)]
        st = Stager(C, es, [128, 512], F32, 4, "st")
        tmpr = Stager(C, es, [128, 512], F32, 2, "gt")
        tmp2r = Stager(C, es, [128, 512], F32, 2, "gt2")
        cn = [0]

        def epiF(tag, c0, tb, bank):
            s_ = st.next()
            ch = c0 // 128
            if tag == "raw":
                eng = "scalar" if cn[0] % 2 == 0 else "vector"
                cn[0] += 1
                act_epi(P, bank, bank.t[:, 0:512], s_, s_.t[:], AF.Copy, eng)
                dst = raw_d[ch, :, tb * 512:(tb + 1) * 512]
            elif tag == "zb":
                act_epi(P, bank, bank.t[:, 0:512], s_, s_.t[:], AF.Silu)
                dst = zb_d[ch - 16, :, tb * 512:(tb + 1) * 512]
            else:
                gelu_epi(P, bank, bank.t[:, 0:512], tmpr.next(), tmp2r.next(), s_, s_.t[:], 512)
                dst = u_d[ch - 24, :, tb * 512:(tb + 1) * 512]
            P.dma("sync", dst, s_.t[:], reads=[s_], dram_write=True)

        def epiT(tag, c0, cw, tt, bank):
            s_ = st.next()
            rows = slice(tt * 128, (tt + 1) * 128)
            if tag == "za":
                act_epi(P, bank, bank.t[:, 0:cw], s_, s_.t[:, 0:cw], AF.Silu)
                dst = za_d[rows, c0 - W0_F:c0 - W0_F + cw]
            elif tag == "dt":
                act_epi(P, bank, bank.t[:, 0:cw], s_, s_.t[:, 0:cw], AF.Copy, "vector")
                dst = dt_d[rows, :]
            else:
                gelu_epi(P, bank, bank.t[:, 0:cw], tmpr.next(), tmp2r.next(), s_, s_.t[:, 0:cw], cw)
                v0 = c0 - (W0_F + 1040)
                dst = vg_d[rows, v0:v0 + cw]
            P.dma("sync", dst, s_.t[:, 0:cw], reads=[s_], dram_write=True)

        fblocks = [(i * 512, 512, "raw") for i in range(4)] + [(2048 + i * 512, 512, "zb") for i in range(2)] + \
                  [(3072 + i * 512, 512, "u") for i in range(2)]
        gemm_F(C, hnT, KC, S, W0_d, fblocks, epiF, wring, C.banks[0:4])
        tblocks = [(W0_F, 512, "za"), (W0_F + 512, 512, "za"), (W0_F + 1024, 16, "dt")] + \
                  [(W0_F + 1040 + i * 512, 512, "v") for i in range(4)]
        gemm_T(C, hnT, KC, S // 128, W0_d, tblocks, epiT, wring, C.banks[4:8])
        P.end_phase()
        P.release_bufs()

    with ExitStack() as es:
        def bcast_load(name, src, n):
            b = C.sb(es, [128, n], F32, name)
            P.dma("sync", b.t[:], src.partition_broadcast(128), writes=[b])
            return b

        cw_sb = C.sb(es, [128, 16, 4], F32, "convw")
        P.dma("sync", cw_sb.t[:], prm["conv_w"], writes=[cw_sb])
        cb_sb = C.sb(es, [128, 16], F32, "convb")
        P.dma("sync", cb_sb.t[:], prm["conv_b"], writes=[cb_sb])
        dtb = bcast_load("dtb", prm["dt_bias"], 16)
        a_bc = bcast_load("a_bc", prm["a_log"], 16)
        dsk = bcast_load("dsk", prm["d_skip"], 16)
        nrmw = bcast_load("nrmw", prm["ssd_norm_w"], 1024)
        P.op("scalar", lambda E: E.activation(out=a_bc.t[:], in_=a_bc.t[:], func=AF.Exp), reads=[a_bc], writes=[a_bc])
        P.op("vector", lambda E: E.tensor_scalar(out=a_bc.t[:], in0=a_bc.t[:], scalar1=-1.0, scalar2=None,
                                                 op0=ALU.mult), reads=[a_bc], writes=[a_bc])
        xbcT = C.sb(es, [128, 16, S], BF16, "xbcT")
        rawr = [C.sb(es, [128, 3 + S], F32, "raw") for _ in range(2)]
        accr = [C.sb(es, [128, S], F32, "cacc") for _ in range(2)]
        for r_ in rawr:
            P.op("vector", lambda E, r_=r_: E.memset(r_.t[:, 0:3], 0.0), writes=[r_])
        for c in range(16):
            rw, acc = rawr[c % 2], accr[c % 2]
            P.dma("sync", rw.t[:, 3:3 + S], raw_d[c, :, :], pwrites=[rw])
            P.op("vector", lambda E, rw=rw, acc=acc, c=c: E.tensor_scalar(
                out=acc.t[:], in0=rw.t[:, 3:3 + S], scalar1=cw_sb.t[:, c, 3:4], scalar2=None, op0=ALU.mult),
                reads=[rw, cw_sb], writes=[acc])
            for k in range(3):
                P.op("vector", lambda E, rw=rw, acc=acc, c=c, k=k: E.scalar_tensor_tensor(
                    out=acc.t[:], in0=rw.t[:, k:k + S], scalar=cw_sb.t[:, c, k:k + 1], in1=acc.t[:],
                    op0=ALU.mult, op1=ALU.add), reads=[rw, cw_sb, acc], writes=[acc])
            P.op("scalar", lambda E, acc=acc, c=c: E.activation(out=xbcT.t[:, c, :], in_=acc.t[:], func=AF.Silu,
                                                                bias=cb_sb.t[:, c:c + 1]),
                 reads=[acc, cb_sb], pwrites=[xbcT])

        prev32 = C.sb(es, [128, 1024], F32, "prev32")
        prevbf = C.sb(es, [128, 1024], BF16, "prevbf")
        P.op("vector", lambda E: E.memset(prev32.t[:], 0.0), writes=[prev32])
        P.op("vector", lambda E: E.memset(prevbf.t[:], 0.0), writes=[prevbf])
        R2 = lambda shape, dt, nm: Stager(C, es, shape, dt, 2, nm)
        zar, dtr_, smr = R2([128, 1024], F32, "za"), R2([128, 16], F32, "dtraw"), R2([128, 8, 16], F32, "ssm")
        xsr, xdr, xder, btr = R2([128, 1024], BF16, "xs"), R2([128, 1024], BF16, "xd"), R2([128, 1024], BF16, "xde"), \
            R2([128, 512], BF16, "btm")
        cbr = R2([128, 4, 128], F32, "cbm")
        Ar, Er, MTr = Stager(C, es, [128, 4, 128], F32, 3, "A"), Stager(C, es, [128, 512], F32, 3, "E"), \
            Stager(C, es, [128, 4, 128], BF16, 4, "MT")
        ssd_deferred = []
        t1r, t3r, ynr = R2([128, 1024], F32, "t1"), R2([128, 1024], F32, "t3"), R2([128, 1024], BF16, "yn")
        junk = C.sb(es, [128, 256], BF16, "junk")
        ssr = R2([128, 8], F32, "gss")
        ystr = R2([128, 8, 128], BF16, "yst")
        bX, bT, bS0, bS1, bD0, bD1, bO0, bO1 = C.banks
        allsc = C.sb(es, [128, 8, 256], F32, "allsc")
        A_ = [allsc.t[:, i, :] for i in range(8)]
        A3 = [allsc.t[:, i, :].rearrange("p (n h) -> p n h", h=16) for i in range(8)]
        with nc_allow(C):
            P.dma("sync", A3[7], dt_d.rearrange("(n p) h -> p n h", p=128), writes=[allsc])
        P.op("vector", lambda E: E.tensor_tensor(out=A3[7], in0=A3[7], in1=dtb.t[:].unsqueeze(1).to_broadcast([128, 16, 16]),
                                                 op=ALU.add), reads=[allsc, dtb], writes=[allsc])
        P.op("scalar", lambda E: E.activation(out=A_[7], in_=A_[7], func=AF.Exp), reads=[allsc], writes=[allsc])
        P.op("scalar", lambda E: E.activation(out=A_[0], in_=A_[7], func=AF.Ln, bias=1.0), reads=[allsc], writes=[allsc])
        P.op("vector", lambda E: E.tensor_tensor(out=A3[1], in0=A3[0], in1=a_bc.t[:].unsqueeze(1).to_broadcast([128, 16, 16]),
                                                 op=ALU.mult), reads=[allsc, a_bc], writes=[allsc])

        def csmm(E):
            E.matmul(bX.t[:, 0:256], Uf.t[:], A_[1], start=True, stop=True)
            return E.matmul(bT.t[:, 0:256], ones.t[:], A_[1], start=True, stop=True)

        P.op("tensor", csmm, reads=[allsc, Uf, ones], writes=[bX, bT])
        P.op("scalar", lambda E: E.copy(out=A_[2], in_=bX.t[:, 0:256]), reads=[bX], writes=[allsc])
        P.op("scalar", lambda E: E.activation(out=A_[3], in_=bX.t[:, 0:256], func=AF.Exp), reads=[bX], writes=[allsc])
        P.op("scalar", lambda E: E.activation(out=A_[5], in_=bT.t[:, 0:256], func=AF.Exp), reads=[bT], writes=[allsc])
        P.op("vector", lambda E: E.tensor_tensor(out=A_[7], in0=bT.t[:, 0:256], in1=A_[2], op=ALU.subtract),
             reads=[bT, allsc], writes=[allsc])
        P.op("scalar", lambda E: E.activation(out=A_[4], in_=A_[7], func=AF.Exp), reads=[allsc], writes=[allsc])
        P.op("vector", lambda E: E.tensor_tensor(out=A_[6], in0=A_[0], in1=A_[4], op=ALU.mult), reads=[allsc], writes=[allsc])
        for n in range(16):
            tok = slice(n * 128, (n + 1) * 128)
            za = zar.next()
            P.dma("sync", za.t[:], za_d[tok, :], writes=[za])
            sm = allsc
            hs = slice(n * 16, (n + 1) * 16)
            DT, DA, ECS, CD, DTD = [allsc.t[:, i, hs] for i in (0, 1, 3, 5, 6)]
            xs, xd, xde, btm = xsr.next(), xdr.next(), xder.next(), btr.next()

            def trx(E, tok=tok):
                ins = None
                for c in range(8):
                    ins = E.transpose(out=bfv(bT)[:, c * 128:(c + 1) * 128], in_=xbcT.t[:, c, tok], identity=ident.t[:])
                return ins

            P.op("tensor", trx, reads=[xbcT, ident], writes=[bT])
            P.op("scalar", lambda E, xs=xs: E.copy(out=xs.t[:], in_=bfv(bT)[:, 0:1024]), reads=[bT], writes=[xs])
            P.op("vector", lambda E, xs=xs, xd=xd, DT=DT: E.tensor_tensor(
                out=xd.t[:].rearrange("p (h d) -> p h d", d=64), in0=xs.t[:].rearrange("p (h d) -> p h d", d=64),
                in1=bc3(DT, 64), op=ALU.mult), reads=[xs, sm], writes=[xd])
            P.op("gpsimd", lambda E, xs=xs, xde=xde, DTD=DTD: E.tensor_tensor(
                out=xde.t[:].rearrange("p (h d) -> p h d", d=64), in0=xs.t[:].rearrange("p (h d) -> p h d", d=64),
                in1=bc3(DTD, 64), op=ALU.mult), reads=[xs, sm], writes=[xde])

            def trb(E, tok=tok):
                ins = None
                for g in range(4):
                    ins = E.transpose(out=bfv(bT)[:, g * 128:(g + 1) * 128], in_=xbcT.t[:, 8 + g, tok],
                                      identity=ident.t[:])
                return ins

            P.op("tensor", trb, reads=[xbcT, ident], writes=[bT])
            P.op("scalar", lambda E, btm=btm: E.copy(out=btm.t[:], in_=bfv(bT)[:, 0:512]), reads=[bT], writes=[btm])
            cbm = cbr.next()

            def cbmm(E, tok=tok):
                ins = None
                for g in range(4):
                    ins = E.matmul(bX.t[:, g * 128:(g + 1) * 128], xbcT.t[:, 8 + g, tok], xbcT.t[:, 12 + g, tok],
                                   start=True, stop=True)
                return ins

            P.op("tensor", cbmm, reads=[xbcT], writes=[bX])
            P.op("vector", lambda E, cbm=cbm: E.tensor_tensor(
                out=cbm.t[:], in0=bX.t[:, 0:512].rearrange("p (g l) -> p g l", l=128),
                in1=Uf.t[:].unsqueeze(1).to_broadcast([128, 4, 128]), op=ALU.mult), reads=[bX, Uf], writes=[cbm])
            while ssd_deferred:
                ssd_deferred.pop(0)()
            grp = []
            for g in range(4):
                grp.append((Ar.next(), Er.next(), MTr.next(), bS0 if g % 2 == 0 else bS1, bD0 if g < 2 else bD1))

            def front(g, grp=grp, DA=DA, sm=sm):
                A, Eb, MT, bS, bD = grp[g]
                P.op("gpsimd", lambda E: E.tensor_tensor(
                    out=A.t[:], in0=Lgt.t[:].unsqueeze(1).to_broadcast([128, 4, 128]),
                    in1=bc3(DA[:, 4 * g:4 * g + 4], 128), op=ALU.mult), reads=[Lgt, sm], writes=[A])

                def segmm(E):
                    ins = None
                    for j in range(4):
                        ins = E.matmul(bS.t[:, j * 128:(j + 1) * 128], A.t[:, j, :], Uf.t[:], start=True, stop=True)
                    return ins

                P.op("tensor", segmm, reads=[A, Uf], writes=[bS])
                P.op("scalar", lambda E: E.activation(out=Eb.t[:], in_=bS.t[:, 0:512], func=AF.Exp),
                     reads=[bS], writes=[Eb])

            def back(g, grp=grp, cbm=cbm, xd=xd):
                A, Eb, MT, bS, bD = grp[g]
                P.op("vector", lambda E: E.tensor_tensor(
                    out=MT.t[:], in0=Eb.t[:].rearrange("p (j l) -> p j l", l=128),
                    in1=cbm.t[:, g, :].unsqueeze(1).to_broadcast([128, 4, 128]), op=ALU.mult),
                    reads=[Eb, cbm], writes=[MT])

                def ydmm(E):
                    ins = None
                    for j in range(4):
                        h = 4 * g + j
                        col = (h % 8) * 64
                        ins = E.matmul(bD.t[:, col:col + 64], MT.t[:, j, :], xd.t[:, h * 64:(h + 1) * 64],
                                       start=True, stop=True)
                    return ins

                P.op("tensor", ydmm, reads=[MT, xd], writes=[] if g % 2 == 1 else [bD], pwrites=[bD] if g % 2 == 1 else [])

            front(0)
            front(1)
            back(0)
            front(2)
            back(1)
            front(3)
            back(2)
            back(3)

            def yomm(E, tok=tok):
                ins = None
                for g in range(4):
                    bO = bO0 if g < 2 else bO1
                    col = (g % 2) * 256
                    ins = E.matmul(bO.t[:, col:col + 256], xbcT.t[:, 12 + g, tok], prevbf.t[:, g * 256:(g + 1) * 256],
                                   start=True, stop=True)
                return ins

            P.op("tensor", yomm, reads=[xbcT, prevbf], writes=[bO0, bO1])

            def stmm(E, btm=btm, xde=xde):
                ins = None
                for g in range(4):
                    bS = bS0 if g < 2 else bS1
                    col = (g % 2) * 256
                    ins = E.matmul(bS.t[:, col:col + 256], btm.t[:, g * 128:(g + 1) * 128],
                                   xde.t[:, g * 256:(g + 1) * 256], start=True, stop=True)
                return ins

            P.op("tensor", stmm, reads=[btm, xde], writes=[bS0, bS1])
            P.op("vector", lambda E, CD=CD: E.tensor_tensor(
                out=prev32.t[:].rearrange("p (h d) -> p h d", d=64), in0=prev32.t[:].rearrange("p (h d) -> p h d", d=64),
                in1=bc3(CD, 64), op=ALU.mult), reads=[prev32, sm], writes=[prev32])
            for hb, bS in enumerate([bS0, bS1]):
                sl = slice(hb * 512, (hb + 1) * 512)
                P.op("vector", lambda E, bS=bS, sl=sl: E.tensor_tensor(out=prev32.t[:, sl], in0=prev32.t[:, sl],
                                                                       in1=bS.t[:, 0:512], op=ALU.add),
                     reads=[bS, prev32], writes=[prev32])
            P.op("scalar", lambda E: E.copy(out=prevbf.t[:], in_=prev32.t[:]), reads=[prev32], writes=[prevbf])
            t1, t3, yn, gss = t1r.next(), t3r.next(), ynr.next(), ssr.next()
            P.op("gpsimd", lambda E, t3=t3, xs=xs: E.tensor_tensor(
                out=t3.t[:].rearrange("p (h d) -> p h d", d=64), in0=xs.t[:].rearrange("p (h d) -> p h d", d=64),
                in1=bc3(dsk.t[:], 64), op=ALU.mult), reads=[xs, dsk], writes=[t3])
            for hb, (bO, bD) in enumerate([(bO0, bD0), (bO1, bD1)]):
                sl = slice(hb * 512, (hb + 1) * 512)
                P.op("vector", lambda E, t1=t1, bO=bO, ECS=ECS, hb=hb, sl=sl: E.tensor_tensor(
                    out=t1.t[:, sl].rearrange("p (h d) -> p h d", d=64),
                    in0=bO.t[:, 0:512].rearrange("p (h d) -> p h d", d=64),
                    in1=bc3(ECS[:, hb * 8:(hb + 1) * 8], 64), op=ALU.mult), reads=[bO, sm], pwrites=[t1])
                P.op("vector", lambda E, t1=t1, bD=bD, sl=sl: E.tensor_tensor(
                    out=t1.t[:, sl], in0=t1.t[:, sl], in1=bD.t[:, 0:512], op=ALU.add), reads=[bD, t1], writes=[t1])
            P.op("vector", lambda E, t1=t1, t3=t3: E.tensor_tensor(out=t1.t[:], in0=t1.t[:], in1=t3.t[:], op=ALU.add),
                 reads=[t1, t3], writes=[t1])
            P.op("vector", lambda E, t1=t1, za=za: E.tensor_tensor(out=t1.t[:], in0=t1.t[:], in1=za.t[:], op=ALU.mult),
                 reads=[t1, za], writes=[t1])
            for gi in range(4):
                P.op("scalar", lambda E, t1=t1, gss=gss, gi=gi: E.activation(
                    out=junk.t[:], in_=t1.t[:, gi * 256:(gi + 1) * 256], func=AF.Square,
                    accum_out=gss.t[:, gi:gi + 1]), reads=[t1], writes=[junk, gss])
            P.op("scalar", lambda E, gss=gss: E.activation(out=gss.t[:, 4:8], in_=gss.t[:, 0:4], func=AF.Sqrt,
                                                           scale=1.0 / 256, bias=EPS), reads=[gss], writes=[gss])
            P.op("vector", lambda E, gss=gss: E.reciprocal(out=gss.t[:, 4:8], in_=gss.t[:, 4:8]),
                 reads=[gss], writes=[gss])
            P.op("vector", lambda E, t1=t1, gss=gss: E.tensor_tensor(
                out=t1.t[:].rearrange("p (g d) -> p g d", d=256), in0=t1.t[:].rearrange("p (g d) -> p g d", d=256),
                in1=bc3(gss.t[:, 4:8], 256), op=ALU.mult), reads=[t1, gss], writes=[t1])
            P.op("gpsimd", lambda E, t1=t1, yn=yn: E.tensor_tensor(out=yn.t[:], in0=t1.t[:], in1=nrmw.t[:], op=ALU.mult),
                 reads=[t1, nrmw], writes=[yn])
            def late(yn=yn, n=n):
                yst = ystr.next()

                def try_(E, yn=yn):
                    ins = None
                    for c in range(8):
                        ins = E.transpose(out=bfv(bT)[:, c * 128:(c + 1) * 128], in_=yn.t[:, c * 128:(c + 1) * 128],
                                          identity=ident.t[:])
                    return ins

                P.op("tensor", try_, reads=[yn, ident], writes=[bT])
                P.op("scalar", lambda E, yst=yst: E.copy(out=yst.t[:], in_=bfv(bT)[:, 0:1024].rearrange("p (c t) -> p c t", t=128)),
                     reads=[bT], writes=[yst])
                if YT_FLAT:
                    P.dma("sync", yT_d[0:1024, n * 128:(n + 1) * 128].rearrange("(c p) t -> p c t", p=128), yst.t[:],
                          reads=[yst], dram_write=True)
                else:
                    j, off = n // 8, (n % 8) * 128
                    P.dma("sync", yT_d[j, 0:1024, off:off + 128].rearrange("(c p) t -> p c t", p=128), yst.t[:],
                          reads=[yst], dram_write=True)

            ssd_deferred.append(late)
        while ssd_deferred:
            ssd_deferred.pop(0)()
        P.end_phase()
        P.release_bufs()
    if ygather is not None:
        for k in range(4):
            ygather(k)

    with ExitStack() as es:
        lnw = C.sb(es, [128, 1024], F32, "lnw")
        lnb = C.sb(es, [128, 1024], F32, "lnb")
        P.dma("sync", lnw.t[:], prm["sgu_ln_w"].partition_broadcast(128), writes=[lnw])
        P.dma("sync", lnb.t[:], prm["sgu_ln_b"].partition_broadcast(128), writes=[lnb])
        sb_bc = C.sb(es, [128, 8, 128], F32, "sgub")
        P.dma("sync", sb_bc.t[:], prm["sgu_b"].rearrange("g t -> (g t)").partition_broadcast(128)
              .rearrange("p (g t) -> p g t", g=8), writes=[sb_bc])
        wsf = C.sb(es, [128, 8, 128], F32, "wsf")
        P.dma("sync", wsf.t[:], prm["wsT"].rearrange("g s t -> s g t"), writes=[wsf])
        wsm = C.sb(es, [128, 8, 128], BF16, "wsm")
        P.op("vector", lambda E: E.tensor_tensor(out=wsm.t[:], in0=wsf.t[:],
                                                 in1=Uf.t[:].unsqueeze(1).to_broadcast([128, 8, 128]), op=ALU.mult),
             reads=[wsf, Uf], writes=[wsm])
        vn = C.sb(es, [128, 16, 1024], BF16, "vn")
        vgr = [C.sb(es, [128, 2048], F32, "vg") for _ in range(2)]
        junk = C.sb(es, [128, 2048], BF16, "junkv")
        vtr = [C.sb(es, [128, 1024], F32, "vt") for _ in range(2)]
        str_ = Stager(C, es, [128, 8], F32, 2, "lnst")
        for tt in range(16):
            vg, vt, s_ = vgr[tt % 2], vtr[tt % 2], str_.next()
            P.dma("sync", vg.t[:], vg_d[tt * 128:(tt + 1) * 128, :], writes=[vg])
            P.op("vector", lambda E, vg=vg, s_=s_: E.tensor_reduce(out=s_.t[:, 0:1], in_=vg.t[:], axis=AX.X, op=ALU.add),
                 reads=[vg], writes=[s_])
            P.op("scalar", lambda E, vg=vg, s_=s_: E.activation(out=junk.t[:], in_=vg.t[:], func=AF.Square,
                                                                accum_out=s_.t[:, 1:2]), reads=[vg, s_], writes=[junk, s_])
            P.op("vector", lambda E, s_=s_: E.tensor_scalar(out=s_.t[:, 2:3], in0=s_.t[:, 0:1], scalar1=1.0 / 2048,
                                                            scalar2=None, op0=ALU.mult), reads=[s_], writes=[s_])
            P.op("vector", lambda E, s_=s_: E.tensor_tensor(out=s_.t[:, 3:4], in0=s_.t[:, 2:3], in1=s_.t[:, 2:3],
                                                            op=ALU.mult), reads=[s_], writes=[s_])
            P.op("vector", lambda E, s_=s_: E.scalar_tensor_tensor(out=s_.t[:, 4:5], in0=s_.t[:, 1:2], scalar=1.0 / 2048,
                                                                   in1=s_.t[:, 3:4], op0=ALU.mult, op1=ALU.subtract),
                 reads=[s_], writes=[s_])
            P.op("scalar", lambda E, s_=s_: E.activation(out=s_.t[:, 5:6], in_=s_.t[:, 4:5], func=AF.Sqrt, bias=EPS),
                 reads=[s_], writes=[s_])
            P.op("vector", lambda E, s_=s_: E.reciprocal(out=s_.t[:, 5:6], in_=s_.t[:, 5:6]), reads=[s_], writes=[s_])
            P.op("vector", lambda E, s_=s_: E.scalar_tensor_tensor(out=s_.t[:, 6:7], in0=s_.t[:, 2:3], scalar=-1.0,
                                                                   in1=s_.t[:, 5:6], op0=ALU.mult, op1=ALU.mult),
                 reads=[s_], writes=[s_])
            P.op("scalar", lambda E, vg=vg, vt=vt, s_=s_: E.activation(out=vt.t[:], in_=vg.t[:, 0:1024], func=AF.Identity,
                                                                       scale=s_.t[:, 5:6], bias=s_.t[:, 6:7]),
                 reads=[vg, s_], writes=[vt])
            P.op("vector", lambda E, vt=vt: E.tensor_tensor(out=vt.t[:], in0=vt.t[:], in1=lnw.t[:], op=ALU.mult),
                 reads=[vt, lnw], writes=[vt])
            P.op("vector", lambda E, vt=vt, tt=tt: E.tensor_tensor(out=vn.t[:, tt, :], in0=vt.t[:], in1=lnb.t[:], op=ALU.add),
                 reads=[vt, lnb], pwrites=[vn])
        gur = [C.sb(es, [128, S], F32, "gu") for _ in range(2)]
        szr = [C.sb(es, [128, S], F32, "sz") for _ in range(2)]
        mr = Stager(C, es, [128, 512], F32, 2, "m")
        ybr = [C.sb(es, [128, S], BF16, "ybT") for _ in range(2)]
        nb = [0]
        sgu_toks = []
        for g in range(8):
            gu, sz, yb = gur[g % 2], szr[g % 2], ybr[g % 2]
            P.dma("sync", gu.t[:], u_d[g, :, :], writes=[gu])
            P.dma("sync", sz.t[:], zb_d[g, :, :], writes=[sz])
            for tb in range(4):
                bank = C.banks[nb[0] % 4]
                nb[0] += 1

                def spmm(E, bank=bank, tb=tb, g=g):
                    ins = None
                    for i in range(4):
                        n = 4 * tb + i
                        ins = E.matmul(bank.t[:, i * 128:(i + 1) * 128], vn.t[:, n, g * 128:(g + 1) * 128],
                                       wsm.t[:, g, :], start=True, stop=True)
                    return ins

                P.op("tensor", spmm, reads=[vn, wsm], writes=[bank])
                m = mr.next()
                sl = slice(tb * 512, (tb + 1) * 512)
                P.op("vector", lambda E, m=m, bank=bank, g=g: E.tensor_tensor(
                    out=m.t[:].rearrange("p (i t) -> p i t", t=128), in0=bank.t[:, 0:512].rearrange("p (i t) -> p i t", t=128),
                    in1=sb_bc.t[:, g, :].unsqueeze(1).to_broadcast([128, 4, 128]), op=ALU.add),
                    reads=[bank, sb_bc], writes=[m])
                P.op("gpsimd", lambda E, m=m, gu=gu, sl=sl: E.tensor_tensor(out=m.t[:], in0=m.t[:], in1=gu.t[:, sl],
                                                                            op=ALU.mult), reads=[m, gu], writes=[m])
                P.op("vector", lambda E, m=m, sz=sz, yb=yb, sl=sl: E.tensor_tensor(out=yb.t[:, sl], in0=m.t[:],
                                                                                   in1=sz.t[:, sl], op=ALU.mult),
                     reads=[m, sz], pwrites=[yb])
            if YT_FLAT:
                tok = P.dma("sync", yT_d[1024 + g * 128:1024 + (g + 1) * 128, :], yb.t[:], reads=[yb], dram_write=True)
                sgu_toks.append(tok)
                if ygather is not None and g % 2 == 1:
                    for t_ in sgu_toks[-2:]:
                        P._wait("gpsimd", t_)
                    ygather(4 + g // 2)
            else:
                P.dma("sync", yT_d[:, 1024 + g * 128:1024 + (g + 1) * 128, :].rearrange("j p t -> p j t"),
                      yb.t[:].rearrange("p (j t) -> p j t", j=2), reads=[yb], dram_write=True)
        P.end_phase()
        P.release_bufs()


def _consts():
    i = np.arange(128)
    U = (i[:, None] <= i[None, :]).astype(np.float32)
    return {
        "c_ident": np.eye(128, dtype=np.float32).astype(NPBF),
        "c_Ubf": U.astype(NPBF),
        "c_Uf": U,
        "c_Lgt": (i[:, None] > i[None, :]).astype(np.float32),
        "c_ones": np.ones((128, 128), np.float32),
    }


CONST_SPECS = [("c_ident", BF16), ("c_Ubf", BF16), ("c_Uf", F32), ("c_Lgt", F32), ("c_ones", F32)]


def load_consts(C, es, nc):
    cst = {}
    for nm, dt in CONST_SPECS:
        d = nc.dram_tensor(nm, [128, 128], dt, kind="ExternalInput").ap()
        b = C.sb(es, [128, 128], dt, nm)
        C.P.dma("sync", b.t[:], d, writes=[b])
        cst[nm[2:]] = b
    return cst


def _din(nc, name, shape, dt):
    return nc.dram_tensor(name, list(shape), dt, kind="ExternalInput").ap()


def _dout(nc, name, shape, dt):
    return nc.dram_tensor(name, list(shape), dt, kind="ExternalOutput").ap()


def _dint(nc, name, shape, dt):
    return nc.dram_tensor(name, list(shape), dt).ap()


MIX0_PRM = [("conv_w", [128, 16, 4]), ("conv_b", [128, 16]), ("dt_bias", [16]), ("a_log", [16]), ("d_skip", [16]),
            ("ssd_norm_w", [1024]), ("sgu_ln_w", [1024]), ("sgu_ln_b", [1024]), ("wsT", [8, 128, 128]),
            ("sgu_b", [8, 128])]


def mix0_scratch(nc):
    return {"raw": _dint(nc, "s_raw", [16, 128, S], F32), "zb": _dint(nc, "s_zb", [8, 128, S], F32),
            "u": _dint(nc, "s_u", [8, 128, S], F32), "za": _dint(nc, "s_za", [S, 1024], F32),
            "dt": _dint(nc, "s_dt", [S, 16], F32), "vg": _dint(nc, "s_vg", [S, 2048], F32)}


def mix1_scratch(nc):
    return {"qkT": _dint(nc, "s_qkT", [32, 128, S], BF16), "v": _dint(nc, "s_v", [S, 8, 256], BF16),
            "sg": _dint(nc, "s_sg", [S, 8, 256], F32)}


def build_launch(which):
    nc = bass.Bass("TRN2", target_bir_lowering=False)
    with ExitStack() as es:
        C = Ctx(nc, es)
        cst = load_consts(C, es, nc)
        if which == "normt":
            phase_normt(C, _din(nc, "x", [TL, D], F32), _din(nc, "nw", [D], F32), _dout(nc, "hnT", [D, TL], BF16),
                        cst["ident"])
        elif which == "mix0":
            prm = {k: _din(nc, "p_" + k, shp, F32) for k, shp in MIX0_PRM}
            phase_mix0(C, _din(nc, "hnT_all", [2, D, TL], BF16), _din(nc, "W0", [D, W0_COLS], F32), prm,
                       _dout(nc, "yT", [2, 2048, TL], BF16), mix0_scratch(nc), cst)
        elif which == "mix1":
            phase_mix1(C, _din(nc, "hnT_all", [2, D, TL], BF16), _din(nc, "W1", [D, 8192], F32),
                       _din(nc, "lam", [4, 128], F32), _din(nc, "subw", [256], F32),
                       _dout(nc, "yT", [2, 2048, TL], BF16), mix1_scratch(nc), cst)
        elif which == "out_norm":
            h = _dout(nc, "h", [TL, D], F32)
            phase_out(C, _din(nc, "yTt", [4096, TL], BF16), _din(nc, "Wo", [4096, D], F32),
                      _din(nc, "res", [TL, D], F32), h)
            phase_normt(C, h, _din(nc, "nw", [D], F32), _dout(nc, "hnT", [D, TL], BF16), cst["ident"])
        elif which == "out_final":
            h = _dint(nc, "h", [TL, D], F32)
            phase_out(C, _din(nc, "yTt", [4096, TL], BF16), _din(nc, "Wo", [4096, D], F32),
                      _din(nc, "res", [TL, D], F32), h)
            phase_finalnorm(C, h, _din(nc, "nw", [D], F32), _dout(nc, "out", [TL, D], F32))
    return nc


def _run(nc, in_maps):
    res = run_bass_kernel_spmd(nc, in_maps, core_ids=list(range(8)))
    return res.results


def _f32(a):
    return np.ascontiguousarray(np.asarray(a), dtype=np.float32)


def prep_layer0(r, w_in, conv_w, conv_b, dt_bias, a_log, d_skip, ssd_norm_w, ln_w, ln_b, ws, sb):
    o = 1024 * r
    cols = np.concatenate([
        2048 + o + np.arange(1024), 4096 + 512 * r + np.arange(512), 5120 + 512 * r + np.arange(512),
        6176 + o + np.arange(1024), 8224 + o + np.arange(1024),
        o + np.arange(1024), 6144 + 16 * r + np.arange(16),
        10272 + o + np.arange(1024), 10272 + 1024 * (1 - r) + np.arange(1024)])
    cidx = np.concatenate([o + np.arange(1024), 2048 + 512 * r + np.arange(512), 3072 + 512 * r + np.arange(512)])
    return {
        "W0": np.ascontiguousarray(w_in[:, cols]),
        "p_conv_w": np.ascontiguousarray(conv_w[:, cidx].T.reshape(16, 128, 4).transpose(1, 0, 2)),
        "p_conv_b": np.ascontiguousarray(conv_b[cidx].reshape(16, 128).T),
        "p_dt_bias": np.ascontiguousarray(dt_bias[16 * r:16 * r + 16]),
        "p_a_log": np.ascontiguousarray(a_log[16 * r:16 * r + 16]),
        "p_d_skip": np.ascontiguousarray(d_skip[16 * r:16 * r + 16]),
        "p_ssd_norm_w": np.ascontiguousarray(ssd_norm_w[o:o + 1024]),
        "p_sgu_ln_w": np.ascontiguousarray(ln_w[o:o + 1024]), "p_sgu_ln_b": np.ascontiguousarray(ln_b[o:o + 1024]),
        "p_wsT": np.ascontiguousarray(ws[8 * r:8 * r + 8].transpose(0, 2, 1)),
        "p_sgu_b": np.ascontiguousarray(sb[8 * r:8 * r + 8]),
    }


def prep_layer1(r, w_in):
    cols = []
    for hh in range(8):
        h = 8 * r + hh
        for base in (0, 4096, 8192, 12288):
            cols.append(base + h * 256 + np.arange(256))
    return np.ascontiguousarray(w_in[:, np.concatenate(cols)])


PAIRS = [[0, 1], [2, 3], [4, 5], [6, 7]]
DC = D // 2
CCH = 256


def gather_rows(P, send, gathered, nrows):
    for k in range(nrows // CCH):
        P.collective("AllGather", [send[k * CCH:(k + 1) * CCH, :]],
                     [gathered[2 * k * CCH:(2 * k + 2) * CCH, :]], PAIRS)
    P.end_phase()


def prefetch_wo(C, es, W_d):
    w = C.sb(es, [128, 32, DC], BF16, "wo")
    for hf in range(2):
        C.P.dma("gpsimd", w.t[:, :, hf * 512:(hf + 1) * 512],
                W_d[:, hf * 512:(hf + 1) * 512].rearrange("(kc p) c -> p kc c", p=128), pwrites=[w])
    return w


def phase_out_cs(C, yg, w, res_d, h_d):
    P = C.P
    with ExitStack() as es:
        yring = [C.sb(es, [128, 32, 512], BF16, "yTq") for _ in range(2)]
        rring = Stager(C, es, [128, 512], F32, 3, "res")
        oring = Stager(C, es, [128, 512], F32, 3, "ho")
        nb = 0
        for tq in range(4):
            yT = yring[tq % 2]
            for k in range(8):
                for s_ in range(2):
                    r0 = (2 * k + s_) * CCH
                    P.dma("sync", yT.t[:, s_ * 16 + 2 * k:s_ * 16 + 2 * k + 2, :],
                          yg[r0:r0 + CCH, tq * 512:(tq + 1) * 512].rearrange("(kc p) t -> p kc t", p=128),
                          pwrites=[yT])
            for tt in range(4):
                rows = slice(tq * 512 + tt * 128, tq * 512 + (tt + 1) * 128)
                for cb in range(2):
                    bank = C.banks[nb % 4]
                    nb += 1
                    pairs = [(yT.t[:, k, tt * 128:(tt + 1) * 128], w.t[:, k, cb * 512:(cb + 1) * 512])
                             for k in range(32)]
                    mm_group(P, bank, bank.t[:, 0:512], pairs, reads=[yT, w])
                    r, o = rring.next(), oring.next()
                    P.dma("sync", r.t[:], res_d[rows, cb * 512:(cb + 1) * 512], writes=[r])
                    P.op("vector", lambda E, o=o, bank=bank, r=r: E.tensor_tensor(out=o.t[:], in0=bank.t[:, 0:512],
                                                                                  in1=r.t[:], op=ALU.add),
                         reads=[bank, r], writes=[o])
                    P.dma("sync", h_d[rows, cb * 512:(cb + 1) * 512], o.t[:], reads=[o], dram_write=True)
        P.end_phase()
        P.release_bufs()


def phase_norm_cs(C, h_d, nw_d, ss_send, ss_g, ident, hn_send=None, hn_g=None, out_d=None):
    P = C.P
    with ExitStack() as es:
        nwbc = C.sb(es, [128, DC], F32, "nwbc")
        P.dma("sync", nwbc.t[:], nw_d.partition_broadcast(128), writes=[nwbc])
        hres = C.sb(es, [128, 16, DC], F32, "hres")
        junk = C.sb(es, [128, DC], BF16, "junk")
        ssc = C.sb(es, [128, 16], F32, "ssc")
        for tt in range(16):
            P.dma("sync", hres.t[:, tt, :], h_d[tt * 128:(tt + 1) * 128, :], pwrites=[hres])
        for tt in range(16):
            P.op("scalar", lambda E, tt=tt: E.activation(out=junk.t[:], in_=hres.t[:, tt, :], func=AF.Square,
                                                         accum_out=ssc.t[:, tt:tt + 1]),
                 reads=[hres], writes=[junk], pwrites=[ssc])
        P.dma("sync", ss_send, ssc.t[:], reads=[ssc], dram_write=True)
        P.end_phase()
        P.collective("AllGather", [ss_send], [ss_g], PAIRS)
        P.end_phase()
        ss2 = C.sb(es, [128, 2, 16], F32, "ss2")
        P.dma("sync", ss2.t[:], ss_g.rearrange("(s p) t -> p s t", p=128), writes=[ss2])
        rs = C.sb(es, [128, 16], F32, "rs")
        P.op("vector", lambda E: E.tensor_tensor(out=rs.t[:], in0=ss2.t[:, 0, :], in1=ss2.t[:, 1, :], op=ALU.add),
             reads=[ss2], writes=[rs])
        P.op("scalar", lambda E: E.activation(out=rs.t[:], in_=rs.t[:], func=AF.Sqrt, scale=1.0 / D, bias=EPS),
             reads=[rs], writes=[rs])
        P.op("vector", lambda E: E.reciprocal(out=rs.t[:], in_=rs.t[:]), reads=[rs], writes=[rs])
        if out_d is not None:
            orr = [C.sb(es, [128, DC], F32, "o") for _ in range(2)]
            for tt in range(16):
                o = orr[tt % 2]
                P.op("vector", lambda E, o=o, tt=tt: E.scalar_tensor_tensor(
                    out=o.t[:], in0=hres.t[:, tt, :], scalar=rs.t[:, tt:tt + 1], in1=nwbc.t[:], op0=ALU.mult,
                    op1=ALU.mult), reads=[hres, rs, nwbc], writes=[o])
                P.dma("sync", out_d[tt * 128:(tt + 1) * 128, :], o.t[:], reads=[o], dram_write=True)
            P.end_phase()
            P.release_bufs()
            return
        hnr = [C.sb(es, [128, DC], BF16, "hn") for _ in range(2)]
        hnT = C.sb(es, [128, 8, S], BF16, "hnTo")
        for tt in range(16):
            hn = hnr[tt % 2]
            P.op("vector", lambda E, hn=hn, tt=tt: E.scalar_tensor_tensor(
                out=hn.t[:], in0=hres.t[:, tt, :], scalar=rs.t[:, tt:tt + 1], in1=nwbc.t[:], op0=ALU.mult,
                op1=ALU.mult), reads=[hres, rs, nwbc], writes=[hn])
            bank = C.banks[tt % 4]

            def tr(E, hn=hn, bank=bank):
                ins = None
                for j in range(8):
                    ins = E.transpose(out=bfv(bank)[:, j * 128:(j + 1) * 128], in_=hn.t[:, j * 128:(j + 1) * 128],
                                      identity=ident.t[:])
                return ins

            P.op("tensor", tr, reads=[hn, ident], writes=[bank])
            eng = "scalar" if tt % 2 == 0 else "vector"

            def cp(E, bank=bank, tt=tt, eng=eng):
                src = bfv(bank).rearrange("p (j t) -> p j t", t=128)
                dst = hnT.t[:, :, tt * 128:(tt + 1) * 128]
                return E.copy(out=dst, in_=src) if eng == "scalar" else E.tensor_copy(out=dst, in_=src)

            P.op(eng, cp, reads=[bank], pwrites=[hnT])
        P.dma("sync", hn_send.rearrange("(kc p) t -> p kc t", p=128), hnT.t[:], reads=[hnT], dram_write=True)
        P.end_phase()
        P.release_bufs()
    gather_rows(P, hn_send, hn_g, DC)


def build_fused():
    nc = bass.Bass("TRN2", target_bir_lowering=False)
    with ExitStack() as es:
        C = Ctx(nc, es)
        P = C.P
        cst = load_consts(C, es, nc)
        x_d = _din(nc, "x", [S, D], F32)
        xc_d = _din(nc, "xc", [S, DC], F32)
        nw0, nw1c, nwfc = _din(nc, "nw0", [D], F32), _din(nc, "nw1c", [DC], F32), _din(nc, "nwfc", [DC], F32)
        prm = {k: _din(nc, "p_" + k, shp, F32) for k, shp in MIX0_PRM}
        W0 = _din(nc, "W0", [D, W0_COLS], F32)
        Wo0 = _din(nc, "Wo0", [4096, DC], F32)
        W1 = _din(nc, "W1", [D, 8192], F32)
        lam = _din(nc, "lam", [4, 128], F32)
        subw = _din(nc, "subw", [256], F32)
        Wo1 = _din(nc, "Wo1", [4096, DC], F32)
        out_d = _dout(nc, "out", [S, DC], F32)
        hn0 = _dint(nc, "x_hn0", [D, S], BF16)
        hn_send = _dint(nc, "x_hnsend", [DC, S], BF16)
        hn_g = _dint(nc, "x_hng", [2 * DC, S], BF16)
        y_own = _dint(nc, "x_yown", [2048, S], BF16)
        y_g = _dint(nc, "x_yg", [4096, S], BF16)
        ss_send = _dint(nc, "x_sssend", [128, 16], F32)
        ss_g = _dint(nc, "x_ssg", [256, 16], F32)
        h1 = _dint(nc, "x_h1", [S, DC], F32)
        h2 = _dint(nc, "x_h2", [S, DC], F32)
        scr0, scr1 = mix0_scratch(nc), mix1_scratch(nc)
        hn0_blocks = [(hn0[q * 512:(q + 1) * 512, :], 4 * q, 4, 0, S) for q in range(4)]
        hn1_blocks = [(hn_g[(2 * k + s_) * CCH:(2 * k + s_ + 1) * CCH, :], s_ * 8 + 2 * k, 2, 0, S)
                      for k in range(4) for s_ in range(2)]

        def ygather(k):
            P.collective("AllGather", [y_own[k * CCH:(k + 1) * CCH, :]], [y_g[2 * k * CCH:(2 * k + 2) * CCH, :]], PAIRS)

        phase_normt(C, x_d, nw0, hn0, cst["ident"], NT=16)
        phase_mix0(C, hn0_blocks, W0, prm, y_own, scr0, cst, ygather)
        with ExitStack() as es2:
            w = prefetch_wo(C, es2, Wo0)
            P.end_phase()
            phase_out_cs(C, y_g, w, xc_d, h1)
        phase_norm_cs(C, h1, nw1c, ss_send, ss_g, cst["ident"], hn_send=hn_send, hn_g=hn_g)
        phase_mix1(C, hn1_blocks, W1, lam, subw, y_own, scr1, cst, ygather)
        with ExitStack() as es2:
            w = prefetch_wo(C, es2, Wo1)
            P.end_phase()
            phase_out_cs(C, y_g, w, h1, h2)
        phase_norm_cs(C, h2, nwfc, ss_send, ss_g, cst["ident"], out_d=out_d)
    return nc


WOUT0_PERM = np.concatenate([np.concatenate([s * 1024 + np.arange(1024), 2048 + s * 1024 + np.arange(1024)])
                             for s in range(2)])

DEBUG = {}
FUSED = True


def kernel(x, norm_w, even_w_in, even_conv_w, even_conv_b, even_dt_bias, even_a_log, even_d_skip,
           even_ssd_norm_w, even_sgu_ln_w, even_sgu_ln_b, even_sgu_ws, even_sgu_b, even_w_out, odd_w_in,
           odd_lam_q1, odd_lam_k1, odd_lam_q2, odd_lam_k2, odd_subln_w, odd_w_out, final_norm_w):
    x = _f32(x)
    norm_w = _f32(norm_w)
    cs = _consts()
    cores = [(b, r) for b in range(4) for r in range(2)]
    xo = [np.ascontiguousarray(x[b, r * TL:(r + 1) * TL, :]) for b, r in cores]
    l0 = [prep_layer0(r, _f32(even_w_in)[0], _f32(even_conv_w)[0], _f32(even_conv_b)[0], _f32(even_dt_bias)[0],
                      _f32(even_a_log)[0], _f32(even_d_skip)[0], _f32(even_ssd_norm_w)[0], _f32(even_sgu_ln_w)[0],
                      _f32(even_sgu_ln_b)[0], _f32(even_sgu_ws)[0], _f32(even_sgu_b)[0]) for r in range(2)]
    w1 = [prep_layer1(r, _f32(odd_w_in)[0]) for r in range(2)]
    wo0 = np.ascontiguousarray(_f32(even_w_out)[0][WOUT0_PERM])
    wo1 = _f32(odd_w_out)[0]
    lam = np.stack([_f32(odd_lam_q1)[0], _f32(odd_lam_k1)[0], _f32(odd_lam_q2)[0], _f32(odd_lam_k2)[0]])
    subw = _f32(odd_subln_w)[0]

    if FUSED:
        global YT_FLAT
        YT_FLAT = True
        maps = []
        fw = _f32(final_norm_w)
        for c, (b, r) in enumerate(cores):
            cols = slice(r * DC, (r + 1) * DC)
            m = dict(cs, x=np.ascontiguousarray(x[b]), xc=np.ascontiguousarray(x[b][:, cols]), nw0=norm_w[0],
                     nw1c=np.ascontiguousarray(norm_w[1][cols]), nwfc=np.ascontiguousarray(fw[cols]),
                     Wo0=np.ascontiguousarray(wo0[:, cols]), W1=w1[r], lam=lam, subw=subw,
                     Wo1=np.ascontiguousarray(wo1[:, cols]), **l0[r])
            maps.append(m)
        res = _run(build_fused(), maps)
        out = np.empty((4, S, D), np.float32)
        for c, (b, r) in enumerate(cores):
            out[b, :, r * DC:(r + 1) * DC] = np.asarray(res[c]["out"])
        return out

    def gather_hn(res):
        return [np.ascontiguousarray(np.stack([np.asarray(res[2 * b]["hnT"]), np.asarray(res[2 * b + 1]["hnT"])]))
                for b, r in cores]

    def a2a(res):
        return [np.ascontiguousarray(np.concatenate([np.asarray(res[2 * b]["yT"])[r],
                                                     np.asarray(res[2 * b + 1]["yT"])[r]], axis=0))
                for b, r in cores]

    res = _run(build_launch("normt"), [dict(cs, x=xo[c], nw=norm_w[0]) for c in range(8)])
    hn_all = gather_hn(res)
    res = _run(build_launch("mix0"), [dict(cs, hnT_all=hn_all[c], **l0[cores[c][1]]) for c in range(8)])
    yTt = a2a(res)
    DEBUG["y0"] = yTt
    res = _run(build_launch("out_norm"), [dict(cs, yTt=yTt[c], Wo=wo0, res=xo[c], nw=norm_w[1]) for c in range(8)])
    h1 = [np.asarray(res[c]["h"]) for c in range(8)]
    DEBUG["h1"] = h1
    hn_all = gather_hn(res)
    res = _run(build_launch("mix1"), [dict(cs, hnT_all=hn_all[c], W1=w1[cores[c][1]], lam=lam, subw=subw)
                                      for c in range(8)])
    yTt = a2a(res)
    DEBUG["y1"] = yTt
    res = _run(build_launch("out_final"), [dict(cs, yTt=yTt[c], Wo=wo1, res=h1[c], nw=_f32(final_norm_w))
                                           for c in range(8)])
    out = np.empty((4, S, D), np.float32)
    for c, (b, r) in enumerate(cores):
        out[b, r * TL:(r + 1) * TL, :] = np.asarray(res[c]["out"])
    return out
```

```python
import math
from contextlib import ExitStack

import numpy as np
import ml_dtypes
import concourse.bass as bass
import concourse.mybir as mybir
from concourse.bass_utils import run_bass_kernel_spmd

F32 = mybir.dt.float32
BF16 = mybir.dt.bfloat16
AF = mybir.ActivationFunctionType
ALU = mybir.AluOpType
AX = mybir.AxisListType
ENG = ["sync", "scalar", "vector", "gpsimd", "tensor"]
EPS = 1e-6
NPBF = ml_dtypes.bfloat16

D = 2048
S = 2048
TL = 1024
KC = D // 128


class Buf:
    def __init__(self, name, t=None):
        self.name = name
        self.t = t
        self.w = {}
        self.r = {}
        self.dsem = None
        self.excl = False


class Prog:
    def __init__(self, nc, es):
        self.nc = nc
        self.es = es
        self.ccsem = None
        self.q = {e: [] for e in ENG}
        self.cnt = {e: 0 for e in ENG}
        self.esem = {e: es.enter_context(nc.semaphore("prog_" + e)) for e in ENG}
        self.pool = [[es.enter_context(nc.semaphore("dsem%d" % i)), 0] for i in range(76)]
        self.free_slots = list(range(len(self.pool)))
        self.phase_slots = []
        self.waited = {e: {} for e in ENG}
        self.pending = []
        self.nblocks = 0

    def _wait(self, e, tok):
        if tok is None:
            return
        key, h, val = tok
        if self.waited[e].get(key, 0) >= val:
            return
        self.waited[e][key] = val
        self.q[e].append(lambda E, h=h, val=val: E.wait_ge(h, val))

    def _deps(self, e, reads, writes, pwrites=()):
        for b in reads:
            for tok in b.w.values():
                self._wait(e, tok)
            if b.excl:
                for tok in b.r.values():
                    if tok[0] != "E" + e:
                        self._wait(e, tok)
        for b in list(writes) + list(pwrites):
            if b in writes:
                for tok in b.w.values():
                    self._wait(e, tok)
            for tok in b.r.values():
                if tok[0] == "E" + e:
                    continue
                self._wait(e, tok)

    def _commit(self, tok, reads, writes, pwrites=()):
        for b in reads:
            if b in writes or b in pwrites:
                continue
            b.r[tok[0]] = tok
        for b in writes:
            b.w = {tok[0]: tok}
            b.r = {}
        for b in pwrites:
            b.w[tok[0]] = tok

    def op(self, e, fn, reads=(), writes=(), pwrites=()):
        self._deps(e, reads, writes, pwrites)
        self.cnt[e] += 1
        h = self.esem[e]
        self.q[e].append(lambda E, fn=fn, h=h: fn(E).then_inc(h, 1))
        tok = ("E" + e, h, self.cnt[e])
        self._commit(tok, reads, writes, pwrites)
        return tok

    def _slot(self, b):
        if b.dsem is None:
            i = self.free_slots.pop(0)
            self.phase_slots.append((b, i))
            b.dsem = i
        return b.dsem

    def dma(self, e, out, in_, reads=(), writes=(), pwrites=(), dram_write=False, sem_buf=None):
        self._deps(e, reads, writes, pwrites)
        sb = sem_buf if sem_buf is not None else (list(writes) + list(pwrites) + list(reads))[0]
        i = self._slot(sb)
        self.pool[i][1] += 16
        h, val = self.pool[i][0], self.pool[i][1]

        def issue(E, out=out, in_=in_, h=h):
            o_ = out(E) if callable(out) else out
            i_ = in_(E) if callable(in_) else in_
            return E.dma_start(out=o_, in_=i_).then_inc(h, 16)

        self.q[e].append(issue)
        tok = ("D%d" % i, h, val)
        self._commit(tok, reads, writes, pwrites)
        if dram_write:
            self.pending.append(tok)
        return tok

    def collective(self, kind, ins, outs, groups):
        if self.ccsem is None:
            self.ccsem = [self.es.enter_context(self.nc.semaphore("ccsem")), 0]
        self.ccsem[1] += 1
        h, val = self.ccsem
        self.q["gpsimd"].append(lambda E: E.collective_compute(kind, ALU.bypass, replica_groups=groups, ins=ins,
                                                               outs=outs).then_inc(h, 1))
        self.pending.append(("CC", h, val))

    def end_phase(self, last=False):
        for tok in self.pending:
            self._wait("sync", tok)
        self.pending = []
        nc = self.nc
        q = self.q
        with nc.Block() as block:
            for e in ENG:
                if not q[e]:
                    continue

                def body(E, lst=q[e]):
                    for fn in lst:
                        fn(E)

                getattr(block, e)(body)
        self.q = {e: [] for e in ENG}
        self.nblocks += 1

    def release_bufs(self):
        for b, i in self.phase_slots:
            b.dsem = None
            self.free_slots.append(i)
        self.phase_slots = []


class Ctx:
    def __init__(self, nc, es):
        self.nc = nc
        self.es = es
        self.P = Prog(nc, es)
        self.n = 0
        self.banks = []
        for i in range(8):
            t = es.enter_context(nc.psum_tensor("bank%d" % i, [128, 512], F32))
            self.banks.append(Buf("bank%d" % i, t))
            self.banks[-1].excl = True

    def sb(self, es, shape, dtype, name=None):
        self.n += 1
        nm = "%s_%d" % (name or "t", self.n)
        t = es.enter_context(self.nc.sbuf_tensor(nm, list(shape), dtype))
        return Buf(nm, t)


def bfv(bank):
    return bank.t[:].bitcast(BF16)


def load_w(P, wb, W, c0, cw, kc=KC):
    src = W[:, c0:c0 + cw].rearrange("(kc p) c -> p kc c", p=128)
    P.dma("gpsimd", wb.t[:, 0:kc, 0:cw], src, writes=[wb])


def mm_group(P, bank, out_ap, pairs, reads):
    def fn(E, pairs=pairs, out_ap=out_ap):
        n = len(pairs)
        ins = None
        for i, (l, r) in enumerate(pairs):
            ins = E.matmul(out_ap, l, r, start=(i == 0), stop=(i == n - 1))
        return ins

    return P.op("tensor", fn, reads=reads, writes=[bank])


def rstd_from_ss(P, ss, rstd, n):
    P.op("scalar", lambda E: E.activation(out=rstd.t[:], in_=ss.t[:], func=AF.Sqrt, scale=1.0 / n, bias=EPS),
         reads=[ss], writes=[rstd])
    P.op("vector", lambda E: E.reciprocal(out=rstd.t[:], in_=rstd.t[:]), reads=[rstd], writes=[rstd])


def phase_normt(C, x_d, nw_d, hnT_d, ident, NT=8, keep=None):
    P = C.P
    with ExitStack() as es_:
        es = keep if keep is not None else es_
        nwbc = C.sb(es, [128, D], F32, "nwbc")
        P.dma("sync", nwbc.t[:], nw_d.partition_broadcast(128), writes=[nwbc])
        xr = [C.sb(es, [128, D], F32, "x") for _ in range(2)]
        junk = C.sb(es, [128, D], BF16, "junk")
        ssr = [C.sb(es, [128, 1], F32, "ss") for _ in range(2)]
        rsr = [C.sb(es, [128, 1], F32, "rs") for _ in range(2)]
        hnr = [C.sb(es, [128, D], BF16, "hn") for _ in range(2)]
        hnT = C.sb(es, [128, KC, NT * 128], BF16, "hnT")
        for i in range(NT):
            x, ss, rs, hn = xr[i % 2], ssr[i % 2], rsr[i % 2], hnr[i % 2]
            P.dma("sync", x.t[:], x_d[i * 128:(i + 1) * 128, :], writes=[x])
            P.op("scalar", lambda E, x=x, ss=ss: E.activation(out=junk.t[:], in_=x.t[:], func=AF.Square,
                                                              accum_out=ss.t[:]),
                 reads=[x], writes=[junk, ss])
            rstd_from_ss(P, ss, rs, D)
            P.op("vector", lambda E, x=x, rs=rs, hn=hn: E.scalar_tensor_tensor(
                out=hn.t[:], in0=x.t[:], scalar=rs.t[:, 0:1], in1=nwbc.t[:], op0=ALU.mult, op1=ALU.mult),
                reads=[x, rs, nwbc], writes=[hn])
            for half in range(2):
                bank = C.banks[(2 * i + half) % 4]

                def tr(E, hn=hn, bank=bank, half=half):
                    ins = None
                    for j in range(8):
                        k = half * 8 + j
                        ins = E.transpose(out=bfv(bank)[:, j * 128:(j + 1) * 128],
                                          in_=hn.t[:, k * 128:(k + 1) * 128], identity=ident.t[:])
                    return ins

                P.op("tensor", tr, reads=[hn, ident], writes=[bank])
                eng = "scalar" if half == 0 else "vector"

                def cp(E, bank=bank, half=half, i=i, eng=eng):
                    src = bfv(bank).rearrange("p (j t) -> p j t", t=128)
                    dst = hnT.t[:, half * 8:(half + 1) * 8, i * 128:(i + 1) * 128]
                    if eng == "scalar":
                        return E.copy(out=dst, in_=src)
                    return E.tensor_copy(out=dst, in_=src)

                P.op(eng, cp, reads=[bank], pwrites=[hnT])
        if keep is not None:
            return hnT
        for h in range(2):
            P.dma("sync", hnT_d[h * 1024:(h + 1) * 1024, :].rearrange("(kc p) t -> p kc t", p=128),
                  hnT.t[:, h * 8:(h + 1) * 8, :], reads=[hnT], dram_write=True)
        P.end_phase()
        P.release_bufs()


class Stager:
    def __init__(self, C, es, shape, dtype, n=3, name="stg"):
        self.bufs = [C.sb(es, shape, dtype, name) for _ in range(n)]
        self.i = 0

    def next(self):
        b = self.bufs[self.i % len(self.bufs)]
        self.i += 1
        return b


def gemm_T(C, actT, kc, ntt, W, blocks, epi, wring, banks):
    P = C.P
    cnt = 0
    for bi, (c0, cw, tag) in enumerate(blocks):
        wb = wring[bi % len(wring)]
        load_w(P, wb, W, c0, cw, kc)
        for tt in range(ntt):
            bank = banks[cnt % len(banks)]
            cnt += 1
            pairs = [(actT.t[:, k, tt * 128:(tt + 1) * 128], wb.t[:, k, 0:cw]) for k in range(kc)]
            mm_group(P, bank, bank.t[:, 0:cw], pairs, reads=[actT, wb])
            epi(tag, c0, cw, tt, bank)


def gemm_F(C, actT, kc, T, W, blocks, epi, wring, banks):
    P = C.P
    cnt = 0
    for bi, (c0, cw, tag) in enumerate(blocks):
        wb = wring[bi % len(wring)]
        load_w(P, wb, W, c0, cw, kc)
        for ch in range(cw // 128):
            for tb in range(T // 512):
                bank = banks[cnt % len(banks)]
                cnt += 1
                pairs = [(wb.t[:, k, ch * 128:(ch + 1) * 128], actT.t[:, k, tb * 512:(tb + 1) * 512])
                         for k in range(kc)]
                mm_group(P, bank, bank.t[:, 0:512], pairs, reads=[actT, wb])
                epi(tag, c0 + ch * 128, tb, bank)


def act_epi(P, bank, src, dst_buf, dst, func, eng="scalar"):
    if eng == "scalar":
        P.op("scalar", lambda E: E.activation(out=dst, in_=src, func=func), reads=[bank], writes=[dst_buf])
    else:
        P.op("vector", lambda E: E.tensor_copy(out=dst, in_=src), reads=[bank], writes=[dst_buf])


GELU_C = 0.044715
GELU_S = 2.0 * math.sqrt(2.0 / math.pi)


def gelu_epi(P, bank, src, tmp, tmp2, dst_buf, dst, n):
    P.op("scalar", lambda E: E.activation(out=tmp.t[:, 0:n], in_=src, func=AF.Square), reads=[bank], writes=[tmp])
    P.op("vector", lambda E: E.tensor_scalar(out=tmp.t[:, 0:n], in0=tmp.t[:, 0:n], scalar1=GELU_C, scalar2=1.0,
                                             op0=ALU.mult, op1=ALU.add), reads=[tmp], writes=[tmp])
    P.op("vector", lambda E: E.tensor_tensor(out=tmp.t[:, 0:n], in0=tmp.t[:, 0:n], in1=src, op=ALU.mult),
         reads=[tmp, bank], writes=[tmp])
    P.op("scalar", lambda E: E.activation(out=tmp2.t[:, 0:n], in_=tmp.t[:, 0:n], func=AF.Sigmoid, scale=GELU_S),
         reads=[tmp], writes=[tmp2])
    P.op("vector", lambda E: E.tensor_tensor(out=dst, in0=tmp2.t[:, 0:n], in1=src, op=ALU.mult),
         reads=[tmp2, bank], writes=[dst_buf])


def load_hnT(C, es, src):
    hnT = C.sb(es, [128, KC, S], BF16, "hnTall")
    if not isinstance(src, list):
        src = [(src[j, h * 1024:(h + 1) * 1024, :], h * 8, 8, j * TL, TL) for j in range(2) for h in range(2)]
    for ap, kc0, nk, t0, nt in src:
        C.P.dma("sync", hnT.t[:, kc0:kc0 + nk, t0:t0 + nt], ap.rearrange("(kc p) t -> p kc t", p=128), pwrites=[hnT])
    return hnT


def phase_out(C, yT_d, W_d, res_d, h_d, ysrc=None, dyn_eng="sync"):
    P = C.P
    with ExitStack() as es:
        yT = C.sb(es, [128, 32, TL], BF16, "yT")
        if ysrc is None:
            for q in range(4):
                P.dma("sync", yT.t[:, q * 8:(q + 1) * 8, :],
                      yT_d[q * 1024:(q + 1) * 1024, :].rearrange("(kc p) t -> p kc t", p=128), pwrites=[yT])
        else:
            for s_ in range(2):
                P.dma(dyn_eng, yT.t[:, s_ * 16:(s_ + 1) * 16, :], ysrc(s_), pwrites=[yT])
        wring = [C.sb(es, [128, 32, 512], BF16, "wo") for _ in range(2)]
        rring = Stager(C, es, [128, 512], F32, 3, "res")
        oring = Stager(C, es, [128, 512], F32, 3, "ho")

        def epi(tag, c0, cw, tt, bank):
            r = rring.next()
            o = oring.next()
            P.dma("sync", r.t[:], res_d[tt * 128:(tt + 1) * 128, c0:c0 + cw], writes=[r])
            P.op("vector", lambda E: E.tensor_tensor(out=o.t[:], in0=bank.t[:, 0:cw], in1=r.t[:], op=ALU.add),
                 reads=[bank, r], writes=[o])
            P.dma("sync", h_d[tt * 128:(tt + 1) * 128, c0:c0 + cw], o.t[:], reads=[o], dram_write=True)

        gemm_T(C, yT, 32, TL // 128, W_d, [(c * 512, 512, None) for c in range(4)], epi, wring, C.banks[0:4])
        P.end_phase()
        P.release_bufs()


def phase_finalnorm(C, x_d, nw_d, out_d, NT=8):
    P = C.P
    with ExitStack() as es:
        nwbc = C.sb(es, [128, D], F32, "nwbc")
        P.dma("sync", nwbc.t[:], nw_d.partition_broadcast(128), writes=[nwbc])
        xr = [C.sb(es, [128, D], F32, "x") for _ in range(2)]
        junk = C.sb(es, [128, D], BF16, "junk")
        ssr = [C.sb(es, [128, 1], F32, "ss") for _ in range(2)]
        rsr = [C.sb(es, [128, 1], F32, "rs") for _ in range(2)]
        orr = [C.sb(es, [128, D], F32, "o") for _ in range(2)]
        for i in range(NT):
            x, ss, rs, o = xr[i % 2], ssr[i % 2], rsr[i % 2], orr[i % 2]
            P.dma("sync", x.t[:], x_d[i * 128:(i + 1) * 128, :], writes=[x])
            P.op("scalar", lambda E, x=x, ss=ss: E.activation(out=junk.t[:], in_=x.t[:], func=AF.Square,
                                                              accum_out=ss.t[:]),
                 reads=[x], writes=[junk, ss])
            rstd_from_ss(P, ss, rs, D)
            P.op("vector", lambda E, x=x, rs=rs, o=o: E.scalar_tensor_tensor(
                out=o.t[:], in0=x.t[:], scalar=rs.t[:, 0:1], in1=nwbc.t[:], op0=ALU.mult, op1=ALU.mult),
                reads=[x, rs, nwbc], writes=[o])
            P.dma("sync", out_d[i * 128:(i + 1) * 128, :], o.t[:], reads=[o], dram_write=True)
        P.end_phase()
        P.release_bufs()


NH1 = 8
MIX1_ATT = True
YT_FLAT = False
LAMBDA_INIT = 0.8 - 0.6 * math.exp(-0.3 * 1)


def phase_mix1(C, hnT_all_d, W1_d, lam_d, subw_d, yT_d, scr, cst, ygather=None):
    P = C.P
    nc = C.nc
    qkT_d, v_d, sg_d = scr["qkT"], scr["v"], scr["sg"]
    with ExitStack() as es:
        hnT = load_hnT(C, es, hnT_all_d)
        wring = [C.sb(es, [128, KC, 512], BF16, "w1") for _ in range(2)]
        stq = Stager(C, es, [128, 512], BF16, 3, "stq")
        stv = Stager(C, es, [128, 256], BF16, 3, "stv")
        stg = Stager(C, es, [128, 256], F32, 3, "stg")
        cntq = [0]

        def epiF(tag, c0, tb, bank):
            hh = tag
            cc = (c0 - hh * 1024) // 128
            st = stq.next()
            eng = "scalar" if cntq[0] % 2 == 0 else "vector"
            cntq[0] += 1
            act_epi(P, bank, bank.t[:, 0:512], st, st.t[:], AF.Copy, eng)
            P.dma("sync", qkT_d[hh * 4 + cc, :, tb * 512:(tb + 1) * 512], st.t[:], reads=[st], dram_write=True)

        def epiT(tag, c0, cw, tt, bank):
            hh = tag
            sv = stv.next()
            sg = stg.next()
            P.op("vector", lambda E: E.tensor_copy(out=sv.t[:], in_=bank.t[:, 0:256]), reads=[bank], writes=[sv])
            P.op("scalar", lambda E: E.activation(out=sg.t[:], in_=bank.t[:, 256:512], func=AF.Silu),
                 reads=[bank], writes=[sg])
            P.dma("sync", v_d[tt * 128:(tt + 1) * 128, hh, :], sv.t[:], reads=[sv], dram_write=True)
            P.dma("sync", sg_d[tt * 128:(tt + 1) * 128, hh, :], sg.t[:], reads=[sg], dram_write=True)

        for hh in range(NH1):
            gemm_F(C, hnT, KC, S, W1_d, [(hh * 1024, 512, hh)], epiF, [wring[0]], C.banks[0:4])
            gemm_T(C, hnT, KC, S // 128, W1_d, [(hh * 1024 + 512, 512, hh)], epiT, [wring[1]], C.banks[4:8])
        P.end_phase()
        P.release_bufs()

    if not MIX1_ATT:
        return
    with ExitStack() as es:
        ident, Ubf = cst["ident"], cst["Ubf"]
        scale = 128.0 ** -0.5
        lamv = C.sb(es, [128, 4, 128], F32, "lamv")
        P.dma("sync", lamv.t[:], lam_d.rearrange("a d -> (a d)").partition_broadcast(128)
              .rearrange("p (a d) -> p a d", a=4), writes=[lamv])
        lprod = C.sb(es, [128, 2, 128], F32, "lprod")
        lsum = C.sb(es, [128, 2], F32, "lsum")
        lam = C.sb(es, [128, 1], F32, "lam")
        P.op("vector", lambda E: E.tensor_tensor(out=lprod.t[:, 0, :], in0=lamv.t[:, 0, :], in1=lamv.t[:, 1, :],
                                                 op=ALU.mult), reads=[lamv], writes=[lprod])
        P.op("vector", lambda E: E.tensor_tensor(out=lprod.t[:, 1, :], in0=lamv.t[:, 2, :], in1=lamv.t[:, 3, :],
                                                 op=ALU.mult), reads=[lamv, lprod], writes=[lprod])
        P.op("vector", lambda E: E.tensor_reduce(out=lsum.t[:], in_=lprod.t[:], axis=AX.X, op=ALU.add),
             reads=[lprod], writes=[lsum])
        P.op("scalar", lambda E: E.activation(out=lsum.t[:], in_=lsum.t[:], func=AF.Exp), reads=[lsum], writes=[lsum])
        P.op("vector", lambda E: E.tensor_tensor(out=lam.t[:], in0=lsum.t[:, 0:1], in1=lsum.t[:, 1:2],
                                                 op=ALU.subtract), reads=[lsum], writes=[lam])
        P.op("vector", lambda E: E.tensor_scalar(out=lam.t[:], in0=lam.t[:], scalar1=LAMBDA_INIT, scalar2=None,
                                                 op0=ALU.add), reads=[lam], writes=[lam])
        subw = C.sb(es, [128, 256], F32, "subw")
        P.dma("sync", subw.t[:], subw_d.partition_broadcast(128), writes=[subw])
        P.op("vector", lambda E: E.tensor_scalar(out=subw.t[:], in0=subw.t[:], scalar1=1.0 - LAMBDA_INIT,
                                                 scalar2=None, op0=ALU.mult), reads=[subw], writes=[subw])

        qkr = [C.sb(es, [128, 4, S], BF16, "qk") for _ in range(2)]
        vr = [C.sb(es, [128, 16, 258], BF16, "vaug") for _ in range(2)]
        sgr = [C.sb(es, [128, 16, 256], F32, "sg") for _ in range(2)]
        for v in vr:
            P.op("vector", lambda E, v=v: E.memset(v.t[:, :, 256:258], 1.0), writes=[v])
        PTS = [[[C.sb(es, [128, 512], BF16, "PT") for _ in range(16)] for _ in range(2)] for _ in range(2)]
        yTr = [C.sb(es, [128, 2, S], BF16, "yTh") for _ in range(2)]
        o2r = Stager(C, es, [128, 256], F32, 2, "o2")
        orr = Stager(C, es, [128, 256], F32, 2, "o")
        junk = C.sb(es, [128, 256], BF16, "junk")
        smr = Stager(C, es, [128, 4], F32, 4, "small")
        wgr = Stager(C, es, [128, 256], F32, 2, "wg")
        ybr = Stager(C, es, [128, 256], BF16, 3, "yb")
        sbanks = [C.banks[0], C.banks[1], C.banks[6]]
        abanks = C.banks[2:6]
        tbanks = [C.banks[7]]
        ns = [0]
        na = [0]
        nt = [0]
        units = [(hh, R) for hh in range(NH1) for R in range(4)]
        deferred = []

        def flush():
            while deferred:
                deferred.pop(0)()

        def st_tiles(u):
            hh, R = units[u]
            qk, va, sg = qkr[hh % 2], vr[hh % 2], sgr[hh % 2]
            PT = PTS[u % 2]
            out = []
            if R == 0:
                def loads():
                    P.dma("sync", qk.t[:], qkT_d[hh * 4:(hh + 1) * 4, :, :].rearrange("c p t -> p c t"), writes=[qk])
                    P.dma("sync", va.t[:, :, 0:256], v_d[:, hh, :].rearrange("(tt p) d -> p tt d", p=128), pwrites=[va])
                    P.dma("sync", sg.t[:], sg_d[:, hh, :].rearrange("(tt p) d -> p tt d", p=128), writes=[sg])
                out.append(loads)
            nk = 4 * R + 4
            for j in range(2):
                for kc in range(nk):
                    def tile(j=j, kc=kc):
                        off = max(0, kc - 4 * R) * 128
                        bank = sbanks[ns[0] % 3]
                        ns[0] += 1
                        pt = PT[j][kc]
                        mm_group(P, bank, bank.t[:, off:512],
                                 [(qk.t[:, 2 + j, kc * 128:(kc + 1) * 128],
                                   qk.t[:, j, R * 512 + off:(R + 1) * 512])], reads=[qk])
                        P.op("scalar", lambda E: E.activation(out=pt.t[:, off:512], in_=bank.t[:, off:512],
                                                              func=AF.Exp, scale=scale), reads=[bank], writes=[pt])
                        if kc >= 4 * R:
                            P.op("gpsimd", lambda E: E.tensor_tensor(
                                out=pt.t[:, off:off + 128], in0=pt.t[:, off:off + 128], in1=Ubf.t[:], op=ALU.mult),
                                reads=[pt, Ubf], writes=[pt])
                    out.append(tile)
            return out

        def emit_PV(u, nxt):
            hh, R = units[u]
            va, sg, yTh = vr[hh % 2], sgr[hh % 2], yTr[hh % 2]
            PT = PTS[u % 2]
            per = (len(nxt) + 7) // 8
            for qs in range(4):
                qb = 4 * R + qs
                accs = []
                for j in range(2):
                    bank = abanks[na[0] % 4]
                    na[0] += 1
                    accs.append(bank)
                    pairs = [(PT[j][kc].t[:, qs * 128:(qs + 1) * 128], va.t[:, kc, 0:257]) for kc in range(qb + 1)]
                    mm_group(P, bank, bank.t[:, 0:257], pairs, reads=[va] + [PT[j][kc] for kc in range(qb + 1)])
                    if j == 1:
                        flush()
                    for _ in range(per):
                        if nxt:
                            nxt.pop(0)()
                a1, a2 = accs
                sm, o2, o, wg, yb = smr.next(), o2r.next(), orr.next(), wgr.next(), ybr.next()
                P.op("vector", lambda E, sm=sm, a1=a1: E.reciprocal(out=sm.t[:, 0:1], in_=a1.t[:, 256:257]),
                     reads=[a1], writes=[sm])
                P.op("vector", lambda E, sm=sm, a2=a2: E.reciprocal(out=sm.t[:, 1:2], in_=a2.t[:, 256:257]),
                     reads=[a2, sm], writes=[sm])
                P.op("vector", lambda E, sm=sm: E.tensor_tensor(out=sm.t[:, 1:2], in0=sm.t[:, 1:2], in1=lam.t[:],
                                                                op=ALU.mult), reads=[sm, lam], writes=[sm])
                P.op("vector", lambda E, sm=sm, a2=a2, o2=o2: E.tensor_scalar(
                    out=o2.t[:], in0=a2.t[:, 0:256], scalar1=sm.t[:, 1:2], scalar2=None, op0=ALU.mult),
                    reads=[a2, sm], writes=[o2])
                P.op("vector", lambda E, sm=sm, a1=a1, o2=o2, o=o: E.scalar_tensor_tensor(
                    out=o.t[:], in0=a1.t[:, 0:256], scalar=sm.t[:, 0:1], in1=o2.t[:], op0=ALU.mult,
                    op1=ALU.subtract), reads=[a1, sm, o2], writes=[o])
                sm2 = smr.next()
                P.op("vector", lambda E, o=o, sm2=sm2: E.scalar_tensor_tensor(
                    out=junk.t[:], in0=o.t[:], scalar=1.0, in1=o.t[:], op0=ALU.mult, op1=ALU.mult,
                    accum_out=sm2.t[:, 0:1]), reads=[o], writes=[junk, sm2])
                P.op("scalar", lambda E, sm2=sm2: E.activation(out=sm2.t[:, 1:2], in_=sm2.t[:, 0:1], func=AF.Sqrt,
                                                               scale=1.0 / 256, bias=EPS), reads=[sm2], writes=[sm2])
                P.op("vector", lambda E, sm2=sm2: E.reciprocal(out=sm2.t[:, 2:3], in_=sm2.t[:, 1:2]),
                     reads=[sm2], writes=[sm2])
                P.op("gpsimd", lambda E, wg=wg, sg=sg, qb=qb: E.tensor_tensor(
                    out=wg.t[:], in0=sg.t[:, qb, :], in1=subw.t[:], op=ALU.mult), reads=[sg, subw], writes=[wg])
                P.op("vector", lambda E, o=o, sm2=sm2, wg=wg, yb=yb: E.scalar_tensor_tensor(
                    out=yb.t[:], in0=o.t[:], scalar=sm2.t[:, 2:3], in1=wg.t[:], op0=ALU.mult, op1=ALU.mult),
                    reads=[o, sm2, wg], writes=[yb])

                def late(yb=yb, yTh=yTh, qb=qb):
                    tbank = tbanks[0]
                    nt[0] += 1

                    def tr(E):
                        ins = None
                        for c in range(2):
                            ins = E.transpose(out=bfv(tbank)[:, c * 128:(c + 1) * 128],
                                              in_=yb.t[:, c * 128:(c + 1) * 128], identity=ident.t[:])
                        return ins

                    P.op("tensor", tr, reads=[yb, ident], writes=[tbank])
                    P.op("vector", lambda E: E.tensor_copy(
                        out=yTh.t[:, :, qb * 128:(qb + 1) * 128],
                        in_=bfv(tbank)[:, 0:256].rearrange("p (c t) -> p c t", t=128)), reads=[tbank], pwrites=[yTh])

                deferred.append(late)
            if R == 3:
                def store(hh=hh, yTh=yTh):
                    if YT_FLAT:
                        tok = P.dma("sync", yT_d[hh * 256:(hh + 1) * 256, :].rearrange("(c p) t -> p c t", p=128),
                                    yTh.t[:], reads=[yTh], dram_write=True)
                        if ygather is not None:
                            P._wait("gpsimd", tok)
                            ygather(hh)
                    else:
                        for j in range(2):
                            P.dma("sync", yT_d[j, hh * 256:(hh + 1) * 256, :].rearrange("(c p) t -> p c t", p=128),
                                  yTh.t[:, :, j * TL:(j + 1) * TL], reads=[yTh], dram_write=True)

                deferred.append(store)

        for f in st_tiles(0):
            f()
        for u in range(len(units)):
            nxt = st_tiles(u + 1) if u + 1 < len(units) else []
            emit_PV(u, nxt)
            while nxt:
                nxt.pop(0)()
        flush()
        P.end_phase()
        P.release_bufs()


W0_F = 4096
W0_COLS = 4096 + 1024 + 16 + 2048


class nc_allow:
    def __init__(self, C):
        self.C = C

    def __enter__(self):
        return self

    def __exit__(self, *a):
        return False


def bc3(ap2, n):
    return ap2.unsqueeze(2).to_broadcast([ap2.shape[0], ap2.shape[1], n])


def phase_mix0(C, hnT_all_d, W0_d, prm, yT_d, scr, cst, ygather=None):
    P = C.P
    raw_d, zb_d, u_d, za_d, dt_d, vg_d = scr["raw"], scr["zb"], scr["u"], scr["za"], scr["dt"], scr["vg"]
    ident, Ubf, Uf, Lgt, ones = cst["ident"], cst["Ubf"], cst["Uf"], cst["Lgt"], cst["ones"]
    with ExitStack() as es:
        hnT = hnT_all_d(es) if callable(hnT_all_d) else load_hnT(C, es, hnT_all_d)
        wring = [C.sb(es, [128, KC, 512], BF16, "w0") for _ in range(2)]
        st = Stager(C, es, [128, 512], F32, 4, "st")
        tmpr = Stager(C, es, [128, 512], F32, 2, "gt")
        tmp2r = Stager(C, es, [128, 512], F32, 2, "gt2")
        cn = [0]

        def epiF(tag, c0, tb, bank):
            s_ = st.next()
            ch = c0 // 128
            if tag == "raw":
                eng = "scalar" if cn[0] % 2 == 0 else "vector"
                cn[0] += 1
                act_epi(P, bank, bank.t[:, 0:512], s_, s_.t[:], AF.Copy, eng)
                dst = raw_d[ch, :, tb * 512:(tb + 1) * 512]
            elif tag == "zb":
                act_epi(P, bank, bank.t[:, 0:512], s_, s_.t[:], AF.Silu)
                dst = zb_d[ch - 16, :, tb * 512:(tb + 1) * 512]
            else:
                gelu_epi(P, bank, bank.t[:, 0:512], tmpr.next(), tmp2r.next(), s_, s_.t[:], 512)
                dst = u_d[ch - 24, :, tb * 512:(tb + 1) * 512]
            P.dma("sync", dst, s_.t[:], reads=[s_], dram_write=True)

        def epiT(tag, c0, cw, tt, bank):
            s_ = st.next()
            rows = slice(tt * 128, (tt + 1) * 128)
            if tag == "za":
                act_epi(P, bank, bank.t[:, 0:cw], s_, s_.t[:, 0:cw], AF.Silu)
                dst = za_d[rows, c0 - W0_F:c0 - W0_F + cw]
            elif tag == "dt":
                act_epi(P, bank, bank.t[:, 0:cw], s_, s_.t[:, 0:cw], AF.Copy, "vector")
                dst = dt_d[rows, :]
            else:
                gelu_epi(P, bank, bank.t[:, 0:cw], tmpr.next(), tmp2r.next(), s_, s_.t[:, 0:cw], cw)
                v0 = c0 - (W0_F + 1040)
                dst = vg_d[rows, v0:v0 + cw]
            P.dma("sync", dst, s_.t[:, 0:cw], reads=[s_], dram_write=True)

        fblocks = [(i * 512, 512, "raw") for i in range(4)] + [(2048 + i * 512, 512, "zb") for i in range(2)] + \
                  [(3072 + i * 512, 512, "u") for i in range(2)]
        gemm_F(C, hnT, KC, S, W0_d, fblocks, epiF, wring, C.banks[0:4])
        tblocks = [(W0_F, 512, "za"), (W0_F + 512, 512, "za"), (W0_F + 1024, 16, "dt")] + \
                  [(W0_F + 1040 + i * 512, 512, "v") for i in range(4)]
        gemm_T(C, hnT, KC, S // 128, W0_d, tblocks, epiT, wring, C.banks[4:8])
        P.end_phase()
        P.release_bufs()

    with ExitStack() as es:
        def bcast_load(name, src, n):
            b = C.sb(es, [128, n], F32, name)
            P.dma("sync", b.t[:], src.partition_broadcast(128), writes=[b])
            return b

        cw_sb = C.sb(es, [128, 16, 4], F32, "convw")
        P.dma("sync", cw_sb.t[:], prm["conv_w"], writes=[cw_sb])
        cb_sb = C.sb(es, [128, 16], F32, "convb")
        P.dma("sync", cb_sb.t[:], prm["conv_b"], writes=[cb_sb])
        dtb = bcast_load("dtb", prm["dt_bias"], 16)
        a_bc = bcast_load("a_bc", prm["a_log"], 16)
        dsk = bcast_load("dsk", prm["d_skip"], 16)
        nrmw = bcast_load("nrmw", prm["ssd_norm_w"], 1024)
        P.op("scalar", lambda E: E.activation(out=a_bc.t[:], in_=a_bc.t[:], func=AF.Exp), reads=[a_bc], writes=[a_bc])
        P.op("vector", lambda E: E.tensor_scalar(out=a_bc.t[:], in0=a_bc.t[:], scalar1=-1.0, scalar2=None,
                                                 op0=ALU.mult), reads=[a_bc], writes=[a_bc])
        xbcT = C.sb(es, [128, 16, S], BF16, "xbcT")
        rawr = [C.sb(es, [128, 3 + S], F32, "raw") for _ in range(2)]
        accr = [C.sb(es, [128, S], F32, "cacc") for _ in range(2)]
        for r_ in rawr:
            P.op("vector", lambda E, r_=r_: E.memset(r_.t[:, 0:3], 0.0), writes=[r_])
        for c in range(16):
            rw, acc = rawr[c % 2], accr[c % 2]
            P.dma("sync", rw.t[:, 3:3 + S], raw_d[c, :, :], pwrites=[rw])
            P.op("vector", lambda E, rw=rw, acc=acc, c=c: E.tensor_scalar(
                out=acc.t[:], in0=rw.t[:, 3:3 + S], scalar1=cw_sb.t[:, c, 3:4], scalar2=None, op0=ALU.mult),
                reads=[rw, cw_sb], writes=[acc])
            for k in range(3):
                P.op("vector", lambda E, rw=rw, acc=acc, c=c, k=k: E.scalar_tensor_tensor(
                    out=acc.t[:], in0=rw.t[:, k:k + S], scalar=cw_sb.t[:, c, k:k + 1], in1=acc.t[:],
                    op0=ALU.mult, op1=ALU.add), reads=[rw, cw_sb, acc], writes=[acc])
            P.op("scalar", lambda E, acc=acc, c=c: E.activation(out=xbcT.t[:, c, :], in_=acc.t[:], func=AF.Silu,
                                                                bias=cb_sb.t[:, c:c + 1]),
                 reads=[acc, cb_sb], pwrites=[xbcT])

        prev32 = C.sb(es, [128, 1024], F32, "prev32")
        prevbf = C.sb(es, [128, 1024], BF16, "prevbf")
        P.op("vector", lambda E: E.memset(prev32.t[:], 0.0), writes=[prev32])
        P.op("vector", lambda E: E.memset(prevbf.t[:], 0.0), writes=[prevbf])
        R2 = lambda shape, dt, nm: Stager(C, es, shape, dt, 2, nm)
        zar, dtr_, smr = R2([128, 1024], F32, "za"), R2([128, 16], F32, "dtraw"), R2([128, 8, 16], F32, "ssm")
        xsr, xdr, xder, btr = R2([128, 1024], BF16, "xs"), R2([128, 1024], BF16, "xd"), R2([128, 1024], BF16, "xde"), \
            R2([128, 512], BF16, "btm")
        cbr = R2([128, 4, 128], F32, "cbm")
        Ar, Er, MTr = Stager(C, es, [128, 4, 128], F32, 3, "A"), Stager(C, es, [128, 512], F32, 3, "E"), \
            Stager(C, es, [128, 4, 128], BF16, 4, "MT")
        ssd_deferred = []
        t1r, t3r, ynr = R2([128, 1024], F32, "t1"), R2([128, 1024], F32, "t3"), R2([128, 1024], BF16, "yn")
        junk = C.sb(es, [128, 256], BF16, "junk")
        ssr = R2([128, 8], F32, "gss")
        ystr = R2([128, 8, 128], BF16, "yst")
        bX, bT, bS0, bS1, bD0, bD1, bO0, bO1 = C.banks
        allsc = C.sb(es, [128, 8, 256], F32, "allsc")
        A_ = [allsc.t[:, i, :] for i in range(8)]
        A3 = [allsc.t[:, i, :].rearrange("p (n h) -> p n h", h=16) for i in range(8)]
        with nc_allow(C):
            P.dma("sync", A3[7], dt_d.rearrange("(n p) h -> p n h", p=128), writes=[allsc])
        P.op("vector", lambda E: E.tensor_tensor(out=A3[7], in0=A3[7], in1=dtb.t[:].unsqueeze(1).to_broadcast([128, 16, 16]),
                                                 op=ALU.add), reads=[allsc, dtb], writes=[allsc])
        P.op("scalar", lambda E: E.activation(out=A_[7], in_=A_[7], func=AF.Exp), reads=[allsc], writes=[allsc])
        P.op("scalar", lambda E: E.activation(out=A_[0], in_=A_[7], func=AF.Ln, bias=1.0), reads=[allsc], writes=[allsc])
        P.op("vector", lambda E: E.tensor_tensor(out=A3[1], in0=A3[0], in1=a_bc.t[:].unsqueeze(1).to_broadcast([128, 16, 16]),
                                                 op=ALU.mult), reads=[allsc, a_bc], writes=[allsc])

        def csmm(E):
            E.matmul(bX.t[:, 0:256], Uf.t[:], A_[1], start=True, stop=True)
            return E.matmul(bT.t[:, 0:256], ones.t[:], A_[1], start=True, stop=True)

        P.op("tensor", csmm, reads=[allsc, Uf, ones], writes=[bX, bT])
        P.op("scalar", lambda E: E.copy(out=A_[2], in_=bX.t[:, 0:256]), reads=[bX], writes=[allsc])
        P.op("scalar", lambda E: E.activation(out=A_[3], in_=bX.t[:, 0:256], func=AF.Exp), reads=[bX], writes=[allsc])
        P.op("scalar", lambda E: E.activation(out=A_[5], in_=bT.t[:, 0:256], func=AF.Exp), reads=[bT], writes=[allsc])
        P.op("vector", lambda E: E.tensor_tensor(out=A_[7], in0=bT.t[:, 0:256], in1=A_[2], op=ALU.subtract),
             reads=[bT, allsc], writes=[allsc])
        P.op("scalar", lambda E: E.activation(out=A_[4], in_=A_[7], func=AF.Exp), reads=[allsc], writes=[allsc])
        P.op("vector", lambda E: E.tensor_tensor(out=A_[6], in0=A_[0], in1=A_[4], op=ALU.mult), reads=[allsc], writes=[allsc])
        for n in range(16):
            tok = slice(n * 128, (n + 1) * 128)
            za = zar.next()
            P.dma("sync", za.t[:], za_d[tok, :], writes=[za])
            sm = allsc
            hs = slice(n * 16, (n + 1) * 16)
            DT, DA, ECS, CD, DTD = [allsc.t[:, i, hs] for i in (0, 1, 3, 5, 6)]
            xs, xd, xde, btm = xsr.next(), xdr.next(), xder.next(), btr.next()

            def trx(E, tok=tok):
                ins = None
                for c in range(8):
                    ins = E.transpose(out=bfv(bT)[:, c * 128:(c + 1) * 128], in_=xbcT.t[:, c, tok], identity=ident.t[:])
                return ins

            P.op("tensor", trx, reads=[xbcT, ident], writes=[bT])
            P.op("scalar", lambda E, xs=xs: E.copy(out=xs.t[:], in_=bfv(bT)[:, 0:1024]), reads=[bT], writes=[xs])
            P.op("vector", lambda E, xs=xs, xd=xd, DT=DT: E.tensor_tensor(
                out=xd.t[:].rearrange("p (h d) -> p h d", d=64), in0=xs.t[:].rearrange("p (h d) -> p h d", d=64),
                in1=bc3(DT, 64), op=ALU.mult), reads=[xs, sm], writes=[xd])
            P.op("gpsimd", lambda E, xs=xs, xde=xde, DTD=DTD: E.tensor_tensor(
                out=xde.t[:].rearrange("p (h d) -> p h d", d=64), in0=xs.t[:].rearrange("p (h d) -> p h d", d=64),
                in1=bc3(DTD, 64), op=ALU.mult), reads=[xs, sm], writes=[xde])

            def trb(E, tok=tok):
                ins = None
                for g in range(4):
                    ins = E.transpose(out=bfv(bT)[:, g * 128:(g + 1) * 128], in_=xbcT.t[:, 8 + g, tok],
                                      identity=ident.t[:])
                return ins

            P.op("tensor", trb, reads=[xbcT, ident], writes=[bT])
            P.op("scalar", lambda E, btm=btm: E.copy(out=btm.t[:], in_=bfv(bT)[:, 0:512]), reads=[bT], writes=[btm])
            cbm = cbr.next()

            def cbmm(E, tok=tok):
                ins = None
                for g in range(4):
                    ins = E.matmul(bX.t[:, g * 128:(g + 1) * 128], xbcT.t[:, 8 + g, tok], xbcT.t[:, 12 + g, tok],
                                   start=True, stop=True)
                return ins

            P.op("tensor", cbmm, reads=[xbcT], writes=[bX])
            P.op("vector", lambda E, cbm=cbm: E.tensor_tensor(
                out=cbm.t[:], in0=bX.t[:, 0:512].rearrange("p (g l) -> p g l", l=128),
                in1=Uf.t[:].unsqueeze(1).to_broadcast([128, 4, 128]), op=ALU.mult), reads=[bX, Uf], writes=[cbm])
            while ssd_deferred:
                ssd_deferred.pop(0)()
            grp = []
            for g in range(4):
                grp.append((Ar.next(), Er.next(), MTr.next(), bS0 if g % 2 == 0 else bS1, bD0 if g < 2 else bD1))

            def front(g, grp=grp, DA=DA, sm=sm):
                A, Eb, MT, bS, bD = grp[g]
                P.op("gpsimd", lambda E: E.tensor_tensor(
                    out=A.t[:], in0=Lgt.t[:].unsqueeze(1).to_broadcast([128, 4, 128]),
                    in1=bc3(DA[:, 4 * g:4 * g + 4], 128), op=ALU.mult), reads=[Lgt, sm], writes=[A])

                def segmm(E):
                    ins = None
                    for j in range(4):
                        ins = E.matmul(bS.t[:, j * 128:(j + 1) * 128], A.t[:, j, :], Uf.t[:], start=True, stop=True)
                    return ins

                P.op("tensor", segmm, reads=[A, Uf], writes=[bS])
                P.op("scalar", lambda E: E.activation(out=Eb.t[:], in_=bS.t[:, 0:512], func=AF.Exp),
                     reads=[bS], writes=[Eb])

            def back(g, grp=grp, cbm=cbm, xd=xd):
                A, Eb, MT, bS, bD = grp[g]
                P.op("vector", lambda E: E.tensor_tensor(
                    out=MT.t[:], in0=Eb.t[:].rearrange("p (j l) -> p j l", l=128),
                    in1=cbm.t[:, g, :].unsqueeze(1).to_broadcast([128, 4, 128]), op=ALU.mult),
                    reads=[Eb, cbm], writes=[MT])

                def ydmm(E):
                    ins = None
                    for j in range(4):
                        h = 4 * g + j
                        col = (h % 8) * 64
                        ins = E.matmul(bD.t[:, col:col + 64], MT.t[:, j, :], xd.t[:, h * 64:(h + 1) * 64],
                                       start=True, stop=True)
                    return ins

                P.op("tensor", ydmm, reads=[MT, xd], writes=[] if g % 2 == 1 else [bD], pwrites=[bD] if g % 2 == 1 else [])

            front(0)
            front(1)
            back(0)
            front(2)
            back(1)
            front(3)
            back(2)
            back(3)

            def yomm(E, tok=tok):
                ins = None
                for g in range(4):
                    bO = bO0 if g < 2 else bO1
                    col = (g % 2) * 256
                    ins = E.matmul(bO.t[:, col:col + 256], xbcT.t[:, 12 + g, tok], prevbf.t[:, g * 256:(g + 1) * 256],
                                   start=True, stop=True)
                return ins

            P.op("tensor", yomm, reads=[xbcT, prevbf], writes=[bO0, bO1])

            def stmm(E, btm=btm, xde=xde):
                ins = None
                for g in range(4):
                    bS = bS0 if g < 2 else bS1
                    col = (g % 2) * 256
                    ins = E.matmul(bS.t[:, col:col + 256], btm.t[:, g * 128:(g + 1) * 128],
                                   xde.t[:, g * 256:(g + 1) * 256], start=True, stop=True)
                return ins

            P.op("tensor", stmm, reads=[btm, xde], writes=[bS0, bS1])
            P.op("vector", lambda E, CD=CD: E.tensor_tensor(
                out=prev32.t[:].rearrange("p (h d) -> p h d", d=64), in0=prev32.t[:].rearrange("p (h d) -> p h d", d=64),
                in1=bc3(CD, 64), op=ALU.mult), reads=[prev32, sm], writes=[prev32])
            for hb, bS in enumerate([bS0, bS1]):
                sl = slice(hb * 512, (hb + 1) * 512)
                P.op("vector", lambda E, bS=bS, sl=sl: E.tensor_tensor(out=prev32.t[:, sl], in0=prev32.t[:, sl],
                                                                       in1=bS.t[:, 0:512], op=ALU.add),
                     reads=[bS, prev32], writes=[prev32])
            P.op("scalar", lambda E: E.copy(out=prevbf.t[:], in_=prev32.t[:]), reads=[prev32], writes=[prevbf])
            t1, t3, yn, gss = t1r.next(), t3r.next(), ynr.next(), ssr.next()
            P.op("gpsimd", lambda E, t3=t3, xs=xs: E.tensor_tensor(
                out=t3.t[:].rearrange("p (h d) -> p h d", d=64), in0=xs.t[:].rearrange("p (h d) -> p h d", d=64),
                in1=bc3(dsk.t[:], 64), op=ALU.mult), reads=[xs, dsk], writes=[t3])
            for hb, (bO, bD) in enumerate([(bO0, bD0), (bO1, bD1)]):
                sl = slice(hb * 512, (hb + 1) * 512)
                P.op("vector", lambda E, t1=t1, bO=bO, ECS=ECS, hb=hb, sl=sl: E.tensor_tensor(
                    out=t1.t[:, sl].rearrange("p (h d) -> p h d", d=64),
                    in0=bO.t[:, 0:512].rearrange("p (h d) -> p h d", d=64),
                    in1=bc3(ECS[:, hb * 8:(hb + 1) * 8], 64), op=ALU.mult), reads=[bO, sm], pwrites=[t1])
                P.op("vector", lambda E, t1=t1, bD=bD, sl=sl: E.tensor_tensor(
                    out=t1.t[:, sl], in0=t1.t[:, sl], in1=bD.t[:, 0:512], op=ALU.add), reads=[bD, t1], writes=[t1])
            P.op("vector", lambda E, t1=t1, t3=t3: E.tensor_tensor(out=t1.t[:], in0=t1.t[:], in1=t3.t[:], op=ALU.add),
                 reads=[t1, t3], writes=[t1])
            P.op("vector", lambda E, t1=t1, za=za: E.tensor_tensor(out=t1.t[:], in0=t1.t[:], in1=za.t[:], op=ALU.mult),
                 reads=[t1, za], writes=[t1])
            for gi in range(4):
                P.op("scalar", lambda E, t1=t1, gss=gss, gi=gi: E.activation(
                    out=junk.t[:], in_=t1.t[:, gi * 256:(gi + 1) * 256], func=AF.Square,
                    accum_out=gss.t[:, gi:gi + 1]), reads=[t1], writes=[junk, gss])
            P.op("scalar", lambda E, gss=gss: E.activation(out=gss.t[:, 4:8], in_=gss.t[:, 0:4], func=AF.Sqrt,
                                                           scale=1.0 / 256, bias=EPS), reads=[gss], writes=[gss])
            P.op("vector", lambda E, gss=gss: E.reciprocal(out=gss.t[:, 4:8], in_=gss.t[:, 4:8]),
                 reads=[gss], writes=[gss])
            P.op("vector", lambda E, t1=t1, gss=gss: E.tensor_tensor(
                out=t1.t[:].rearrange("p (g d) -> p g d", d=256), in0=t1.t[:].rearrange("p (g d) -> p g d", d=256),
                in1=bc3(gss.t[:, 4:8], 256), op=ALU.mult), reads=[t1, gss], writes=[t1])
            P.op("gpsimd", lambda E, t1=t1, yn=yn: E.tensor_tensor(out=yn.t[:], in0=t1.t[:], in1=nrmw.t[:], op=ALU.mult),
                 reads=[t1, nrmw], writes=[yn])
            def late(yn=yn, n=n):
                yst = ystr.next()

                def try_(E, yn=yn):
                    ins = None
                    for c in range(8):
                        ins = E.transpose(out=bfv(bT)[:, c * 128:(c + 1) * 128], in_=yn.t[:, c * 128:(c + 1) * 128],
                                          identity=ident.t[:])
                    return ins

                P.op("tensor", try_, reads=[yn, ident], writes=[bT])
                P.op("scalar", lambda E, yst=yst: E.copy(out=yst.t[:], in_=bfv(bT)[:, 0:1024].rearrange("p (c t) -> p c t", t=128)),
                     reads=[bT], writes=[yst])
                if YT_FLAT:
                    P.dma("sync", yT_d[0:1024, n * 128:(n + 1) * 128].rearrange("(c p) t -> p c t", p=128), yst.t[:],
                          reads=[yst], dram_write=True)
                else:
                    j, off = n // 8, (n % 8) * 128
                    P.dma("sync", yT_d[j, 0:1024, off:off + 128].rearrange("(c p) t -> p c t", p=128), yst.t[:],
                          reads=[yst], dram_write=True)

            ssd_deferred.append(late)
        while ssd_deferred:
            ssd_deferred.pop(0)()
        P.end_phase()
        P.release_bufs()
    if ygather is not None:
        for k in range(4):
            ygather(k)

    with ExitStack() as es:
        lnw = C.sb(es, [128, 1024], F32, "lnw")
        lnb = C.sb(es, [128, 1024], F32, "lnb")
        P.dma("sync", lnw.t[:], prm["sgu_ln_w"].partition_broadcast(128), writes=[lnw])
        P.dma("sync", lnb.t[:], prm["sgu_ln_b"].partition_broadcast(128), writes=[lnb])
        sb_bc = C.sb(es, [128, 8, 128], F32, "sgub")
        P.dma("sync", sb_bc.t[:], prm["sgu_b"].rearrange("g t -> (g t)").partition_broadcast(128)
              .rearrange("p (g t) -> p g t", g=8), writes=[sb_bc])
        wsf = C.sb(es, [128, 8, 128], F32, "wsf")
        P.dma("sync", wsf.t[:], prm["wsT"].rearrange("g s t -> s g t"), writes=[wsf])
        wsm = C.sb(es, [128, 8, 128], BF16, "wsm")
        P.op("vector", lambda E: E.tensor_tensor(out=wsm.t[:], in0=wsf.t[:],
                                                 in1=Uf.t[:].unsqueeze(1).to_broadcast([128, 8, 128]), op=ALU.mult),
             reads=[wsf, Uf], writes=[wsm])
        vn = C.sb(es, [128, 16, 1024], BF16, "vn")
        vgr = [C.sb(es, [128, 2048], F32, "vg") for _ in range(2)]
        junk = C.sb(es, [128, 2048], BF16, "junkv")
        vtr = [C.sb(es, [128, 1024], F32, "vt") for _ in range(2)]
        str_ = Stager(C, es, [128, 8], F32, 2, "lnst")
        for tt in range(16):
            vg, vt, s_ = vgr[tt % 2], vtr[tt % 2], str_.next()
            P.dma("sync", vg.t[:], vg_d[tt * 128:(tt + 1) * 128, :], writes=[vg])
            P.op("vector", lambda E, vg=vg, s_=s_: E.tensor_reduce(out=s_.t[:, 0:1], in_=vg.t[:], axis=AX.X, op=ALU.add),
                 reads=[vg], writes=[s_])
            P.op("scalar", lambda E, vg=vg, s_=s_: E.activation(out=junk.t[:], in_=vg.t[:], func=AF.Square,
                                                                accum_out=s_.t[:, 1:2]), reads=[vg, s_], writes=[junk, s_])
            P.op("vector", lambda E, s_=s_: E.tensor_scalar(out=s_.t[:, 2:3], in0=s_.t[:, 0:1], scalar1=1.0 / 2048,
                                                            scalar2=None, op0=ALU.mult), reads=[s_], writes=[s_])
            P.op("vector", lambda E, s_=s_: E.tensor_tensor(out=s_.t[:, 3:4], in0=s_.t[:, 2:3], in1=s_.t[:, 2:3],
                                                            op=ALU.mult), reads=[s_], writes=[s_])
            P.op("vector", lambda E, s_=s_: E.scalar_tensor_tensor(out=s_.t[:, 4:5], in0=s_.t[:, 1:2], scalar=1.0 / 2048,
                                                                   in1=s_.t[:, 3:4], op0=ALU.mult, op1=ALU.subtract),
                 reads=[s_], writes=[s_])
            P.op("scalar", lambda E, s_=s_: E.activation(out=s_.t[:, 5:6], in_=s_.t[:, 4:5], func=AF.Sqrt, bias=EPS),
                 reads=[s_], writes=[s_])
            P.op("vector", lambda E, s_=s_: E.reciprocal(out=s_.t[:, 5:6], in_=s_.t[:, 5:6]), reads=[s_], writes=[s_])
            P.op("vector", lambda E, s_=s_: E.scalar_tensor_tensor(out=s_.t[:, 6:7], in0=s_.t[:, 2:3], scalar=-1.0,
                                                                   in1=s_.t[:, 5:6], op0=ALU.mult, op1=ALU.mult),
                 reads=[s_], writes=[s_])
            P.op("scalar", lambda E, vg=vg, vt=vt, s_=s_: E.activation(out=vt.t[:], in_=vg.t[:, 0:1024], func=AF.Identity,
                                                                       scale=s_.t[:, 5:6], bias=s_.t[:, 6:7]),
                 reads=[vg, s_], writes=[vt])
            P.op("vector", lambda E, vt=vt: E.tensor_tensor(out=vt.t[:], in0=vt.t[:], in1=lnw.t[:], op=ALU.mult),
                 reads=[vt, lnw], writes=[vt])
            P.op("vector", lambda E, vt=vt, tt=tt: E.tensor_tensor(out=vn.t[:, tt, :], in0=vt.t[:], in1=lnb.t[:], op=ALU.add),
                 reads=[vt, lnb], pwrites=[vn])
        gur = [C.sb(es, [128, S], F32, "gu") for _ in range(2)]
        szr = [C.sb(es, [128, S], F32, "sz") for _ in range(2)]
        mr = Stager(C, es, [128, 512], F32, 2, "m")
        ybr = [C.sb(es, [128, S], BF16, "ybT") for _ in range(2)]
        nb = [0]
        sgu_toks = []
        for g in range(8):
            gu, sz, yb = gur[g % 2], szr[g % 2], ybr[g % 2]
            P.dma("sync", gu.t[:], u_d[g, :, :], writes=[gu])
            P.dma("sync", sz.t[:], zb_d[g, :, :], writes=[sz])
            for tb in range(4):
                bank = C.banks[nb[0] % 4]
                nb[0] += 1

                def spmm(E, bank=bank, tb=tb, g=g):
                    ins = None
                    for i in range(4):
                        n = 4 * tb + i
                        ins = E.matmul(bank.t[:, i * 128:(i + 1) * 128], vn.t[:, n, g * 128:(g + 1) * 128],
                                       wsm.t[:, g, :], start=True, stop=True)
                    return ins

                P.op("tensor", spmm, reads=[vn, wsm], writes=[bank])
                m = mr.next()
                sl = slice(tb * 512, (tb + 1) * 512)
                P.op("vector", lambda E, m=m, bank=bank, g=g: E.tensor_tensor(
                    out=m.t[:].rearrange("p (i t) -> p i t", t=128), in0=bank.t[:, 0:512].rearrange("p (i t) -> p i t", t=128),
                    in1=sb_bc.t[:, g, :].unsqueeze(1).to_broadcast([128, 4, 128]), op=ALU.add),
                    reads=[bank, sb_bc], writes=[m])
                P.op("gpsimd", lambda E, m=m, gu=gu, sl=sl: E.tensor_tensor(out=m.t[:], in0=m.t[:], in1=gu.t[:, sl],
                                                                            op=ALU.mult), reads=[m, gu], writes=[m])
                P.op("vector", lambda E, m=m, sz=sz, yb=yb, sl=sl: E.tensor_tensor(out=yb.t[:, sl], in0=m.t[:],
                                                                                   in1=sz.t[:, sl], op=ALU.mult),
                     reads=[m, sz], pwrites=[yb])
            if YT_FLAT:
                tok = P.dma("sync", yT_d[1024 + g * 128:1024 + (g + 1) * 128, :], yb.t[:], reads=[yb], dram_write=True)
                sgu_toks.append(tok)
                if ygather is not None and g % 2 == 1:
                    for t_ in sgu_toks[-2:]:
                        P._wait("gpsimd", t_)
                    ygather(4 + g // 2)
            else:
                P.dma("sync", yT_d[:, 1024 + g * 128:1024 + (g + 1) * 128, :].rearrange("j p t -> p j t"),
                      yb.t[:].rearrange("p (j t) -> p j t", j=2), reads=[yb], dram_write=True)
        P.end_phase()
        P.release_bufs()


def _consts():
    i = np.arange(128)
    U = (i[:, None] <= i[None, :]).astype(np.float32)
    return {
        "c_ident": np.eye(128, dtype=np.float32).astype(NPBF),
        "c_Ubf": U.astype(NPBF),
        "c_Uf": U,
        "c_Lgt": (i[:, None] > i[None, :]).astype(np.float32),
        "c_ones": np.ones((128, 128), np.float32),
    }


CONST_SPECS = [("c_ident", BF16), ("c_Ubf", BF16), ("c_Uf", F32), ("c_Lgt", F32), ("c_ones", F32)]


def load_consts(C, es, nc):
    cst = {}
    for nm, dt in CONST_SPECS:
        d = nc.dram_tensor(nm, [128, 128], dt, kind="ExternalInput").ap()
        b = C.sb(es, [128, 128], dt, nm)
        C.P.dma("sync", b.t[:], d, writes=[b])
        cst[nm[2:]] = b
    return cst


def _din(nc, name, shape, dt):
    return nc.dram_tensor(name, list(shape), dt, kind="ExternalInput").ap()


def _dout(nc, name, shape, dt):
    return nc.dram_tensor(name, list(shape), dt, kind="ExternalOutput").ap()


def _dint(nc, name, shape, dt):
    return nc.dram_tensor(name, list(shape), dt).ap()


MIX0_PRM = [("conv_w", [128, 16, 4]), ("conv_b", [128, 16]), ("dt_bias", [16]), ("a_log", [16]), ("d_skip", [16]),
            ("ssd_norm_w", [1024]), ("sgu_ln_w", [1024]), ("sgu_ln_b", [1024]), ("wsT", [8, 128, 128]),
            ("sgu_b", [8, 128])]


def mix0_scratch(nc):
    return {"raw": _dint(nc, "s_raw", [16, 128, S], F32), "zb": _dint(nc, "s_zb", [8, 128, S], F32),
            "u": _dint(nc, "s_u", [8, 128, S], F32), "za": _dint(nc, "s_za", [S, 1024], F32),
            "dt": _dint(nc, "s_dt", [S, 16], F32), "vg": _dint(nc, "s_vg", [S, 2048], F32)}


def mix1_scratch(nc):
    return {"qkT": _dint(nc, "s_qkT", [32, 128, S], BF16), "v": _dint(nc, "s_v", [S, 8, 256], BF16),
            "sg": _dint(nc, "s_sg", [S, 8, 256], F32)}


def build_launch(which):
    nc = bass.Bass("TRN2", target_bir_lowering=False)
    with ExitStack() as es:
        C = Ctx(nc, es)
        cst = load_consts(C, es, nc)
        if which == "normt":
            phase_normt(C, _din(nc, "x", [TL, D], F32), _din(nc, "nw", [D], F32), _dout(nc, "hnT", [D, TL], BF16),
                        cst["ident"])
        elif which == "mix0":
            prm = {k: _din(nc, "p_" + k, shp, F32) for k, shp in MIX0_PRM}
            phase_mix0(C, _din(nc, "hnT_all", [2, D, TL], BF16), _din(nc, "W0", [D, W0_COLS], F32), prm,
                       _dout(nc, "yT", [2, 2048, TL], BF16), mix0_scratch(nc), cst)
        elif which == "mix1":
            phase_mix1(C, _din(nc, "hnT_all", [2, D, TL], BF16), _din(nc, "W1", [D, 8192], F32),
                       _din(nc, "lam", [4, 128], F32), _din(nc, "subw", [256], F32),
                       _dout(nc, "yT", [2, 2048, TL], BF16), mix1_scratch(nc), cst)
        elif which == "out_norm":
            h = _dout(nc, "h", [TL, D], F32)
            phase_out(C, _din(nc, "yTt", [4096, TL], BF16), _din(nc, "Wo", [4096, D], F32),
                      _din(nc, "res", [TL, D], F32), h)
            phase_normt(C, h, _din(nc, "nw", [D], F32), _dout(nc, "hnT", [D, TL], BF16), cst["ident"])
        elif which == "out_final":
            h = _dint(nc, "h", [TL, D], F32)
            phase_out(C, _din(nc, "yTt", [4096, TL], BF16), _din(nc, "Wo", [4096, D], F32),
                      _din(nc, "res", [TL, D], F32), h)
            phase_finalnorm(C, h, _din(nc, "nw", [D], F32), _dout(nc, "out", [TL, D], F32))
    return nc


def _run(nc, in_maps):
    res = run_bass_kernel_spmd(nc, in_maps, core_ids=list(range(8)))
    return res.results


def _f32(a):
    return np.ascontiguousarray(np.asarray(a), dtype=np.float32)


def prep_layer0(r, w_in, conv_w, conv_b, dt_bias, a_log, d_skip, ssd_norm_w, ln_w, ln_b, ws, sb):
    o = 1024 * r
    cols = np.concatenate([
        2048 + o + np.arange(1024), 4096 + 512 * r + np.arange(512), 5120 + 512 * r + np.arange(512),
        6176 + o + np.arange(1024), 8224 + o + np.arange(1024),
        o + np.arange(1024), 6144 + 16 * r + np.arange(16),
        10272 + o + np.arange(1024), 10272 + 1024 * (1 - r) + np.arange(1024)])
    cidx = np.concatenate([o + np.arange(1024), 2048 + 512 * r + np.arange(512), 3072 + 512 * r + np.arange(512)])
    return {
        "W0": np.ascontiguousarray(w_in[:, cols]),
        "p_conv_w": np.ascontiguousarray(conv_w[:, cidx].T.reshape(16, 128, 4).transpose(1, 0, 2)),
        "p_conv_b": np.ascontiguousarray(conv_b[cidx].reshape(16, 128).T),
        "p_dt_bias": np.ascontiguousarray(dt_bias[16 * r:16 * r + 16]),
        "p_a_log": np.ascontiguousarray(a_log[16 * r:16 * r + 16]),
        "p_d_skip": np.ascontiguousarray(d_skip[16 * r:16 * r + 16]),
        "p_ssd_norm_w": np.ascontiguousarray(ssd_norm_w[o:o + 1024]),
        "p_sgu_ln_w": np.ascontiguousarray(ln_w[o:o + 1024]), "p_sgu_ln_b": np.ascontiguousarray(ln_b[o:o + 1024]),
        "p_wsT": np.ascontiguousarray(ws[8 * r:8 * r + 8].transpose(0, 2, 1)),
        "p_sgu_b": np.ascontiguousarray(sb[8 * r:8 * r + 8]),
    }


def prep_layer1(r, w_in):
    cols = []
    for hh in range(8):
        h = 8 * r + hh
        for base in (0, 4096, 8192, 12288):
            cols.append(base + h * 256 + np.arange(256))
    return np.ascontiguousarray(w_in[:, np.concatenate(cols)])


PAIRS = [[0, 1], [2, 3], [4, 5], [6, 7]]
DC = D // 2
CCH = 256


def gather_rows(P, send, gathered, nrows):
    for k in range(nrows // CCH):
        P.collective("AllGather", [send[k * CCH:(k + 1) * CCH, :]],
                     [gathered[2 * k * CCH:(2 * k + 2) * CCH, :]], PAIRS)
    P.end_phase()


def prefetch_wo(C, es, W_d):
    w = C.sb(es, [128, 32, DC], BF16, "wo")
    for hf in range(2):
        C.P.dma("gpsimd", w.t[:, :, hf * 512:(hf + 1) * 512],
                W_d[:, hf * 512:(hf + 1) * 512].rearrange("(kc p) c -> p kc c", p=128), pwrites=[w])
    return w


def phase_out_cs(C, yg, w, res_d, h_d):
    P = C.P
    with ExitStack() as es:
        yring = [C.sb(es, [128, 32, 512], BF16, "yTq") for _ in range(2)]
        rring = Stager(C, es, [128, 512], F32, 3, "res")
        oring = Stager(C, es, [128, 512], F32, 3, "ho")
        nb = 0
        for tq in range(4):
            yT = yring[tq % 2]
            for k in range(8):
                for s_ in range(2):
                    r0 = (2 * k + s_) * CCH
                    P.dma("sync", yT.t[:, s_ * 16 + 2 * k:s_ * 16 + 2 * k + 2, :],
                          yg[r0:r0 + CCH, tq * 512:(tq + 1) * 512].rearrange("(kc p) t -> p kc t", p=128),
                          pwrites=[yT])
            for tt in range(4):
                rows = slice(tq * 512 + tt * 128, tq * 512 + (tt + 1) * 128)
                for cb in range(2):
                    bank = C.banks[nb % 4]
                    nb += 1
                    pairs = [(yT.t[:, k, tt * 128:(tt + 1) * 128], w.t[:, k, cb * 512:(cb + 1) * 512])
                             for k in range(32)]
                    mm_group(P, bank, bank.t[:, 0:512], pairs, reads=[yT, w])
                    r, o = rring.next(), oring.next()
                    P.dma("sync", r.t[:], res_d[rows, cb * 512:(cb + 1) * 512], writes=[r])
                    P.op("vector", lambda E, o=o, bank=bank, r=r: E.tensor_tensor(out=o.t[:], in0=bank.t[:, 0:512],
                                                                                  in1=r.t[:], op=ALU.add),
                         reads=[bank, r], writes=[o])
                    P.dma("sync", h_d[rows, cb * 512:(cb + 1) * 512], o.t[:], reads=[o], dram_write=True)
        P.end_phase()
        P.release_bufs()


def phase_norm_cs(C, h_d, nw_d, ss_send, ss_g, ident, hn_send=None, hn_g=None, out_d=None):
    P = C.P
    with ExitStack() as es:
        nwbc = C.sb(es, [128, DC], F32, "nwbc")
        P.dma("sync", nwbc.t[:], nw_d.partition_broadcast(128), writes=[nwbc])
        hres = C.sb(es, [128, 16, DC], F32, "hres")
        junk = C.sb(es, [128, DC], BF16, "junk")
        ssc = C.sb(es, [128, 16], F32, "ssc")
        for tt in range(16):
            P.dma("sync", hres.t[:, tt, :], h_d[tt * 128:(tt + 1) * 128, :], pwrites=[hres])
        for tt in range(16):
            P.op("scalar", lambda E, tt=tt: E.activation(out=junk.t[:], in_=hres.t[:, tt, :], func=AF.Square,
                                                         accum_out=ssc.t[:, tt:tt + 1]),
                 reads=[hres], writes=[junk], pwrites=[ssc])
        P.dma("sync", ss_send, ssc.t[:], reads=[ssc], dram_write=True)
        P.end_phase()
        P.collective("AllGather", [ss_send], [ss_g], PAIRS)
        P.end_phase()
        ss2 = C.sb(es, [128, 2, 16], F32, "ss2")
        P.dma("sync", ss2.t[:], ss_g.rearrange("(s p) t -> p s t", p=128), writes=[ss2])
        rs = C.sb(es, [128, 16], F32, "rs")
        P.op("vector", lambda E: E.tensor_tensor(out=rs.t[:], in0=ss2.t[:, 0, :], in1=ss2.t[:, 1, :], op=ALU.add),
             reads=[ss2], writes=[rs])
        P.op("scalar", lambda E: E.activation(out=rs.t[:], in_=rs.t[:], func=AF.Sqrt, scale=1.0 / D, bias=EPS),
             reads=[rs], writes=[rs])
        P.op("vector", lambda E: E.reciprocal(out=rs.t[:], in_=rs.t[:]), reads=[rs], writes=[rs])
        if out_d is not None:
            orr = [C.sb(es, [128, DC], F32, "o") for _ in range(2)]
            for tt in range(16):
                o = orr[tt % 2]
                P.op("vector", lambda E, o=o, tt=tt: E.scalar_tensor_tensor(
                    out=o.t[:], in0=hres.t[:, tt, :], scalar=rs.t[:, tt:tt + 1], in1=nwbc.t[:], op0=ALU.mult,
                    op1=ALU.mult), reads=[hres, rs, nwbc], writes=[o])
                P.dma("sync", out_d[tt * 128:(tt + 1) * 128, :], o.t[:], reads=[o], dram_write=True)
            P.end_phase()
            P.release_bufs()
            return
        hnr = [C.sb(es, [128, DC], BF16, "hn") for _ in range(2)]
        hnT = C.sb(es, [128, 8, S], BF16, "hnTo")
        for tt in range(16):
            hn = hnr[tt % 2]
            P.op("vector", lambda E, hn=hn, tt=tt: E.scalar_tensor_tensor(
                out=hn.t[:], in0=hres.t[:, tt, :], scalar=rs.t[:, tt:tt + 1], in1=nwbc.t[:], op0=ALU.mult,
                op1=ALU.mult), reads=[hres, rs, nwbc], writes=[hn])
            bank = C.banks[tt % 4]

            def tr(E, hn=hn, bank=bank):
                ins = None
                for j in range(8):
                    ins = E.transpose(out=bfv(bank)[:, j * 128:(j + 1) * 128], in_=hn.t[:, j * 128:(j + 1) * 128],
                                      identity=ident.t[:])
                return ins

            P.op("tensor", tr, reads=[hn, ident], writes=[bank])
            eng = "scalar" if tt % 2 == 0 else "vector"

            def cp(E, bank=bank, tt=tt, eng=eng):
                src = bfv(bank).rearrange("p (j t) -> p j t", t=128)
                dst = hnT.t[:, :, tt * 128:(tt + 1) * 128]
                return E.copy(out=dst, in_=src) if eng == "scalar" else E.tensor_copy(out=dst, in_=src)

            P.op(eng, cp, reads=[bank], pwrites=[hnT])
        P.dma("sync", hn_send.rearrange("(kc p) t -> p kc t", p=128), hnT.t[:], reads=[hnT], dram_write=True)
        P.end_phase()
        P.release_bufs()
    gather_rows(P, hn_send, hn_g, DC)


def build_fused():
    nc = bass.Bass("TRN2", target_bir_lowering=False)
    with ExitStack() as es:
        C = Ctx(nc, es)
        P = C.P
        cst = load_consts(C, es, nc)
        x_d = _din(nc, "x", [S, D], F32)
        xc_d = _din(nc, "xc", [S, DC], F32)
        nw0, nw1c, nwfc = _din(nc, "nw0", [D], F32), _din(nc, "nw1c", [DC], F32), _din(nc, "nwfc", [DC], F32)
        prm = {k: _din(nc, "p_" + k, shp, F32) for k, shp in MIX0_PRM}
        W0 = _din(nc, "W0", [D, W0_COLS], F32)
        Wo0 = _din(nc, "Wo0", [4096, DC], F32)
        W1 = _din(nc, "W1", [D, 8192], F32)
        lam = _din(nc, "lam", [4, 128], F32)
        subw = _din(nc, "subw", [256], F32)
        Wo1 = _din(nc, "Wo1", [4096, DC], F32)
        out_d = _dout(nc, "out", [S, DC], F32)
        hn0 = _dint(nc, "x_hn0", [D, S], BF16)
        hn_send = _dint(nc, "x_hnsend", [DC, S], BF16)
        hn_g = _dint(nc, "x_hng", [2 * DC, S], BF16)
        y_own = _dint(nc, "x_yown", [2048, S], BF16)
        y_g = _dint(nc, "x_yg", [4096, S], BF16)
        ss_send = _dint(nc, "x_sssend", [128, 16], F32)
        ss_g = _dint(nc, "x_ssg", [256, 16], F32)
        h1 = _dint(nc, "x_h1", [S, DC], F32)
        h2 = _dint(nc, "x_h2", [S, DC], F32)
        scr0, scr1 = mix0_scratch(nc), mix1_scratch(nc)
        hn0_blocks = [(hn0[q * 512:(q + 1) * 512, :], 4 * q, 4, 0, S) for q in range(4)]
        hn1_blocks = [(hn_g[(2 * k + s_) * CCH:(2 * k + s_ + 1) * CCH, :], s_ * 8 + 2 * k, 2, 0, S)
                      for k in range(4) for s_ in range(2)]

        def ygather(k):
            P.collective("AllGather", [y_own[k * CCH:(k + 1) * CCH, :]], [y_g[2 * k * CCH:(2 * k + 2) * CCH, :]], PAIRS)

        phase_mix0(C, lambda es_g: phase_normt(C, x_d, nw0, None, cst["ident"], NT=16, keep=es_g), W0, prm, y_own,
                   scr0, cst, ygather)
        with ExitStack() as es2:
            w = prefetch_wo(C, es2, Wo0)
            P.end_phase()
            phase_out_cs(C, y_g, w, xc_d, h1)
        phase_norm_cs(C, h1, nw1c, ss_send, ss_g, cst["ident"], hn_send=hn_send, hn_g=hn_g)
        phase_mix1(C, hn1_blocks, W1, lam, subw, y_own, scr1, cst, ygather)
        with ExitStack() as es2:
            w = prefetch_wo(C, es2, Wo1)
            P.end_phase()
            phase_out_cs(C, y_g, w, h1, h2)
        phase_norm_cs(C, h2, nwfc, ss_send, ss_g, cst["ident"], out_d=out_d)
    return nc


WOUT0_PERM = np.concatenate([np.concatenate([s * 1024 + np.arange(1024), 2048 + s * 1024 + np.arange(1024)])
                             for s in range(2)])

DEBUG = {}
FUSED = True


def kernel(x, norm_w, even_w_in, even_conv_w, even_conv_b, even_dt_bias, even_a_log, even_d_skip,
           even_ssd_norm_w, even_sgu_ln_w, even_sgu_ln_b, even_sgu_ws, even_sgu_b, even_w_out, odd_w_in,
           odd_lam_q1, odd_lam_k1, odd_lam_q2, odd_lam_k2, odd_subln_w, odd_w_out, final_norm_w):
    x = _f32(x)
    norm_w = _f32(norm_w)
    cs = _consts()
    cores = [(b, r) for b in range(4) for r in range(2)]
    xo = [np.ascontiguousarray(x[b, r * TL:(r + 1) * TL, :]) for b, r in cores]
    l0 = [prep_layer0(r, _f32(even_w_in)[0], _f32(even_conv_w)[0], _f32(even_conv_b)[0], _f32(even_dt_bias)[0],
                      _f32(even_a_log)[0], _f32(even_d_skip)[0], _f32(even_ssd_norm_w)[0], _f32(even_sgu_ln_w)[0],
                      _f32(even_sgu_ln_b)[0], _f32(even_sgu_ws)[0], _f32(even_sgu_b)[0]) for r in range(2)]
    w1 = [prep_layer1(r, _f32(odd_w_in)[0]) for r in range(2)]
    wo0 = np.ascontiguousarray(_f32(even_w_out)[0][WOUT0_PERM])
    wo1 = _f32(odd_w_out)[0]
    lam = np.stack([_f32(odd_lam_q1)[0], _f32(odd_lam_k1)[0], _f32(odd_lam_q2)[0], _f32(odd_lam_k2)[0]])
    subw = _f32(odd_subln_w)[0]

    if FUSED:
        global YT_FLAT
        YT_FLAT = True
        maps = []
        fw = _f32(final_norm_w)
        for c, (b, r) in enumerate(cores):
            cols = slice(r * DC, (r + 1) * DC)
            m = dict(cs, x=np.ascontiguousarray(x[b]), xc=np.ascontiguousarray(x[b][:, cols]), nw0=norm_w[0],
                     nw1c=np.ascontiguousarray(norm_w[1][cols]), nwfc=np.ascontiguousarray(fw[cols]),
                     Wo0=np.ascontiguousarray(wo0[:, cols]), W1=w1[r], lam=lam, subw=subw,
                     Wo1=np.ascontiguousarray(wo1[:, cols]), **l0[r])
            maps.append(m)
        res = _run(build_fused(), maps)
        out = np.empty((4, S, D), np.float32)
        for c, (b, r) in enumerate(cores):
            out[b, :, r * DC:(r + 1) * DC] = np.asarray(res[c]["out"])
        return out

    def gather_hn(res):
        return [np.ascontiguousarray(np.stack([np.asarray(res[2 * b]["hnT"]), np.asarray(res[2 * b + 1]["hnT"])]))
                for b, r in cores]

    def a2a(res):
        return [np.ascontiguousarray(np.concatenate([np.asarray(res[2 * b]["yT"])[r],
                                                     np.asarray(res[2 * b + 1]["yT"])[r]], axis=0))
                for b, r in cores]

    res = _run(build_launch("normt"), [dict(cs, x=xo[c], nw=norm_w[0]) for c in range(8)])
    hn_all = gather_hn(res)
    res = _run(build_launch("mix0"), [dict(cs, hnT_all=hn_all[c], **l0[cores[c][1]]) for c in range(8)])
    yTt = a2a(res)
    DEBUG["y0"] = yTt
    res = _run(build_launch("out_norm"), [dict(cs, yTt=yTt[c], Wo=wo0, res=xo[c], nw=norm_w[1]) for c in range(8)])
    h1 = [np.asarray(res[c]["h"]) for c in range(8)]
    DEBUG["h1"] = h1
    hn_all = gather_hn(res)
    res = _run(build_launch("mix1"), [dict(cs, hnT_all=hn_all[c], W1=w1[cores[c][1]], lam=lam, subw=subw)
                                      for c in range(8)])
    yTt = a2a(res)
    DEBUG["y1"] = yTt
    res = _run(build_launch("out_final"), [dict(cs, yTt=yTt[c], Wo=wo1, res=h1[c], nw=_f32(final_norm_w))
                                           for c in range(8)])
    out = np.empty((4, S, D), np.float32)
    for c, (b, r) in enumerate(cores):
        out[b, r * TL:(r + 1) * TL, :] = np.asarray(res[c]["out"])
    return out
```

```python
import math
from contextlib import ExitStack

import numpy as np
import ml_dtypes
import concourse.bass as bass
import concourse.mybir as mybir
from concourse.bass_utils import run_bass_kernel_spmd

F32 = mybir.dt.float32
BF16 = mybir.dt.bfloat16
AF = mybir.ActivationFunctionType
ALU = mybir.AluOpType
AX = mybir.AxisListType
ENG = ["sync", "scalar", "vector", "gpsimd", "tensor"]
EPS = 1e-6
NPBF = ml_dtypes.bfloat16

D = 2048
S = 2048
TL = 1024
KC = D // 128


class Buf:
    def __init__(self, name, t=None):
        self.name = name
        self.t = t
        self.w = {}
        self.r = {}
        self.dsem = None
        self.excl = False


class Prog:
    def __init__(self, nc, es):
        self.nc = nc
        self.es = es
        self.ccsem = None
        self.q = {e: [] for e in ENG}
        self.cnt = {e: 0 for e in ENG}
        self.esem = {e: es.enter_context(nc.semaphore("prog_" + e)) for e in ENG}
        self.pool = [[es.enter_context(nc.semaphore("dsem%d" % i)), 0] for i in range(76)]
        self.free_slots = list(range(len(self.pool)))
        self.phase_slots = []
        self.waited = {e: {} for e in ENG}
        self.pending = []
        self.nblocks = 0

    def _wait(self, e, tok):
        if tok is None:
            return
        key, h, val = tok
        if self.waited[e].get(key, 0) >= val:
            return
        self.waited[e][key] = val
        self.q[e].append(lambda E, h=h, val=val: E.wait_ge(h, val))

    def _deps(self, e, reads, writes, pwrites=()):
        for b in reads:
            for tok in b.w.values():
                self._wait(e, tok)
            if b.excl:
                for tok in b.r.values():
                    if tok[0] != "E" + e:
                        self._wait(e, tok)
        for b in list(writes) + list(pwrites):
            if b in writes:
                for tok in b.w.values():
                    self._wait(e, tok)
            for tok in b.r.values():
                if tok[0] == "E" + e:
                    continue
                self._wait(e, tok)

    def _commit(self, tok, reads, writes, pwrites=()):
        for b in reads:
            if b in writes or b in pwrites:
                continue
            b.r[tok[0]] = tok
        for b in writes:
            b.w = {tok[0]: tok}
            b.r = {}
        for b in pwrites:
            b.w[tok[0]] = tok

    def op(self, e, fn, reads=(), writes=(), pwrites=()):
        self._deps(e, reads, writes, pwrites)
        self.cnt[e] += 1
        h = self.esem[e]
        self.q[e].append(lambda E, fn=fn, h=h: fn(E).then_inc(h, 1))
        tok = ("E" + e, h, self.cnt[e])
        self._commit(tok, reads, writes, pwrites)
        return tok

    def _slot(self, b):
        if b.dsem is None:
            i = self.free_slots.pop(0)
            self.phase_slots.append((b, i))
            b.dsem = i
        return b.dsem

    def dma(self, e, out, in_, reads=(), writes=(), pwrites=(), dram_write=False, sem_buf=None):
        self._deps(e, reads, writes, pwrites)
        sb = sem_buf if sem_buf is not None else (list(writes) + list(pwrites) + list(reads))[0]
        i = self._slot(sb)
        self.pool[i][1] += 16
        h, val = self.pool[i][0], self.pool[i][1]

        def issue(E, out=out, in_=in_, h=h):
            o_ = out(E) if callable(out) else out
            i_ = in_(E) if callable(in_) else in_
            return E.dma_start(out=o_, in_=i_).then_inc(h, 16)

        self.q[e].append(issue)
        tok = ("D%d" % i, h, val)
        self._commit(tok, reads, writes, pwrites)
        if dram_write:
            self.pending.append(tok)
        return tok

    def collective(self, kind, ins, outs, groups):
        if self.ccsem is None:
            self.ccsem = [self.es.enter_context(self.nc.semaphore("ccsem")), 0]
        self.ccsem[1] += 1
        h, val = self.ccsem
        self.q["gpsimd"].append(lambda E: E.collective_compute(kind, ALU.bypass, replica_groups=groups, ins=ins,
                                                               outs=outs).then_inc(h, 1))
        self.pending.append(("CC", h, val))

    def end_phase(self, last=False):
        for tok in self.pending:
            self._wait("sync", tok)
        self.pending = []
        nc = self.nc
        q = self.q
        with nc.Block() as block:
            for e in ENG:
                if not q[e]:
                    continue

                def body(E, lst=q[e]):
                    for fn in lst:
                        fn(E)

                getattr(block, e)(body)
        self.q = {e: [] for e in ENG}
        self.nblocks += 1

    def release_bufs(self):
        for b, i in self.phase_slots:
            b.dsem = None
            self.free_slots.append(i)
        self.phase_slots = []


class Ctx:
    def __init__(self, nc, es):
        self.nc = nc
        self.es = es
        self.P = Prog(nc, es)
        self.n = 0
        self.banks = []
        for i in range(8):
            t = es.enter_context(nc.psum_tensor("bank%d" % i, [128, 512], F32))
            self.banks.append(Buf("bank%d" % i, t))
            self.banks[-1].excl = True

    def sb(self, es, shape, dtype, name=None):
        self.n += 1
        nm = "%s_%d" % (name or "t", self.n)
        t = es.enter_context(self.nc.sbuf_tensor(nm, list(shape), dtype))
        return Buf(nm, t)


def bfv(bank):
    return bank.t[:].bitcast(BF16)


def load_w(P, wb, W, c0, cw, kc=KC):
    src = W[:, c0:c0 + cw].rearrange("(kc p) c -> p kc c", p=128)
    P.dma("gpsimd", wb.t[:, 0:kc, 0:cw], src, writes=[wb])


def mm_group(P, bank, out_ap, pairs, reads):
    def fn(E, pairs=pairs, out_ap=out_ap):
        n = len(pairs)
        ins = None
        for i, (l, r) in enumerate(pairs):
            ins = E.matmul(out_ap, l, r, start=(i == 0), stop=(i == n - 1))
        return ins

    return P.op("tensor", fn, reads=reads, writes=[bank])


def rstd_from_ss(P, ss, rstd, n):
    P.op("scalar", lambda E: E.activation(out=rstd.t[:], in_=ss.t[:], func=AF.Sqrt, scale=1.0 / n, bias=EPS),
         reads=[ss], writes=[rstd])
    P.op("vector", lambda E: E.reciprocal(out=rstd.t[:], in_=rstd.t[:]), reads=[rstd], writes=[rstd])


def phase_normt(C, x_d, nw_d, hnT_d, ident, NT=8, keep=None):
    P = C.P
    with ExitStack() as es_:
        es = keep if keep is not None else es_
        nwbc = C.sb(es, [128, D], F32, "nwbc")
        P.dma("sync", nwbc.t[:], nw_d.partition_broadcast(128), writes=[nwbc])
        xr = [C.sb(es, [128, D], F32, "x") for _ in range(2)]
        junk = C.sb(es, [128, D], BF16, "junk")
        ssr = [C.sb(es, [128, 1], F32, "ss") for _ in range(2)]
        rsr = [C.sb(es, [128, 1], F32, "rs") for _ in range(2)]
        hnr = [C.sb(es, [128, D], BF16, "hn") for _ in range(2)]
        hnT = C.sb(es, [128, KC, NT * 128], BF16, "hnT")
        for i in range(NT):
            x, ss, rs, hn = xr[i % 2], ssr[i % 2], rsr[i % 2], hnr[i % 2]
            P.dma("sync", x.t[:], x_d[i * 128:(i + 1) * 128, :], writes=[x])
            P.op("scalar", lambda E, x=x, ss=ss: E.activation(out=junk.t[:], in_=x.t[:], func=AF.Square,
                                                              accum_out=ss.t[:]),
                 reads=[x], writes=[junk, ss])
            rstd_from_ss(P, ss, rs, D)
            P.op("vector", lambda E, x=x, rs=rs, hn=hn: E.scalar_tensor_tensor(
                out=hn.t[:], in0=x.t[:], scalar=rs.t[:, 0:1], in1=nwbc.t[:], op0=ALU.mult, op1=ALU.mult),
                reads=[x, rs, nwbc], writes=[hn])
            for half in range(2):
                bank = C.banks[(2 * i + half) % 4]

                def tr(E, hn=hn, bank=bank, half=half):
                    ins = None
                    for j in range(8):
                        k = half * 8 + j
                        ins = E.transpose(out=bfv(bank)[:, j * 128:(j + 1) * 128],
                                          in_=hn.t[:, k * 128:(k + 1) * 128], identity=ident.t[:])
                    return ins

                P.op("tensor", tr, reads=[hn, ident], writes=[bank])
                eng = "scalar" if half == 0 else "vector"

                def cp(E, bank=bank, half=half, i=i, eng=eng):
                    src = bfv(bank).rearrange("p (j t) -> p j t", t=128)
                    dst = hnT.t[:, half * 8:(half + 1) * 8, i * 128:(i + 1) * 128]
                    if eng == "scalar":
                        return E.copy(out=dst, in_=src)
                    return E.tensor_copy(out=dst, in_=src)

                P.op(eng, cp, reads=[bank], pwrites=[hnT])
        if keep is not None:
            return hnT
        for h in range(2):
            P.dma("sync", hnT_d[h * 1024:(h + 1) * 1024, :].rearrange("(kc p) t -> p kc t", p=128),
                  hnT.t[:, h * 8:(h + 1) * 8, :], reads=[hnT], dram_write=True)
        P.end_phase()
        P.release_bufs()


class Stager:
    def __init__(self, C, es, shape, dtype, n=3, name="stg"):
        self.bufs = [C.sb(es, shape, dtype, name) for _ in range(n)]
        self.i = 0

    def next(self):
        b = self.bufs[self.i % len(self.bufs)]
        self.i += 1
        return b


def gemm_T(C, actT, kc, ntt, W, blocks, epi, wring, banks):
    P = C.P
    cnt = 0
    for bi, (c0, cw, tag) in enumerate(blocks):
        wb = wring[bi % len(wring)]
        load_w(P, wb, W, c0, cw, kc)
        for tt in range(ntt):
            bank = banks[cnt % len(banks)]
            cnt += 1
            pairs = [(actT.t[:, k, tt * 128:(tt + 1) * 128], wb.t[:, k, 0:cw]) for k in range(kc)]
            mm_group(P, bank, bank.t[:, 0:cw], pairs, reads=[actT, wb])
            epi(tag, c0, cw, tt, bank)


def gemm_F(C, actT, kc, T, W, blocks, epi, wring, banks):
    P = C.P
    cnt = 0
    for bi, (c0, cw, tag) in enumerate(blocks):
        wb = wring[bi % len(wring)]
        load_w(P, wb, W, c0, cw, kc)
        for ch in range(cw // 128):
            for tb in range(T // 512):
                bank = banks[cnt % len(banks)]
                cnt += 1
                pairs = [(wb.t[:, k, ch * 128:(ch + 1) * 128], actT.t[:, k, tb * 512:(tb + 1) * 512])
                         for k in range(kc)]
                mm_group(P, bank, bank.t[:, 0:512], pairs, reads=[actT, wb])
                epi(tag, c0 + ch * 128, tb, bank)


def act_epi(P, bank, src, dst_buf, dst, func, eng="scalar"):
    if eng == "scalar":
        P.op("scalar", lambda E: E.activation(out=dst, in_=src, func=func), reads=[bank], writes=[dst_buf])
    else:
        P.op("vector", lambda E: E.tensor_copy(out=dst, in_=src), reads=[bank], writes=[dst_buf])


GELU_C = 0.044715
GELU_S = 2.0 * math.sqrt(2.0 / math.pi)


def gelu_epi(P, bank, src, tmp, tmp2, dst_buf, dst, n):
    P.op("scalar", lambda E: E.activation(out=tmp.t[:, 0:n], in_=src, func=AF.Square), reads=[bank], writes=[tmp])
    P.op("vector", lambda E: E.tensor_scalar(out=tmp.t[:, 0:n], in0=tmp.t[:, 0:n], scalar1=GELU_C, scalar2=1.0,
                                             op0=ALU.mult, op1=ALU.add), reads=[tmp], writes=[tmp])
    P.op("vector", lambda E: E.tensor_tensor(out=tmp.t[:, 0:n], in0=tmp.t[:, 0:n], in1=src, op=ALU.mult),
         reads=[tmp, bank], writes=[tmp])
    P.op("scalar", lambda E: E.activation(out=tmp2.t[:, 0:n], in_=tmp.t[:, 0:n], func=AF.Sigmoid, scale=GELU_S),
         reads=[tmp], writes=[tmp2])
    P.op("vector", lambda E: E.tensor_tensor(out=dst, in0=tmp2.t[:, 0:n], in1=src, op=ALU.mult),
         reads=[tmp2, bank], writes=[dst_buf])


def load_hnT(C, es, src):
    hnT = C.sb(es, [128, KC, S], BF16, "hnTall")
    if not isinstance(src, list):
        src = [(src[j, h * 1024:(h + 1) * 1024, :], h * 8, 8, j * TL, TL) for j in range(2) for h in range(2)]
    for ap, kc0, nk, t0, nt in src:
        C.P.dma("sync", hnT.t[:, kc0:kc0 + nk, t0:t0 + nt], ap.rearrange("(kc p) t -> p kc t", p=128), pwrites=[hnT])
    return hnT


def phase_out(C, yT_d, W_d, res_d, h_d, ysrc=None, dyn_eng="sync"):
    P = C.P
    with ExitStack() as es:
        yT = C.sb(es, [128, 32, TL], BF16, "yT")
        if ysrc is None:
            for q in range(4):
                P.dma("sync", yT.t[:, q * 8:(q + 1) * 8, :],
                      yT_d[q * 1024:(q + 1) * 1024, :].rearrange("(kc p) t -> p kc t", p=128), pwrites=[yT])
        else:
            for s_ in range(2):
                P.dma(dyn_eng, yT.t[:, s_ * 16:(s_ + 1) * 16, :], ysrc(s_), pwrites=[yT])
        wring = [C.sb(es, [128, 32, 512], BF16, "wo") for _ in range(2)]
        rring = Stager(C, es, [128, 512], F32, 3, "res")
        oring = Stager(C, es, [128, 512], F32, 3, "ho")

        def epi(tag, c0, cw, tt, bank):
            r = rring.next()
            o = oring.next()
            P.dma("sync", r.t[:], res_d[tt * 128:(tt + 1) * 128, c0:c0 + cw], writes=[r])
            P.op("vector", lambda E: E.tensor_tensor(out=o.t[:], in0=bank.t[:, 0:cw], in1=r.t[:], op=ALU.add),
                 reads=[bank, r], writes=[o])
            P.dma("sync", h_d[tt * 128:(tt + 1) * 128, c0:c0 + cw], o.t[:], reads=[o], dram_write=True)

        gemm_T(C, yT, 32, TL // 128, W_d, [(c * 512, 512, None) for c in range(4)], epi, wring, C.banks[0:4])
        P.end_phase()
        P.release_bufs()


def phase_finalnorm(C, x_d, nw_d, out_d, NT=8):
    P = C.P
    with ExitStack() as es:
        nwbc = C.sb(es, [128, D], F32, "nwbc")
        P.dma("sync", nwbc.t[:], nw_d.partition_broadcast(128), writes=[nwbc])
        xr = [C.sb(es, [128, D], F32, "x") for _ in range(2)]
        junk = C.sb(es, [128, D], BF16, "junk")
        ssr = [C.sb(es, [128, 1], F32, "ss") for _ in range(2)]
        rsr = [C.sb(es, [128, 1], F32, "rs") for _ in range(2)]
        orr = [C.sb(es, [128, D], F32, "o") for _ in range(2)]
        for i in range(NT):
            x, ss, rs, o = xr[i % 2], ssr[i % 2], rsr[i % 2], orr[i % 2]
            P.dma("sync", x.t[:], x_d[i * 128:(i + 1) * 128, :], writes=[x])
            P.op("scalar", lambda E, x=x, ss=ss: E.activation(out=junk.t[:], in_=x.t[:], func=AF.Square,
                                                              accum_out=ss.t[:]),
                 reads=[x], writes=[junk, ss])
            rstd_from_ss(P, ss, rs, D)
            P.op("vector", lambda E, x=x, rs=rs, o=o: E.scalar_tensor_tensor(
                out=o.t[:], in0=x.t[:], scalar=rs.t[:, 0:1], in1=nwbc.t[:], op0=ALU.mult, op1=ALU.mult),
                reads=[x, rs, nwbc], writes=[o])
            P.dma("sync", out_d[i * 128:(i + 1) * 128, :], o.t[:], reads=[o], dram_write=True)
        P.end_phase()
        P.release_bufs()


NH1 = 8
MIX1_ATT = True
YT_FLAT = False
LAMBDA_INIT = 0.8 - 0.6 * math.exp(-0.3 * 1)


def phase_mix1(C, hnT_all_d, W1_d, lam_d, subw_d, yT_d, scr, cst, ygather=None):
    P = C.P
    nc = C.nc
    qkT_d, v_d, sg_d = scr["qkT"], scr["v"], scr["sg"]
    with ExitStack() as es:
        hnT = load_hnT(C, es, hnT_all_d)
        wring = [C.sb(es, [128, KC, 512], BF16, "w1") for _ in range(2)]
        stq = Stager(C, es, [128, 512], BF16, 3, "stq")
        stv = Stager(C, es, [128, 256], BF16, 3, "stv")
        stg = Stager(C, es, [128, 256], F32, 3, "stg")
        cntq = [0]

        def epiF(tag, c0, tb, bank):
            hh = tag
            cc = (c0 - hh * 1024) // 128
            st = stq.next()
            eng = "scalar" if cntq[0] % 2 == 0 else "vector"
            cntq[0] += 1
            act_epi(P, bank, bank.t[:, 0:512], st, st.t[:], AF.Copy, eng)
            P.dma("sync", qkT_d[hh * 4 + cc, :, tb * 512:(tb + 1) * 512], st.t[:], reads=[st], dram_write=True)

        def epiT(tag, c0, cw, tt, bank):
            hh = tag
            sv = stv.next()
            sg = stg.next()
            P.op("vector", lambda E: E.tensor_copy(out=sv.t[:], in_=bank.t[:, 0:256]), reads=[bank], writes=[sv])
            P.op("scalar", lambda E: E.activation(out=sg.t[:], in_=bank.t[:, 256:512], func=AF.Silu),
                 reads=[bank], writes=[sg])
            P.dma("sync", v_d[tt * 128:(tt + 1) * 128, hh, :], sv.t[:], reads=[sv], dram_write=True)
            P.dma("sync", sg_d[tt * 128:(tt + 1) * 128, hh, :], sg.t[:], reads=[sg], dram_write=True)

        for hh in range(NH1):
            gemm_F(C, hnT, KC, S, W1_d, [(hh * 1024, 512, hh)], epiF, [wring[0]], C.banks[0:4])
            gemm_T(C, hnT, KC, S // 128, W1_d, [(hh * 1024 + 512, 512, hh)], epiT, [wring[1]], C.banks[4:8])
        P.end_phase()
        P.release_bufs()

    if not MIX1_ATT:
        return
    with ExitStack() as es:
        ident, Ubf = cst["ident"], cst["Ubf"]
        scale = 128.0 ** -0.5
        lamv = C.sb(es, [128, 4, 128], F32, "lamv")
        P.dma("sync", lamv.t[:], lam_d.rearrange("a d -> (a d)").partition_broadcast(128)
              .rearrange("p (a d) -> p a d", a=4), writes=[lamv])
        lprod = C.sb(es, [128, 2, 128], F32, "lprod")
        lsum = C.sb(es, [128, 2], F32, "lsum")
        lam = C.sb(es, [128, 1], F32, "lam")
        P.op("vector", lambda E: E.tensor_tensor(out=lprod.t[:, 0, :], in0=lamv.t[:, 0, :], in1=lamv.t[:, 1, :],
                                                 op=ALU.mult), reads=[lamv], writes=[lprod])
        P.op("vector", lambda E: E.tensor_tensor(out=lprod.t[:, 1, :], in0=lamv.t[:, 2, :], in1=lamv.t[:, 3, :],
                                                 op=ALU.mult), reads=[lamv, lprod], writes=[lprod])
        P.op("vector", lambda E: E.tensor_reduce(out=lsum.t[:], in_=lprod.t[:], axis=AX.X, op=ALU.add),
             reads=[lprod], writes=[lsum])
        P.op("scalar", lambda E: E.activation(out=lsum.t[:], in_=lsum.t[:], func=AF.Exp), reads=[lsum], writes=[lsum])
        P.op("vector", lambda E: E.tensor_tensor(out=lam.t[:], in0=lsum.t[:, 0:1], in1=lsum.t[:, 1:2],
                                                 op=ALU.subtract), reads=[lsum], writes=[lam])
        P.op("vector", lambda E: E.tensor_scalar(out=lam.t[:], in0=lam.t[:], scalar1=LAMBDA_INIT, scalar2=None,
                                                 op0=ALU.add), reads=[lam], writes=[lam])
        subw = C.sb(es, [128, 256], F32, "subw")
        P.dma("sync", subw.t[:], subw_d.partition_broadcast(128), writes=[subw])
        P.op("vector", lambda E: E.tensor_scalar(out=subw.t[:], in0=subw.t[:], scalar1=1.0 - LAMBDA_INIT,
                                                 scalar2=None, op0=ALU.mult), reads=[subw], writes=[subw])

        qkr = [C.sb(es, [128, 4, S], BF16, "qk") for _ in range(2)]
        vr = [C.sb(es, [128, 16, 258], BF16, "vaug") for _ in range(2)]
        sgr = [C.sb(es, [128, 16, 256], F32, "sg") for _ in range(2)]
        for v in vr:
            P.op("vector", lambda E, v=v: E.memset(v.t[:, :, 256:258], 1.0), writes=[v])
        PTS = [[[C.sb(es, [128, 512], BF16, "PT") for _ in range(16)] for _ in range(2)] for _ in range(2)]
        yTr = [C.sb(es, [128, 2, S], BF16, "yTh") for _ in range(2)]
        o2r = Stager(C, es, [128, 256], F32, 2, "o2")
        orr = Stager(C, es, [128, 256], F32, 2, "o")
        junk = C.sb(es, [128, 256], BF16, "junk")
        smr = Stager(C, es, [128, 4], F32, 4, "small")
        wgr = Stager(C, es, [128, 256], F32, 2, "wg")
        ybr = Stager(C, es, [128, 256], BF16, 3, "yb")
        sbanks = [C.banks[0], C.banks[1], C.banks[6]]
        abanks = C.banks[2:6]
        tbanks = [C.banks[7]]
        ns = [0]
        na = [0]
        nt = [0]
        units = [(hh, R) for hh in range(NH1) for R in range(4)]
        deferred = []

        def flush():
            while deferred:
                deferred.pop(0)()

        def st_tiles(u):
            hh, R = units[u]
            qk, va, sg = qkr[hh % 2], vr[hh % 2], sgr[hh % 2]
            PT = PTS[u % 2]
            out = []
            if R == 0:
                def loads():
                    P.dma("sync", qk.t[:], qkT_d[hh * 4:(hh + 1) * 4, :, :].rearrange("c p t -> p c t"), writes=[qk])
                    P.dma("sync", va.t[:, :, 0:256], v_d[:, hh, :].rearrange("(tt p) d -> p tt d", p=128), pwrites=[va])
                    P.dma("sync", sg.t[:], sg_d[:, hh, :].rearrange("(tt p) d -> p tt d", p=128), writes=[sg])
                out.append(loads)
            nk = 4 * R + 4
            for j in range(2):
                for kc in range(nk):
                    def tile(j=j, kc=kc):
                        off = max(0, kc - 4 * R) * 128
                        bank = sbanks[ns[0] % 3]
                        ns[0] += 1
                        pt = PT[j][kc]
                        mm_group(P, bank, bank.t[:, off:512],
                                 [(qk.t[:, 2 + j, kc * 128:(kc + 1) * 128],
                                   qk.t[:, j, R * 512 + off:(R + 1) * 512])], reads=[qk])
                        P.op("scalar", lambda E: E.activation(out=pt.t[:, off:512], in_=bank.t[:, off:512],
                                                              func=AF.Exp, scale=scale), reads=[bank], writes=[pt])
                        if kc >= 4 * R:
                            P.op("gpsimd", lambda E: E.tensor_tensor(
                                out=pt.t[:, off:off + 128], in0=pt.t[:, off:off + 128], in1=Ubf.t[:], op=ALU.mult),
                                reads=[pt, Ubf], writes=[pt])
                    out.append(tile)
            return out

        def emit_PV(u, nxt):
            hh, R = units[u]
            va, sg, yTh = vr[hh % 2], sgr[hh % 2], yTr[hh % 2]
            PT = PTS[u % 2]
            per = (len(nxt) + 7) // 8
            for qs in range(4):
                qb = 4 * R + qs
                accs = []
                for j in range(2):
                    bank = abanks[na[0] % 4]
                    na[0] += 1
                    accs.append(bank)
                    pairs = [(PT[j][kc].t[:, qs * 128:(qs + 1) * 128], va.t[:, kc, 0:257]) for kc in range(qb + 1)]
                    mm_group(P, bank, bank.t[:, 0:257], pairs, reads=[va] + [PT[j][kc] for kc in range(qb + 1)])
                    if j == 1:
                        flush()
                    for _ in range(per):
                        if nxt:
                            nxt.pop(0)()
                a1, a2 = accs
                sm, o2, o, wg, yb = smr.next(), o2r.next(), orr.next(), wgr.next(), ybr.next()
                P.op("vector", lambda E, sm=sm, a1=a1: E.reciprocal(out=sm.t[:, 0:1], in_=a1.t[:, 256:257]),
                     reads=[a1], writes=[sm])
                P.op("vector", lambda E, sm=sm, a2=a2: E.reciprocal(out=sm.t[:, 1:2], in_=a2.t[:, 256:257]),
                     reads=[a2, sm], writes=[sm])
                P.op("vector", lambda E, sm=sm: E.tensor_tensor(out=sm.t[:, 1:2], in0=sm.t[:, 1:2], in1=lam.t[:],
                                                                op=ALU.mult), reads=[sm, lam], writes=[sm])
                P.op("vector", lambda E, sm=sm, a2=a2, o2=o2: E.tensor_scalar(
                    out=o2.t[:], in0=a2.t[:, 0:256], scalar1=sm.t[:, 1:2], scalar2=None, op0=ALU.mult),
                    reads=[a2, sm], writes=[o2])
                P.op("vector", lambda E, sm=sm, a1=a1, o2=o2, o=o: E.scalar_tensor_tensor(
                    out=o.t[:], in0=a1.t[:, 0:256], scalar=sm.t[:, 0:1], in1=o2.t[:], op0=ALU.mult,
                    op1=ALU.subtract), reads=[a1, sm, o2], writes=[o])
                sm2 = smr.next()
                P.op("vector", lambda E, o=o, sm2=sm2: E.scalar_tensor_tensor(
                    out=junk.t[:], in0=o.t[:], scalar=1.0, in1=o.t[:], op0=ALU.mult, op1=ALU.mult,
                    accum_out=sm2.t[:, 0:1]), reads=[o], writes=[junk, sm2])
                P.op("scalar", lambda E, sm2=sm2: E.activation(out=sm2.t[:, 1:2], in_=sm2.t[:, 0:1], func=AF.Sqrt,
                                                               scale=1.0 / 256, bias=EPS), reads=[sm2], writes=[sm2])
                P.op("vector", lambda E, sm2=sm2: E.reciprocal(out=sm2.t[:, 2:3], in_=sm2.t[:, 1:2]),
                     reads=[sm2], writes=[sm2])
                P.op("gpsimd", lambda E, wg=wg, sg=sg, qb=qb: E.tensor_tensor(
                    out=wg.t[:], in0=sg.t[:, qb, :], in1=subw.t[:], op=ALU.mult), reads=[sg, subw], writes=[wg])
                P.op("vector", lambda E, o=o, sm2=sm2, wg=wg, yb=yb: E.scalar_tensor_tensor(
                    out=yb.t[:], in0=o.t[:], scalar=sm2.t[:, 2:3], in1=wg.t[:], op0=ALU.mult, op1=ALU.mult),
                    reads=[o, sm2, wg], writes=[yb])

                def late(yb=yb, yTh=yTh, qb=qb):
                    tbank = tbanks[0]
                    nt[0] += 1

                    def tr(E):
                        ins = None
                        for c in range(2):
                            ins = E.transpose(out=bfv(tbank)[:, c * 128:(c + 1) * 128],
                                              in_=yb.t[:, c * 128:(c + 1) * 128], identity=ident.t[:])
                        return ins

                    P.op("tensor", tr, reads=[yb, ident], writes=[tbank])
                    P.op("vector", lambda E: E.tensor_copy(
                        out=yTh.t[:, :, qb * 128:(qb + 1) * 128],
                        in_=bfv(tbank)[:, 0:256].rearrange("p (c t) -> p c t", t=128)), reads=[tbank], pwrites=[yTh])

                deferred.append(late)
            if R == 3:
                def store(hh=hh, yTh=yTh):
                    if YT_FLAT:
                        tok = P.dma("sync", yT_d[hh * 256:(hh + 1) * 256, :].rearrange("(c p) t -> p c t", p=128),
                                    yTh.t[:], reads=[yTh], dram_write=True)
                        if ygather is not None:
                            P._wait("gpsimd", tok)
                            ygather(hh)
                    else:
                        for j in range(2):
                            P.dma("sync", yT_d[j, hh * 256:(hh + 1) * 256, :].rearrange("(c p) t -> p c t", p=128),
                                  yTh.t[:, :, j * TL:(j + 1) * TL], reads=[yTh], dram_write=True)

                deferred.append(store)

        for f in st_tiles(0):
            f()
        for u in range(len(units)):
            nxt = st_tiles(u + 1) if u + 1 < len(units) else []
            emit_PV(u, nxt)
            while nxt:
                nxt.pop(0)()
        flush()
        P.end_phase()
        P.release_bufs()


W0_F = 4096
W0_COLS = 4096 + 1024 + 16 + 2048


class nc_allow:
    def __init__(self, C):
        self.C = C

    def __enter__(self):
        return self

    def __exit__(self, *a):
        return False


def bc3(ap2, n):
    return ap2.unsqueeze(2).to_broadcast([ap2.shape[0], ap2.shape[1], n])


def phase_mix0(C, hnT_all_d, W0_d, prm, yT_d, scr, cst, ygather=None):
    P = C.P
    raw_d, zb_d, u_d, za_d, dt_d, vg_d = scr["raw"], scr["zb"], scr["u"], scr["za"], scr["dt"], scr["vg"]
    ident, Ubf, Uf, Lgt, ones = cst["ident"], cst["Ubf"], cst["Uf"], cst["Lgt"], cst["ones"]
    with ExitStack() as es:
        hnT = hnT_all_d(es) if callable(hnT_all_d) else load_hnT(C, es, hnT_all_d)
        wring = [C.sb(es, [128, KC, 512], BF16, "w0") for _ in range(2)]
        st = Stager(C, es, [128, 512], F32, 4, "st")
        tmpr = Stager(C, es, [128, 512], F32, 2, "gt")
        tmp2r = Stager(C, es, [128, 512], F32, 2, "gt2")
        cn = [0]

        def epiF(tag, c0, tb, bank):
            s_ = st.next()
            ch = c0 // 128
            if tag == "raw":
                eng = "scalar" if cn[0] % 2 == 0 else "vector"
                cn[0] += 1
                act_epi(P, bank, bank.t[:, 0:512], s_, s_.t[:], AF.Copy, eng)
                dst = raw_d[ch, :, tb * 512:(tb + 1) * 512]
            elif tag == "zb":
                act_epi(P, bank, bank.t[:, 0:512], s_, s_.t[:], AF.Silu)
                dst = zb_d[ch - 16, :, tb * 512:(tb + 1) * 512]
            else:
                gelu_epi(P, bank, bank.t[:, 0:512], tmpr.next(), tmp2r.next(), s_, s_.t[:], 512)
                dst = u_d[ch - 24, :, tb * 512:(tb + 1) * 512]
            P.dma("sync", dst, s_.t[:], reads=[s_], dram_write=True)

        def epiT(tag, c0, cw, tt, bank):
            s_ = st.next()
            rows = slice(tt * 128, (tt + 1) * 128)
            if tag == "za":
                act_epi(P, bank, bank.t[:, 0:cw], s_, s_.t[:, 0:cw], AF.Silu)
                dst = za_d[rows, c0 - W0_F:c0 - W0_F + cw]
            elif tag == "dt":
                act_epi(P, bank, bank.t[:, 0:cw], s_, s_.t[:, 0:cw], AF.Copy, "vector")
                dst = dt_d[rows, :]
            else:
                gelu_epi(P, bank, bank.t[:, 0:cw], tmpr.next(), tmp2r.next(), s_, s_.t[:, 0:cw], cw)
                v0 = c0 - (W0_F + 1040)
                dst = vg_d[rows, v0:v0 + cw]
            P.dma("sync", dst, s_.t[:, 0:cw], reads=[s_], dram_write=True)

        fblocks = [(i * 512, 512, "raw") for i in range(4)] + [(2048 + i * 512, 512, "zb") for i in range(2)] + \
                  [(3072 + i * 512, 512, "u") for i in range(2)]
        gemm_F(C, hnT, KC, S, W0_d, fblocks, epiF, wring, C.banks[0:4])
        tblocks = [(W0_F, 512, "za"), (W0_F + 512, 512, "za"), (W0_F + 1024, 16, "dt")] + \
                  [(W0_F + 1040 + i * 512, 512, "v") for i in range(4)]
        gemm_T(C, hnT, KC, S // 128, W0_d, tblocks, epiT, wring, C.banks[4:8])
        P.end_phase()
        P.release_bufs()

    with ExitStack() as es:
        def bcast_load(name, src, n):
            b = C.sb(es, [128, n], F32, name)
            P.dma("sync", b.t[:], src.partition_broadcast(128), writes=[b])
            return b

        cw_sb = C.sb(es, [128, 16, 4], F32, "convw")
        P.dma("sync", cw_sb.t[:], prm["conv_w"], writes=[cw_sb])
        cb_sb = C.sb(es, [128, 16], F32, "convb")
        P.dma("sync", cb_sb.t[:], prm["conv_b"], writes=[cb_sb])
        dtb = bcast_load("dtb", prm["dt_bias"], 16)
        a_bc = bcast_load("a_bc", prm["a_log"], 16)
        dsk = bcast_load("dsk", prm["d_skip"], 16)
        nrmw = bcast_load("nrmw", prm["ssd_norm_w"], 1024)
        P.op("scalar", lambda E: E.activation(out=a_bc.t[:], in_=a_bc.t[:], func=AF.Exp), reads=[a_bc], writes=[a_bc])
        P.op("vector", lambda E: E.tensor_scalar(out=a_bc.t[:], in0=a_bc.t[:], scalar1=-1.0, scalar2=None,
                                                 op0=ALU.mult), reads=[a_bc], writes=[a_bc])
        xbcT = C.sb(es, [128, 16, S], BF16, "xbcT")
        rawr = [C.sb(es, [128, 3 + S], F32, "raw") for _ in range(2)]
        accr = [C.sb(es, [128, S], F32, "cacc") for _ in range(2)]
        for r_ in rawr:
            P.op("vector", lambda E, r_=r_: E.memset(r_.t[:, 0:3], 0.0), writes=[r_])
        for c in range(16):
            rw, acc = rawr[c % 2], accr[c % 2]
            P.dma("sync", rw.t[:, 3:3 + S], raw_d[c, :, :], pwrites=[rw])
            P.op("vector", lambda E, rw=rw, acc=acc, c=c: E.tensor_scalar(
                out=acc.t[:], in0=rw.t[:, 3:3 + S], scalar1=cw_sb.t[:, c, 3:4], scalar2=None, op0=ALU.mult),
                reads=[rw, cw_sb], writes=[acc])
            for k in range(3):
                P.op("vector", lambda E, rw=rw, acc=acc, c=c, k=k: E.scalar_tensor_tensor(
                    out=acc.t[:], in0=rw.t[:, k:k + S], scalar=cw_sb.t[:, c, k:k + 1], in1=acc.t[:],
                    op0=ALU.mult, op1=ALU.add), reads=[rw, cw_sb, acc], writes=[acc])
            P.op("scalar", lambda E, acc=acc, c=c: E.activation(out=xbcT.t[:, c, :], in_=acc.t[:], func=AF.Silu,
                                                                bias=cb_sb.t[:, c:c + 1]),
                 reads=[acc, cb_sb], pwrites=[xbcT])

        prev32 = C.sb(es, [128, 1024], F32, "prev32")
        prevbf = C.sb(es, [128, 1024], BF16, "prevbf")
        P.op("vector", lambda E: E.memset(prev32.t[:], 0.0), writes=[prev32])
        P.op("vector", lambda E: E.memset(prevbf.t[:], 0.0), writes=[prevbf])
        R2 = lambda shape, dt, nm: Stager(C, es, shape, dt, 2, nm)
        zar, dtr_, smr = R2([128, 1024], F32, "za"), R2([128, 16], F32, "dtraw"), R2([128, 8, 16], F32, "ssm")
        xsr, xdr, xder, btr = R2([128, 1024], BF16, "xs"), R2([128, 1024], BF16, "xd"), R2([128, 1024], BF16, "xde"), \
            R2([128, 512], BF16, "btm")
        cbr = R2([128, 4, 128], F32, "cbm")
        Ar, Er, MTr = Stager(C, es, [128, 4, 128], F32, 3, "A"), Stager(C, es, [128, 512], F32, 3, "E"), \
            Stager(C, es, [128, 4, 128], BF16, 4, "MT")
        ssd_deferred = []
        t1r, t3r, ynr = R2([128, 1024], F32, "t1"), R2([128, 1024], F32, "t3"), R2([128, 1024], BF16, "yn")
        junk = C.sb(es, [128, 256], BF16, "junk")
        ssr = R2([128, 8], F32, "gss")
        ystr = R2([128, 8, 128], BF16, "yst")
        bX, bT, bS0, bS1, bD0, bD1, bO0, bO1 = C.banks
        allsc = C.sb(es, [128, 8, 256], F32, "allsc")
        A_ = [allsc.t[:, i, :] for i in range(8)]
        A3 = [allsc.t[:, i, :].rearrange("p (n h) -> p n h", h=16) for i in range(8)]
        with nc_allow(C):
            P.dma("sync", A3[7], dt_d.rearrange("(n p) h -> p n h", p=128), writes=[allsc])
        P.op("vector", lambda E: E.tensor_tensor(out=A3[7], in0=A3[7], in1=dtb.t[:].unsqueeze(1).to_broadcast([128, 16, 16]),
                                                 op=ALU.add), reads=[allsc, dtb], writes=[allsc])
        P.op("scalar", lambda E: E.activation(out=A_[7], in_=A_[7], func=AF.Exp), reads=[allsc], writes=[allsc])
        P.op("scalar", lambda E: E.activation(out=A_[0], in_=A_[7], func=AF.Ln, bias=1.0), reads=[allsc], writes=[allsc])
        P.op("vector", lambda E: E.tensor_tensor(out=A3[1], in0=A3[0], in1=a_bc.t[:].unsqueeze(1).to_broadcast([128, 16, 16]),
                                                 op=ALU.mult), reads=[allsc, a_bc], writes=[allsc])

        def csmm(E):
            E.matmul(bX.t[:, 0:256], Uf.t[:], A_[1], start=True, stop=True)
            return E.matmul(bT.t[:, 0:256], ones.t[:], A_[1], start=True, stop=True)

        P.op("tensor", csmm, reads=[allsc, Uf, ones], writes=[bX, bT])
        P.op("scalar", lambda E: E.copy(out=A_[2], in_=bX.t[:, 0:256]), reads=[bX], writes=[allsc])
        P.op("scalar", lambda E: E.activation(out=A_[3], in_=bX.t[:, 0:256], func=AF.Exp), reads=[bX], writes=[allsc])
        P.op("scalar", lambda E: E.activation(out=A_[5], in_=bT.t[:, 0:256], func=AF.Exp), reads=[bT], writes=[allsc])
        P.op("vector", lambda E: E.tensor_tensor(out=A_[7], in0=bT.t[:, 0:256], in1=A_[2], op=ALU.subtract),
             reads=[bT, allsc], writes=[allsc])
        P.op("scalar", lambda E: E.activation(out=A_[4], in_=A_[7], func=AF.Exp), reads=[allsc], writes=[allsc])
        P.op("vector", lambda E: E.tensor_tensor(out=A_[6], in0=A_[0], in1=A_[4], op=ALU.mult), reads=[allsc], writes=[allsc])
        for n in range(16):
            tok = slice(n * 128, (n + 1) * 128)
            za = zar.next()
            P.dma("sync", za.t[:], za_d[tok, :], writes=[za])
            sm = allsc
            hs = slice(n * 16, (n + 1) * 16)
            DT, DA, ECS, CD, DTD = [allsc.t[:, i, hs] for i in (0, 1, 3, 5, 6)]
            xs, xd, xde, btm = xsr.next(), xdr.next(), xder.next(), btr.next()

            def trx(E, tok=tok):
                ins = None
                for c in range(8):
                    ins = E.transpose(out=bfv(bT)[:, c * 128:(c + 1) * 128], in_=xbcT.t[:, c, tok], identity=ident.t[:])
                return ins

            P.op("tensor", trx, reads=[xbcT, ident], writes=[bT])
            P.op("scalar", lambda E, xs=xs: E.copy(out=xs.t[:], in_=bfv(bT)[:, 0:1024]), reads=[bT], writes=[xs])
            P.op("vector", lambda E, xs=xs, xd=xd, DT=DT: E.tensor_tensor(
                out=xd.t[:].rearrange("p (h d) -> p h d", d=64), in0=xs.t[:].rearrange("p (h d) -> p h d", d=64),
                in1=bc3(DT, 64), op=ALU.mult), reads=[xs, sm], writes=[xd])
            P.op("gpsimd", lambda E, xs=xs, xde=xde, DTD=DTD: E.tensor_tensor(
                out=xde.t[:].rearrange("p (h d) -> p h d", d=64), in0=xs.t[:].rearrange("p (h d) -> p h d", d=64),
                in1=bc3(DTD, 64), op=ALU.mult), reads=[xs, sm], writes=[xde])

            def trb(E, tok=tok):
                ins = None
                for g in range(4):
                    ins = E.transpose(out=bfv(bT)[:, g * 128:(g + 1) * 128], in_=xbcT.t[:, 8 + g, tok],
                                      identity=ident.t[:])
                return ins

            P.op("tensor", trb, reads=[xbcT, ident], writes=[bT])
            P.op("scalar", lambda E, btm=btm: E.copy(out=btm.t[:], in_=bfv(bT)[:, 0:512]), reads=[bT], writes=[btm])
            cbm = cbr.next()

            def cbmm(E, tok=tok):
                ins = None
                for g in range(4):
                    ins = E.matmul(bX.t[:, g * 128:(g + 1) * 128], xbcT.t[:, 8 + g, tok], xbcT.t[:, 12 + g, tok],
                                   start=True, stop=True)
                return ins

            P.op("tensor", cbmm, reads=[xbcT], writes=[bX])
            P.op("vector", lambda E, cbm=cbm: E.tensor_tensor(
                out=cbm.t[:], in0=bX.t[:, 0:512].rearrange("p (g l) -> p g l", l=128),
                in1=Uf.t[:].unsqueeze(1).to_broadcast([128, 4, 128]), op=ALU.mult), reads=[bX, Uf], writes=[cbm])
            while ssd_deferred:
                ssd_deferred.pop(0)()
            grp = []
            for g in range(4):
                grp.append((Ar.next(), Er.next(), MTr.next(), bS0 if g % 2 == 0 else bS1, bD0 if g < 2 else bD1))

            def front(g, grp=grp, DA=DA, sm=sm):
                A, Eb, MT, bS, bD = grp[g]
                P.op("gpsimd", lambda E: E.tensor_tensor(
                    out=A.t[:], in0=Lgt.t[:].unsqueeze(1).to_broadcast([128, 4, 128]),
                    in1=bc3(DA[:, 4 * g:4 * g + 4], 128), op=ALU.mult), reads=[Lgt, sm], writes=[A])

                def segmm(E):
                    ins = None
                    for j in range(4):
                        ins = E.matmul(bS.t[:, j * 128:(j + 1) * 128], A.t[:, j, :], Uf.t[:], start=True, stop=True)
                    return ins

                P.op("tensor", segmm, reads=[A, Uf], writes=[bS])
                P.op("scalar", lambda E: E.activation(out=Eb.t[:], in_=bS.t[:, 0:512], func=AF.Exp),
                     reads=[bS], writes=[Eb])

            def back(g, grp=grp, cbm=cbm, xd=xd):
                A, Eb, MT, bS, bD = grp[g]
                P.op("vector", lambda E: E.tensor_tensor(
                    out=MT.t[:], in0=Eb.t[:].rearrange("p (j l) -> p j l", l=128),
                    in1=cbm.t[:, g, :].unsqueeze(1).to_broadcast([128, 4, 128]), op=ALU.mult),
                    reads=[Eb, cbm], writes=[MT])

                def ydmm(E):
                    ins = None
                    for j in range(4):
                        h = 4 * g + j
                        col = (h % 8) * 64
                        ins = E.matmul(bD.t[:, col:col + 64], MT.t[:, j, :], xd.t[:, h * 64:(h + 1) * 64],
                                       start=True, stop=True)
                    return ins

                P.op("tensor", ydmm, reads=[MT, xd], writes=[] if g % 2 == 1 else [bD], pwrites=[bD] if g % 2 == 1 else [])

            front(0)
            front(1)
            back(0)
            front(2)
            back(1)
            front(3)
            back(2)
            back(3)

            def yomm(E, tok=tok):
                ins = None
                for g in range(4):
                    bO = bO0 if g < 2 else bO1
                    col = (g % 2) * 256
                    ins = E.matmul(bO.t[:, col:col + 256], xbcT.t[:, 12 + g, tok], prevbf.t[:, g * 256:(g + 1) * 256],
                                   start=True, stop=True)
                return ins

            P.op("tensor", yomm, reads=[xbcT, prevbf], writes=[bO0, bO1])

            def stmm(E, btm=btm, xde=xde):
                ins = None
                for g in range(4):
                    bS = bS0 if g < 2 else bS1
                    col = (g % 2) * 256
                    ins = E.matmul(bS.t[:, col:col + 256], btm.t[:, g * 128:(g + 1) * 128],
                                   xde.t[:, g * 256:(g + 1) * 256], start=True, stop=True)
                return ins

            P.op("tensor", stmm, reads=[btm, xde], writes=[bS0, bS1])
            P.op("vector", lambda E, CD=CD: E.tensor_tensor(
                out=prev32.t[:].rearrange("p (h d) -> p h d", d=64), in0=prev32.t[:].rearrange("p (h d) -> p h d", d=64),
                in1=bc3(CD, 64), op=ALU.mult), reads=[prev32, sm], writes=[prev32])
            for hb, bS in enumerate([bS0, bS1]):
                sl = slice(hb * 512, (hb + 1) * 512)
                P.op("vector", lambda E, bS=bS, sl=sl: E.tensor_tensor(out=prev32.t[:, sl], in0=prev32.t[:, sl],
                                                                       in1=bS.t[:, 0:512], op=ALU.add),
                     reads=[bS, prev32], writes=[prev32])
            P.op("scalar", lambda E: E.copy(out=prevbf.t[:], in_=prev32.t[:]), reads=[prev32], writes=[prevbf])
            t1, t3, yn, gss = t1r.next(), t3r.next(), ynr.next(), ssr.next()
            P.op("gpsimd", lambda E, t3=t3, xs=xs: E.tensor_tensor(
                out=t3.t[:].rearrange("p (h d) -> p h d", d=64), in0=xs.t[:].rearrange("p (h d) -> p h d", d=64),
                in1=bc3(dsk.t[:], 64), op=ALU.mult), reads=[xs, dsk], writes=[t3])
            for hb, (bO, bD) in enumerate([(bO0, bD0), (bO1, bD1)]):
                sl = slice(hb * 512, (hb + 1) * 512)
                P.op("vector", lambda E, t1=t1, bO=bO, ECS=ECS, hb=hb, sl=sl: E.tensor_tensor(
                    out=t1.t[:, sl].rearrange("p (h d) -> p h d", d=64),
                    in0=bO.t[:, 0:512].rearrange("p (h d) -> p h d", d=64),
                    in1=bc3(ECS[:, hb * 8:(hb + 1) * 8], 64), op=ALU.mult), reads=[bO, sm], pwrites=[t1])
                P.op("vector", lambda E, t1=t1, bD=bD, sl=sl: E.tensor_tensor(
                    out=t1.t[:, sl], in0=t1.t[:, sl], in1=bD.t[:, 0:512], op=ALU.add), reads=[bD, t1], writes=[t1])
            P.op("vector", lambda E, t1=t1, t3=t3: E.tensor_tensor(out=t1.t[:], in0=t1.t[:], in1=t3.t[:], op=ALU.add),
                 reads=[t1, t3], writes=[t1])
            P.op("vector", lambda E, t1=t1, za=za: E.tensor_tensor(out=t1.t[:], in0=t1.t[:], in1=za.t[:], op=ALU.mult),
                 reads=[t1, za], writes=[t1])
            for gi in range(4):
                P.op("scalar", lambda E, t1=t1, gss=gss, gi=gi: E.activation(
                    out=junk.t[:], in_=t1.t[:, gi * 256:(gi + 1) * 256], func=AF.Square,
                    accum_out=gss.t[:, gi:gi + 1]), reads=[t1], writes=[junk, gss])
            P.op("scalar", lambda E, gss=gss: E.activation(out=gss.t[:, 4:8], in_=gss.t[:, 0:4], func=AF.Sqrt,
                                                           scale=1.0 / 256, bias=EPS), reads=[gss], writes=[gss])
            P.op("vector", lambda E, gss=gss: E.reciprocal(out=gss.t[:, 4:8], in_=gss.t[:, 4:8]),
                 reads=[gss], writes=[gss])
            P.op("vector", lambda E, t1=t1, gss=gss: E.tensor_tensor(
                out=t1.t[:].rearrange("p (g d) -> p g d", d=256), in0=t1.t[:].rearrange("p (g d) -> p g d", d=256),
                in1=bc3(gss.t[:, 4:8], 256), op=ALU.mult), reads=[t1, gss], writes=[t1])
            P.op("gpsimd", lambda E, t1=t1, yn=yn: E.tensor_tensor(out=yn.t[:], in0=t1.t[:], in1=nrmw.t[:], op=ALU.mult),
                 reads=[t1, nrmw], writes=[yn])
            def late(yn=yn, n=n):
                yst = ystr.next()

                def try_(E, yn=yn):
                    ins = None
                    for c in range(8):
                        ins = E.transpose(out=bfv(bT)[:, c * 128:(c + 1) * 128], in_=yn.t[:, c * 128:(c + 1) * 128],
                                          identity=ident.t[:])
                    return ins

                P.op("tensor", try_, reads=[yn, ident], writes=[bT])
                P.op("scalar", lambda E, yst=yst: E.copy(out=yst.t[:], in_=bfv(bT)[:, 0:1024].rearrange("p (c t) -> p c t", t=128)),
                     reads=[bT], writes=[yst])
                if YT_FLAT:
                    P.dma("sync", yT_d[0:1024, n * 128:(n + 1) * 128].rearrange("(c p) t -> p c t", p=128), yst.t[:],
                          reads=[yst], dram_write=True)
                else:
                    j, off = n // 8, (n % 8) * 128
                    P.dma("sync", yT_d[j, 0:1024, off:off + 128].rearrange("(c p) t -> p c t", p=128), yst.t[:],
                          reads=[yst], dram_write=True)

            ssd_deferred.append(late)
        while ssd_deferred:
            ssd_deferred.pop(0)()
        P.end_phase()
        P.release_bufs()
    if ygather is not None:
        for k in range(4):
            ygather(k)

    with ExitStack() as es:
        lnw = C.sb(es, [128, 1024], F32, "lnw")
        lnb = C.sb(es, [128, 1024], F32, "lnb")
        P.dma("sync", lnw.t[:], prm["sgu_ln_w"].partition_broadcast(128), writes=[lnw])
        P.dma("sync", lnb.t[:], prm["sgu_ln_b"].partition_broadcast(128), writes=[lnb])
        sb_bc = C.sb(es, [128, 8, 128], F32, "sgub")
        P.dma("sync", sb_bc.t[:], prm["sgu_b"].rearrange("g t -> (g t)").partition_broadcast(128)
              .rearrange("p (g t) -> p g t", g=8), writes=[sb_bc])
        wsf = C.sb(es, [128, 8, 128], F32, "wsf")
        P.dma("sync", wsf.t[:], prm["wsT"].rearrange("g s t -> s g t"), writes=[wsf])
        wsm = C.sb(es, [128, 8, 128], BF16, "wsm")
        P.op("vector", lambda E: E.tensor_tensor(out=wsm.t[:], in0=wsf.t[:],
                                                 in1=Uf.t[:].unsqueeze(1).to_broadcast([128, 8, 128]), op=ALU.mult),
             reads=[wsf, Uf], writes=[wsm])
        vn = C.sb(es, [128, 16, 1024], BF16, "vn")
        vgr = [C.sb(es, [128, 2048], F32, "vg") for _ in range(2)]
        junk = C.sb(es, [128, 2048], BF16, "junkv")
        vtr = [C.sb(es, [128, 1024], F32, "vt") for _ in range(2)]
        str_ = Stager(C, es, [128, 8], F32, 2, "lnst")
        for tt in range(16):
            vg, vt, s_ = vgr[tt % 2], vtr[tt % 2], str_.next()
            P.dma("sync", vg.t[:], vg_d[tt * 128:(tt + 1) * 128, :], writes=[vg])
            P.op("vector", lambda E, vg=vg, s_=s_: E.tensor_reduce(out=s_.t[:, 0:1], in_=vg.t[:], axis=AX.X, op=ALU.add),
                 reads=[vg], writes=[s_])
            P.op("scalar", lambda E, vg=vg, s_=s_: E.activation(out=junk.t[:], in_=vg.t[:], func=AF.Square,
                                                                accum_out=s_.t[:, 1:2]), reads=[vg, s_], writes=[junk, s_])
            P.op("vector", lambda E, s_=s_: E.tensor_scalar(out=s_.t[:, 2:3], in0=s_.t[:, 0:1], scalar1=1.0 / 2048,
                                                            scalar2=None, op0=ALU.mult), reads=[s_], writes=[s_])
            P.op("vector", lambda E, s_=s_: E.tensor_tensor(out=s_.t[:, 3:4], in0=s_.t[:, 2:3], in1=s_.t[:, 2:3],
                                                            op=ALU.mult), reads=[s_], writes=[s_])
            P.op("vector", lambda E, s_=s_: E.scalar_tensor_tensor(out=s_.t[:, 4:5], in0=s_.t[:, 1:2], scalar=1.0 / 2048,
                                                                   in1=s_.t[:, 3:4], op0=ALU.mult, op1=ALU.subtract),
                 reads=[s_], writes=[s_])
            P.op("scalar", lambda E, s_=s_: E.activation(out=s_.t[:, 5:6], in_=s_.t[:, 4:5], func=AF.Sqrt, bias=EPS),
                 reads=[s_], writes=[s_])
            P.op("vector", lambda E, s_=s_: E.reciprocal(out=s_.t[:, 5:6], in_=s_.t[:, 5:6]), reads=[s_], writes=[s_])
            P.op("vector", lambda E, s_=s_: E.scalar_tensor_tensor(out=s_.t[:, 6:7], in0=s_.t[:, 2:3], scalar=-1.0,
                                                                   in1=s_.t[:, 5:6], op0=ALU.mult, op1=ALU.mult),
                 reads=[s_], writes=[s_])
            P.op("scalar", lambda E, vg=vg, vt=vt, s_=s_: E.activation(out=vt.t[:], in_=vg.t[:, 0:1024], func=AF.Identity,
                                                                       scale=s_.t[:, 5:6], bias=s_.t[:, 6:7]),
                 reads=[vg, s_], writes=[vt])
            P.op("vector", lambda E, vt=vt: E.tensor_tensor(out=vt.t[:], in0=vt.t[:], in1=lnw.t[:], op=ALU.mult),
                 reads=[vt, lnw], writes=[vt])
            P.op("vector", lambda E, vt=vt, tt=tt: E.tensor_tensor(out=vn.t[:, tt, :], in0=vt.t[:], in1=lnb.t[:], op=ALU.add),
                 reads=[vt, lnb], pwrites=[vn])
        gur = [C.sb(es, [128, S], F32, "gu") for _ in range(2)]
        szr = [C.sb(es, [128, S], F32, "sz") for _ in range(2)]
        mr = Stager(C, es, [128, 512], F32, 2, "m")
        ybr = [C.sb(es, [128, S], BF16, "ybT") for _ in range(2)]
        nb = [0]
        sgu_toks = []
        for g in range(8):
            gu, sz, yb = gur[g % 2], szr[g % 2], ybr[g % 2]
            P.dma("sync", gu.t[:], u_d[g, :, :], writes=[gu])
            P.dma("sync", sz.t[:], zb_d[g, :, :], writes=[sz])
            for tb in range(4):
                bank = C.banks[nb[0] % 4]
                nb[0] += 1

                def spmm(E, bank=bank, tb=tb, g=g):
                    ins = None
                    for i in range(4):
                        n = 4 * tb + i
                        ins = E.matmul(bank.t[:, i * 128:(i + 1) * 128], vn.t[:, n, g * 128:(g + 1) * 128],
                                       wsm.t[:, g, :], start=True, stop=True)
                    return ins

                P.op("tensor", spmm, reads=[vn, wsm], writes=[bank])
                m = mr.next()
                sl = slice(tb * 512, (tb + 1) * 512)
                P.op("vector", lambda E, m=m, bank=bank, g=g: E.tensor_tensor(
                    out=m.t[:].rearrange("p (i t) -> p i t", t=128), in0=bank.t[:, 0:512].rearrange("p (i t) -> p i t", t=128),
                    in1=sb_bc.t[:, g, :].unsqueeze(1).to_broadcast([128, 4, 128]), op=ALU.add),
                    reads=[bank, sb_bc], writes=[m])
                P.op("gpsimd", lambda E, m=m, gu=gu, sl=sl: E.tensor_tensor(out=m.t[:], in0=m.t[:], in1=gu.t[:, sl],
                                                                            op=ALU.mult), reads=[m, gu], writes=[m])
                P.op("vector", lambda E, m=m, sz=sz, yb=yb, sl=sl: E.tensor_tensor(out=yb.t[:, sl], in0=m.t[:],
                                                                                   in1=sz.t[:, sl], op=ALU.mult),
                     reads=[m, sz], pwrites=[yb])
            if YT_FLAT:
                tok = P.dma("sync", yT_d[1024 + g * 128:1024 + (g + 1) * 128, :], yb.t[:], reads=[yb], dram_write=True)
                sgu_toks.append(tok)
                if ygather is not None and g % 2 == 1:
                    for t_ in sgu_toks[-2:]:
                        P._wait("gpsimd", t_)
                    ygather(4 + g // 2)
            else:
                P.dma("sync", yT_d[:, 1024 + g * 128:1024 + (g + 1) * 128, :].rearrange("j p t -> p j t"),
                      yb.t[:].rearrange("p (j t) -> p j t", j=2), reads=[yb], dram_write=True)
        P.end_phase()
        P.release_bufs()


def _consts():
    i = np.arange(128)
    U = (i[:, None] <= i[None, :]).astype(np.float32)
    return {
        "c_ident": np.eye(128, dtype=np.float32).astype(NPBF),
        "c_Ubf": U.astype(NPBF),
        "c_Uf": U,
        "c_Lgt": (i[:, None] > i[None, :]).astype(np.float32),
        "c_ones": np.ones((128, 128), np.float32),
    }


CONST_SPECS = [("c_ident", BF16), ("c_Ubf", BF16), ("c_Uf", F32), ("c_Lgt", F32), ("c_ones", F32)]


def load_consts(C, es, nc):
    cst = {}
    for nm, dt in CONST_SPECS:
        d = nc.dram_tensor(nm, [128, 128], dt, kind="ExternalInput").ap()
        b = C.sb(es, [128, 128], dt, nm)
        C.P.dma("sync", b.t[:], d, writes=[b])
        cst[nm[2:]] = b
    return cst


def _din(nc, name, shape, dt):
    return nc.dram_tensor(name, list(shape), dt, kind="ExternalInput").ap()


def _dout(nc, name, shape, dt):
    return nc.dram_tensor(name, list(shape), dt, kind="ExternalOutput").ap()


def _dint(nc, name, shape, dt):
    return nc.dram_tensor(name, list(shape), dt).ap()


MIX0_PRM = [("conv_w", [128, 16, 4]), ("conv_b", [128, 16]), ("dt_bias", [16]), ("a_log", [16]), ("d_skip", [16]),
            ("ssd_norm_w", [1024]), ("sgu_ln_w", [1024]), ("sgu_ln_b", [1024]), ("wsT", [8, 128, 128]),
            ("sgu_b", [8, 128])]


def mix0_scratch(nc):
    return {"raw": _dint(nc, "s_raw", [16, 128, S], F32), "zb": _dint(nc, "s_zb", [8, 128, S], F32),
            "u": _dint(nc, "s_u", [8, 128, S], F32), "za": _dint(nc, "s_za", [S, 1024], F32),
            "dt": _dint(nc, "s_dt", [S, 16], F32), "vg": _dint(nc, "s_vg", [S, 2048], F32)}


def mix1_scratch(nc):
    return {"qkT": _dint(nc, "s_qkT", [32, 128, S], BF16), "v": _dint(nc, "s_v", [S, 8, 256], BF16),
            "sg": _dint(nc, "s_sg", [S, 8, 256], F32)}


def build_launch(which):
    nc = bass.Bass("TRN2", target_bir_lowering=False)
    with ExitStack() as es:
        C = Ctx(nc, es)
        cst = load_consts(C, es, nc)
        if which == "normt":
            phase_normt(C, _din(nc, "x", [TL, D], F32), _din(nc, "nw", [D], F32), _dout(nc, "hnT", [D, TL], BF16),
                        cst["ident"])
        elif which == "mix0":
            prm = {k: _din(nc, "p_" + k, shp, F32) for k, shp in MIX0_PRM}
            phase_mix0(C, _din(nc, "hnT_all", [2, D, TL], BF16), _din(nc, "W0", [D, W0_COLS], F32), prm,
                       _dout(nc, "yT", [2, 2048, TL], BF16), mix0_scratch(nc), cst)
        elif which == "mix1":
            phase_mix1(C, _din(nc, "hnT_all", [2, D, TL], BF16), _din(nc, "W1", [D, 8192], F32),
                       _din(nc, "lam", [4, 128], F32), _din(nc, "subw", [256], F32),
                       _dout(nc, "yT", [2, 2048, TL], BF16), mix1_scratch(nc), cst)
        elif which == "out_norm":
            h = _dout(nc, "h", [TL, D], F32)
            phase_out(C, _din(nc, "yTt", [4096, TL], BF16), _din(nc, "Wo", [4096, D], F32),
                      _din(nc, "res", [TL, D], F32), h)
            phase_normt(C, h, _din(nc, "nw", [D], F32), _dout(nc, "hnT", [D, TL], BF16), cst["ident"])
        elif which == "out_final":
            h = _dint(nc, "h", [TL, D], F32)
            phase_out(C, _din(nc, "yTt", [4096, TL], BF16), _din(nc, "Wo", [4096, D], F32),
                      _din(nc, "res", [TL, D], F32), h)
            phase_finalnorm(C, h, _din(nc, "nw", [D], F32), _dout(nc, "out", [TL, D], F32))
    return nc


def _run(nc, in_maps):
    res = run_bass_kernel_spmd(nc, in_maps, core_ids=list(range(8)))
    return res.results


def _f32(a):
    return np.ascontiguousarray(np.asarray(a), dtype=np.float32)


def prep_layer0(r, w_in, conv_w, conv_b, dt_bias, a_log, d_skip, ssd_norm_w, ln_w, ln_b, ws, sb):
    o = 1024 * r
    cols = np.concatenate([
        2048 + o + np.arange(1024), 4096 + 512 * r + np.arange(512), 5120 + 512 * r + np.arange(512),
        6176 + o + np.arange(1024), 8224 + o + np.arange(1024),
        o + np.arange(1024), 6144 + 16 * r + np.arange(16),
        10272 + o + np.arange(1024), 10272 + 1024 * (1 - r) + np.arange(1024)])
    cidx = np.concatenate([o + np.arange(1024), 2048 + 512 * r + np.arange(512), 3072 + 512 * r + np.arange(512)])
    return {
        "W0": np.ascontiguousarray(w_in[:, cols]),
        "p_conv_w": np.ascontiguousarray(conv_w[:, cidx].T.reshape(16, 128, 4).transpose(1, 0, 2)),
        "p_conv_b": np.ascontiguousarray(conv_b[cidx].reshape(16, 128).T),
        "p_dt_bias": np.ascontiguousarray(dt_bias[16 * r:16 * r + 16]),
        "p_a_log": np.ascontiguousarray(a_log[16 * r:16 * r + 16]),
        "p_d_skip": np.ascontiguousarray(d_skip[16 * r:16 * r + 16]),
        "p_ssd_norm_w": np.ascontiguousarray(ssd_norm_w[o:o + 1024]),
        "p_sgu_ln_w": np.ascontiguousarray(ln_w[o:o + 1024]), "p_sgu_ln_b": np.ascontiguousarray(ln_b[o:o + 1024]),
        "p_wsT": np.ascontiguousarray(ws[8 * r:8 * r + 8].transpose(0, 2, 1)),
        "p_sgu_b": np.ascontiguousarray(sb[8 * r:8 * r + 8]),
    }


def prep_layer1(r, w_in):
    cols = []
    for hh in range(8):
        h = 8 * r + hh
        for base in (0, 4096, 8192, 12288):
            cols.append(base + h * 256 + np.arange(256))
    return np.ascontiguousarray(w_in[:, np.concatenate(cols)])


PAIRS = [[0, 1], [2, 3], [4, 5], [6, 7]]
DC = D // 2
CCH = 256


def gather_rows(P, send, gathered, nrows):
    for k in range(nrows // CCH):
        P.collective("AllGather", [send[k * CCH:(k + 1) * CCH, :]],
                     [gathered[2 * k * CCH:(2 * k + 2) * CCH, :]], PAIRS)
    P.end_phase()


def prefetch_wo(C, es, W_d):
    w = C.sb(es, [128, 32, DC], BF16, "wo")
    for hf in range(2):
        C.P.dma("gpsimd", w.t[:, :, hf * 512:(hf + 1) * 512],
                W_d[:, hf * 512:(hf + 1) * 512].rearrange("(kc p) c -> p kc c", p=128), pwrites=[w])
    return w


def phase_out_cs(C, yg, w, res_d, h_d):
    P = C.P
    with ExitStack() as es:
        yring = [C.sb(es, [128, 32, 512], BF16, "yTq") for _ in range(2)]
        rring = Stager(C, es, [128, 512], F32, 3, "res")
        oring = Stager(C, es, [128, 512], F32, 3, "ho")
        nb = 0
        for tq in range(4):
            yT = yring[tq % 2]
            for k in range(8):
                for s_ in range(2):
                    r0 = (2 * k + s_) * CCH
                    P.dma("scalar", yT.t[:, s_ * 16 + 2 * k:s_ * 16 + 2 * k + 2, :],
                          yg[r0:r0 + CCH, tq * 512:(tq + 1) * 512].rearrange("(kc p) t -> p kc t", p=128),
                          pwrites=[yT])
            for tt in range(4):
                rows = slice(tq * 512 + tt * 128, tq * 512 + (tt + 1) * 128)
                for cb in range(2):
                    bank = C.banks[nb % 8]
                    nb += 1
                    pairs = [(yT.t[:, k, tt * 128:(tt + 1) * 128], w.t[:, k, cb * 512:(cb + 1) * 512])
                             for k in range(32)]
                    mm_group(P, bank, bank.t[:, 0:512], pairs, reads=[yT, w])
                    r, o = rring.next(), oring.next()
                    P.dma("scalar", r.t[:], res_d[rows, cb * 512:(cb + 1) * 512], writes=[r])
                    P.op("vector", lambda E, o=o, bank=bank, r=r: E.tensor_tensor(out=o.t[:], in0=bank.t[:, 0:512],
                                                                                  in1=r.t[:], op=ALU.add),
                         reads=[bank, r], writes=[o])
                    P.dma("sync", h_d[rows, cb * 512:(cb + 1) * 512], o.t[:], reads=[o], dram_write=True)
        P.end_phase()
        P.release_bufs()


def phase_norm_cs(C, h_d, nw_d, ss_send, ss_g, ident, hn_send=None, hn_g=None, out_d=None):
    P = C.P
    with ExitStack() as es:
        nwbc = C.sb(es, [128, DC], F32, "nwbc")
        P.dma("sync", nwbc.t[:], nw_d.partition_broadcast(128), writes=[nwbc])
        hres = C.sb(es, [128, 16, DC], F32, "hres")
        junk = C.sb(es, [128, DC], BF16, "junk")
        ssc = C.sb(es, [128, 16], F32, "ssc")
        for tt in range(16):
            P.dma("sync", hres.t[:, tt, :], h_d[tt * 128:(tt + 1) * 128, :], pwrites=[hres])
        for tt in range(16):
            P.op("scalar", lambda E, tt=tt: E.activation(out=junk.t[:], in_=hres.t[:, tt, :], func=AF.Square,
                                                         accum_out=ssc.t[:, tt:tt + 1]),
                 reads=[hres], writes=[junk], pwrites=[ssc])
        P.dma("sync", ss_send, ssc.t[:], reads=[ssc], dram_write=True)
        P.end_phase()
        P.collective("AllGather", [ss_send], [ss_g], PAIRS)
        P.end_phase()
        ss2 = C.sb(es, [128, 2, 16], F32, "ss2")
        P.dma("sync", ss2.t[:], ss_g.rearrange("(s p) t -> p s t", p=128), writes=[ss2])
        rs = C.sb(es, [128, 16], F32, "rs")
        P.op("vector", lambda E: E.tensor_tensor(out=rs.t[:], in0=ss2.t[:, 0, :], in1=ss2.t[:, 1, :], op=ALU.add),
             reads=[ss2], writes=[rs])
        P.op("scalar", lambda E: E.activation(out=rs.t[:], in_=rs.t[:], func=AF.Sqrt, scale=1.0 / D, bias=EPS),
             reads=[rs], writes=[rs])
        P.op("vector", lambda E: E.reciprocal(out=rs.t[:], in_=rs.t[:]), reads=[rs], writes=[rs])
        if out_d is not None:
            orr = [C.sb(es, [128, DC], F32, "o") for _ in range(2)]
            for tt in range(16):
                o = orr[tt % 2]
                P.op("vector", lambda E, o=o, tt=tt: E.scalar_tensor_tensor(
                    out=o.t[:], in0=hres.t[:, tt, :], scalar=rs.t[:, tt:tt + 1], in1=nwbc.t[:], op0=ALU.mult,
                    op1=ALU.mult), reads=[hres, rs, nwbc], writes=[o])
                P.dma("sync", out_d[tt * 128:(tt + 1) * 128, :], o.t[:], reads=[o], dram_write=True)
            P.end_phase()
            P.release_bufs()
            return
        hnr = [C.sb(es, [128, DC], BF16, "hn") for _ in range(2)]
        hnT = C.sb(es, [128, 8, S], BF16, "hnTo")
        for tt in range(16):
            hn = hnr[tt % 2]
            P.op("vector", lambda E, hn=hn, tt=tt: E.scalar_tensor_tensor(
                out=hn.t[:], in0=hres.t[:, tt, :], scalar=rs.t[:, tt:tt + 1], in1=nwbc.t[:], op0=ALU.mult,
                op1=ALU.mult), reads=[hres, rs, nwbc], writes=[hn])
            bank = C.banks[tt % 4]

            def tr(E, hn=hn, bank=bank):
                ins = None
                for j in range(8):
                    ins = E.transpose(out=bfv(bank)[:, j * 128:(j + 1) * 128], in_=hn.t[:, j * 128:(j + 1) * 128],
                                      identity=ident.t[:])
                return ins

            P.op("tensor", tr, reads=[hn, ident], writes=[bank])
            eng = "scalar" if tt % 2 == 0 else "vector"

            def cp(E, bank=bank, tt=tt, eng=eng):
                src = bfv(bank).rearrange("p (j t) -> p j t", t=128)
                dst = hnT.t[:, :, tt * 128:(tt + 1) * 128]
                return E.copy(out=dst, in_=src) if eng == "scalar" else E.tensor_copy(out=dst, in_=src)

            P.op(eng, cp, reads=[bank], pwrites=[hnT])
        P.dma("sync", hn_send.rearrange("(kc p) t -> p kc t", p=128), hnT.t[:], reads=[hnT], dram_write=True)
        P.end_phase()
        P.release_bufs()
    gather_rows(P, hn_send, hn_g, DC)


def build_fused():
    nc = bass.Bass("TRN2", target_bir_lowering=False)
    with ExitStack() as es:
        C = Ctx(nc, es)
        P = C.P
        cst = load_consts(C, es, nc)
        x_d = _din(nc, "x", [S, D], F32)
        xc_d = _din(nc, "xc", [S, DC], F32)
        nw0, nw1c, nwfc = _din(nc, "nw0", [D], F32), _din(nc, "nw1c", [DC], F32), _din(nc, "nwfc", [DC], F32)
        prm = {k: _din(nc, "p_" + k, shp, F32) for k, shp in MIX0_PRM}
        W0 = _din(nc, "W0", [D, W0_COLS], F32)
        Wo0 = _din(nc, "Wo0", [4096, DC], F32)
        W1 = _din(nc, "W1", [D, 8192], F32)
        lam = _din(nc, "lam", [4, 128], F32)
        subw = _din(nc, "subw", [256], F32)
        Wo1 = _din(nc, "Wo1", [4096, DC], F32)
        out_d = _dout(nc, "out", [S, DC], F32)
        hn0 = _dint(nc, "x_hn0", [D, S], BF16)
        hn_send = _dint(nc, "x_hnsend", [DC, S], BF16)
        hn_g = _dint(nc, "x_hng", [2 * DC, S], BF16)
        y_own = _dint(nc, "x_yown", [2048, S], BF16)
        y_g = _dint(nc, "x_yg", [4096, S], BF16)
        ss_send = _dint(nc, "x_sssend", [128, 16], F32)
        ss_g = _dint(nc, "x_ssg", [256, 16], F32)
        h1 = _dint(nc, "x_h1", [S, DC], F32)
        h2 = _dint(nc, "x_h2", [S, DC], F32)
        scr0, scr1 = mix0_scratch(nc), mix1_scratch(nc)
        hn0_blocks = [(hn0[q * 512:(q + 1) * 512, :], 4 * q, 4, 0, S) for q in range(4)]
        hn1_blocks = [(hn_g[(2 * k + s_) * CCH:(2 * k + s_ + 1) * CCH, :], s_ * 8 + 2 * k, 2, 0, S)
                      for k in range(4) for s_ in range(2)]

        def ygather(k):
            P.collective("AllGather", [y_own[k * CCH:(k + 1) * CCH, :]], [y_g[2 * k * CCH:(2 * k + 2) * CCH, :]], PAIRS)

        phase_mix0(C, lambda es_g: phase_normt(C, x_d, nw0, None, cst["ident"], NT=16, keep=es_g), W0, prm, y_own,
                   scr0, cst, ygather)
        with ExitStack() as es2:
            w = prefetch_wo(C, es2, Wo0)
            P.end_phase()
            phase_out_cs(C, y_g, w, xc_d, h1)
        phase_norm_cs(C, h1, nw1c, ss_send, ss_g, cst["ident"], hn_send=hn_send, hn_g=hn_g)
        phase_mix1(C, hn1_blocks, W1, lam, subw, y_own, scr1, cst, ygather)
        with ExitStack() as es2:
            w = prefetch_wo(C, es2, Wo1)
            P.end_phase()
            phase_out_cs(C, y_g, w, h1, h2)
        phase_norm_cs(C, h2, nwfc, ss_send, ss_g, cst["ident"], out_d=out_d)
    return nc


WOUT0_PERM = np.concatenate([np.concatenate([s * 1024 + np.arange(1024), 2048 + s * 1024 + np.arange(1024)])
                             for s in range(2)])

DEBUG = {}
FUSED = True


def kernel(x, norm_w, even_w_in, even_conv_w, even_conv_b, even_dt_bias, even_a_log, even_d_skip,
           even_ssd_norm_w, even_sgu_ln_w, even_sgu_ln_b, even_sgu_ws, even_sgu_b, even_w_out, odd_w_in,
           odd_lam_q1, odd_lam_k1, odd_lam_q2, odd_lam_k2, odd_subln_w, odd_w_out, final_norm_w):
    x = _f32(x)
    norm_w = _f32(norm_w)
    cs = _consts()
    cores = [(b, r) for b in range(4) for r in range(2)]
    xo = [np.ascontiguousarray(x[b, r * TL:(r + 1) * TL, :]) for b, r in cores]
    l0 = [prep_layer0(r, _f32(even_w_in)[0], _f32(even_conv_w)[0], _f32(even_conv_b)[0], _f32(even_dt_bias)[0],
                      _f32(even_a_log)[0], _f32(even_d_skip)[0], _f32(even_ssd_norm_w)[0], _f32(even_sgu_ln_w)[0],
                      _f32(even_sgu_ln_b)[0], _f32(even_sgu_ws)[0], _f32(even_sgu_b)[0]) for r in range(2)]
    w1 = [prep_layer1(r, _f32(odd_w_in)[0]) for r in range(2)]
    wo0 = np.ascontiguousarray(_f32(even_w_out)[0][WOUT0_PERM])
    wo1 = _f32(odd_w_out)[0]
    lam = np.stack([_f32(odd_lam_q1)[0], _f32(odd_lam_k1)[0], _f32(odd_lam_q2)[0], _f32(odd_lam_k2)[0]])
    subw = _f32(odd_subln_w)[0]

    if FUSED:
        global YT_FLAT
        YT_FLAT = True
        maps = []
        fw = _f32(final_norm_w)
        for c, (b, r) in enumerate(cores):
            cols = slice(r * DC, (r + 1) * DC)
            m = dict(cs, x=np.ascontiguousarray(x[b]), xc=np.ascontiguousarray(x[b][:, cols]), nw0=norm_w[0],
                     nw1c=np.ascontiguousarray(norm_w[1][cols]), nwfc=np.ascontiguousarray(fw[cols]),
                     Wo0=np.ascontiguousarray(wo0[:, cols]), W1=w1[r], lam=lam, subw=subw,
                     Wo1=np.ascontiguousarray(wo1[:, cols]), **l0[r])
            maps.append(m)
        res = _run(build_fused(), maps)
        out = np.empty((4, S, D), np.float32)
        for c, (b, r) in enumerate(cores):
            out[b, :, r * DC:(r + 1) * DC] = np.asarray(res[c]["out"])
        return out

    def gather_hn(res):
        return [np.ascontiguousarray(np.stack([np.asarray(res[2 * b]["hnT"]), np.asarray(res[2 * b + 1]["hnT"])]))
                for b, r in cores]

    def a2a(res):
        return [np.ascontiguousarray(np.concatenate([np.asarray(res[2 * b]["yT"])[r],
                                                     np.asarray(res[2 * b + 1]["yT"])[r]], axis=0))
                for b, r in cores]

    res = _run(build_launch("normt"), [dict(cs, x=xo[c], nw=norm_w[0]) for c in range(8)])
    hn_all = gather_hn(res)
    res = _run(build_launch("mix0"), [dict(cs, hnT_all=hn_all[c], **l0[cores[c][1]]) for c in range(8)])
    yTt = a2a(res)
    DEBUG["y0"] = yTt
    res = _run(build_launch("out_norm"), [dict(cs, yTt=yTt[c], Wo=wo0, res=xo[c], nw=norm_w[1]) for c in range(8)])
    h1 = [np.asarray(res[c]["h"]) for c in range(8)]
    DEBUG["h1"] = h1
    hn_all = gather_hn(res)
    res = _run(build_launch("mix1"), [dict(cs, hnT_all=hn_all[c], W1=w1[cores[c][1]], lam=lam, subw=subw)
                                      for c in range(8)])
    yTt = a2a(res)
    DEBUG["y1"] = yTt
    res = _run(build_launch("out_final"), [dict(cs, yTt=yTt[c], Wo=wo1, res=h1[c], nw=_f32(final_norm_w))
                                           for c in range(8)])
    out = np.empty((4, S, D), np.float32)
    for c, (b, r) in enumerate(cores):
        out[b, r * TL:(r + 1) * TL, :] = np.asarray(res[c]["out"])
    return out
```
